# Optimizing a Trainium2 kernel written in Bass

```python
import functools
import jax
import jax.numpy as jnp
from jax import lax
import numpy as np


D_MODEL = 2048
BATCH = 8
SEQ = 4096
DEPTH = 1

POOL_WIDTH = D_MODEL // 4
POOL_WINDOWS = (2, 4, 8, 16)
N_POOL_GROUPS = len(POOL_WINDOWS)
POOL_GROUP = POOL_WIDTH // N_POOL_GROUPS
RWKV_WIDTH = D_MODEL - POOL_WIDTH
HEAD_SIZE = 64
N_RWKV_HEADS = RWKV_WIDTH // HEAD_SIZE
DECAY_LORA = 64
AAA_LORA = 64
GATE_LORA = 224
RWKV_IN = 3 * RWKV_WIDTH + DECAY_LORA + AAA_LORA + GATE_LORA
IN_WIDTH = POOL_WIDTH + RWKV_IN
D_FF = 5632
MACARON_WEIGHT = 0.5
N_SUBLAYERS = 3
N_MOD = 3
NORM_EPS = 1e-6
LN_X_EPS = 1e-5 * HEAD_SIZE

kernel_name = 'hybrid_pool_rwkv7_macaron_block'


def rms_norm(x, gain):
    xf = x.astype(jnp.float32)
    y = xf * lax.rsqrt(jnp.mean(xf * xf, axis=-1, keepdims=True) + NORM_EPS)
    return (y * gain.astype(jnp.float32)).astype(x.dtype)


def modulate(h, shift, scale):
    return h * (1 + scale[:, None, :]) + shift[:, None, :]


def token_shift(p, mu):
    prev = jnp.pad(p, ((0, 0), (1, 0), (0, 0)))[:, :-1]
    return p + mu * (prev - p)


def swiglu(h, w_gate, w_up, w_down):
    return (jax.nn.silu(h @ w_gate) * (h @ w_up)) @ w_down


def multiscale_pool(u, pool_w, pool_scale):
    B, S, _ = u.shape
    cs = jnp.cumsum(u.astype(jnp.float32), axis=1)
    t = jnp.arange(1, S + 1, dtype=jnp.float32)
    outs = []
    for gi, win in enumerate(POOL_WINDOWS):
        lo, hi = gi * POOL_GROUP, (gi + 1) * POOL_GROUP
        c_g = cs[..., lo:hi]
        lagged = jnp.pad(c_g, ((0, 0), (win, 0), (0, 0)))[:, :S]
        count = jnp.minimum(t, float(win))[None, :, None]
        mean = (c_g - lagged) / count
        outs.append(mean.astype(u.dtype) - u[..., lo:hi])
    pooled = jnp.stack(outs, axis=2)
    mixed = jnp.einsum('bsgc,gcd->bsgd', pooled, pool_w)
    return mixed.reshape(B, S, POOL_WIDTH) * pool_scale


def rwkv7_time_mix(p, mu, w0, w2, a0, a2, g2, k_k, k_a, r_k, lnx_w, lnx_b):
    B, S, _ = p.shape
    H, N, R = N_RWKV_HEADS, HEAD_SIZE, RWKV_WIDTH
    f32 = jnp.float32
    p = token_shift(p, mu)
    r = p[..., :R]
    k = p[..., R:2 * R]
    v = p[..., 2 * R:3 * R]
    o = 3 * R
    xw = p[..., o:o + DECAY_LORA]
    o = o + DECAY_LORA
    xa = p[..., o:o + AAA_LORA]
    o = o + AAA_LORA
    xg = p[..., o:]
    w_log = -jax.nn.softplus(-(w0 + jnp.tanh(xw) @ w2)) - 0.5
    decay = jnp.exp(-jnp.exp(w_log.astype(f32)))
    a = jax.nn.sigmoid(a0 + xa @ a2)
    g = jax.nn.sigmoid(xg) @ g2
    kk = (k * k_k).astype(f32).reshape(B, S, H, N)
    kk = kk / jnp.maximum(jnp.linalg.norm(kk, axis=-1, keepdims=True), 1e-12)
    k = k * (1 + (a - 1) * k_a)
    r_h = r.astype(f32).reshape(B, S, H, N)
    k_h = k.astype(f32).reshape(B, S, H, N)
    v_h = v.astype(f32).reshape(B, S, H, N)
    a_h = a.astype(f32).reshape(B, S, H, N)
    w_h = decay.reshape(B, S, H, N)

    def step(state, inp):
        r_t, w_t, k_t, v_t, kk_t, a_t = inp
        sa = jnp.einsum('bhvk,bhk->bhv', state, -kk_t)
        state = (state * w_t[:, :, None, :]
                 + sa[..., None] * (kk_t * a_t)[:, :, None, :]
                 + v_t[..., None] * k_t[:, :, None, :])
        return state, jnp.einsum('bhvk,bhk->bhv', state, r_t)

    xs = tuple(jnp.moveaxis(t, 1, 0) for t in (r_h, w_h, k_h, v_h, kk, a_h))
    state0 = jnp.zeros((B, H, N, N), f32)
    _, y = lax.scan(step, state0, xs)
    y = jnp.moveaxis(y, 0, 1)
    mean = jnp.mean(y, axis=-1, keepdims=True)
    var = jnp.mean(jnp.square(y - mean), axis=-1, keepdims=True)
    y = ((y - mean) * lax.rsqrt(var + LN_X_EPS) * lnx_w.astype(f32).reshape(H, N)
         + lnx_b.astype(f32).reshape(H, N))
    bonus = jnp.sum(r_h * k_h * r_k.astype(f32), axis=-1, keepdims=True) * v_h
    y = (y + bonus).reshape(B, S, R).astype(p.dtype)
    return y * g


def hybrid_mixer(h, w_in, mu_shift, pool_w, pool_scale, w0, w2, a0, a2, g2,
                 k_k, k_a, r_k, lnx_w, lnx_b, w_out):
    p = h @ w_in
    y_pool = multiscale_pool(p[..., :POOL_WIDTH], pool_w, pool_scale)
    y_rwkv = rwkv7_time_mix(p[..., POOL_WIDTH:], mu_shift, w0, w2, a0, a2, g2,
                            k_k, k_a, r_k, lnx_w, lnx_b)
    return jnp.concatenate([y_pool, y_rwkv], axis=-1) @ w_out


def sandwich_sublayer(x, fn, gain_pre, gain_post, shift, scale, gate, weight):
    h = modulate(rms_norm(x, gain_pre), shift, scale)
    y = rms_norm(fn(h), gain_post)
    return x + weight * (1 + gate[:, None, :]) * y


def setup_inputs(seed: int = 0) -> dict:
    key = jax.random.key(seed)
    ks = jax.random.split(key, 32)
    f32 = jnp.float32
    L, D = DEPTH, D_MODEL

    def nrm(k, shape, scale):
        return jax.random.normal(k, shape, f32) * scale

    return {
        'x': nrm(ks[0], (BATCH, SEQ, D), 1.0),
        'c': nrm(ks[1], (BATCH, D), 1.0),
        'w_ada': nrm(ks[2], (L, D, N_SUBLAYERS * N_MOD * D), 0.1 * D ** -0.5),
        'b_ada': nrm(ks[3], (L, N_SUBLAYERS * N_MOD * D), 0.02),
        'norm_pre': 1.0 + nrm(ks[4], (L, N_SUBLAYERS, D), 0.05),
        'norm_post': 1.0 + nrm(ks[5], (L, N_SUBLAYERS, D), 0.05),
        'ffn1_w_gate': nrm(ks[6], (L, D, D_FF), D ** -0.5),
        'ffn1_w_up': nrm(ks[7], (L, D, D_FF), D ** -0.5),
        'ffn1_w_down': nrm(ks[8], (L, D_FF, D), D_FF ** -0.5),
        'w_in': nrm(ks[9], (L, D, IN_WIDTH), D ** -0.5),
        'mu_shift': jax.random.uniform(ks[10], (L, RWKV_IN), f32, 0.0, 1.0),
        'pool_w': nrm(ks[11], (L, N_POOL_GROUPS, POOL_GROUP, POOL_GROUP), POOL_GROUP ** -0.5),
        'pool_scale': 1.0 + nrm(ks[12], (L, POOL_WIDTH), 0.1),
        'w0': jax.random.uniform(ks[13], (L, RWKV_WIDTH), f32, -6.0, 1.0),
        'w2': nrm(ks[14], (L, DECAY_LORA, RWKV_WIDTH), 0.1 * DECAY_LORA ** -0.5),
        'a0': nrm(ks[15], (L, RWKV_WIDTH), 0.1),
        'a2': nrm(ks[16], (L, AAA_LORA, RWKV_WIDTH), 0.5 * AAA_LORA ** -0.5),
        'g2': nrm(ks[17], (L, GATE_LORA, RWKV_WIDTH), GATE_LORA ** -0.5),
        'k_k': 0.85 + nrm(ks[18], (L, RWKV_WIDTH), 0.05),
        'k_a': 1.0 + nrm(ks[19], (L, RWKV_WIDTH), 0.05),
        'r_k': nrm(ks[20], (L, N_RWKV_HEADS, HEAD_SIZE), 0.1),
        'lnx_w': 1.0 + nrm(ks[21], (L, RWKV_WIDTH), 0.05),
        'lnx_b': nrm(ks[22], (L, RWKV_WIDTH), 0.02),
        'w_out': nrm(ks[23], (L, D, D), D ** -0.5),
        'ffn2_w_gate': nrm(ks[24], (L, D, D_FF), D ** -0.5),
        'ffn2_w_up': nrm(ks[25], (L, D, D_FF), D ** -0.5),
        'ffn2_w_down': nrm(ks[26], (L, D_FF, D), D_FF ** -0.5),
    }


def reference(x, c, w_ada, b_ada, norm_pre, norm_post, ffn1_w_gate, ffn1_w_up, ffn1_w_down,
              w_in, mu_shift, pool_w, pool_scale, w0, w2, a0, a2, g2, k_k, k_a, r_k,
              lnx_w, lnx_b, w_out, ffn2_w_gate, ffn2_w_up, ffn2_w_down):
    B = x.shape[0]
    for l in range(DEPTH):
        mod = (jax.nn.silu(c) @ w_ada[l] + b_ada[l]).reshape(B, N_SUBLAYERS, N_MOD, D_MODEL)
        ffn1 = functools.partial(swiglu, w_gate=ffn1_w_gate[l], w_up=ffn1_w_up[l], w_down=ffn1_w_down[l])
        mixer = functools.partial(
            hybrid_mixer, w_in=w_in[l], mu_shift=mu_shift[l], pool_w=pool_w[l],
            pool_scale=pool_scale[l], w0=w0[l], w2=w2[l], a0=a0[l], a2=a2[l], g2=g2[l],
            k_k=k_k[l], k_a=k_a[l], r_k=r_k[l], lnx_w=lnx_w[l], lnx_b=lnx_b[l], w_out=w_out[l])
        ffn2 = functools.partial(swiglu, w_gate=ffn2_w_gate[l], w_up=ffn2_w_up[l], w_down=ffn2_w_down[l])
        x = sandwich_sublayer(x, ffn1, norm_pre[l, 0], norm_post[l, 0],
                              mod[:, 0, 0], mod[:, 0, 1], mod[:, 0, 2], MACARON_WEIGHT)
        x = sandwich_sublayer(x, mixer, norm_pre[l, 1], norm_post[l, 1],
                              mod[:, 1, 0], mod[:, 1, 1], mod[:, 1, 2], 1.0)
        x = sandwich_sublayer(x, ffn2, norm_pre[l, 2], norm_post[l, 2],
                              mod[:, 2, 0], mod[:, 2, 1], mod[:, 2, 2], MACARON_WEIGHT)
    return x
```

```python
import numpy as np
from contextlib import ExitStack
import concourse.bass as bass
import concourse.mybir as mybir
from concourse.bass_utils import run_bass_kernel_spmd

F32 = mybir.dt.float32
BF16 = mybir.dt.bfloat16
AF = mybir.ActivationFunctionType
ALU = mybir.AluOpType

PE, ACT, DVE, POOL, SP = "pe", "act", "dve", "pool", "sp"
ENGINES = [PE, ACT, DVE, POOL, SP]
EPOCH = 20000

D = 2048
FF = 5632
NKC = 16
NFC = 44
TT = 512
CH = 128
INW = 5472
RW = 1536
NPAIR = 12
NORM_EPS = 1e-6
LNX_EPS = 1e-5 * 64


class Buf:
    __slots__ = ("name", "last_w", "readers")

    def __init__(self, name):
        self.name = name
        self.last_w = None
        self.readers = []


class Op:
    __slots__ = ("eng", "emit", "deps", "is_dma", "dsem", "need_sig", "sig")

    def __init__(self, eng, emit, is_dma, dsem):
        self.eng = eng
        self.emit = emit
        self.deps = []
        self.is_dma = is_dma
        self.dsem = dsem
        self.need_sig = is_dma
        self.sig = None


class Prog:
    def __init__(self, nc, stack):
        self.nc = nc
        self.stack = stack
        self.ops = {e: [] for e in ENGINES}
        self.dma_sems = {}
        self.key_bufs = {}
        self.final_waits = []

    def new_sem(self, name):
        return self.stack.enter_context(self.nc.semaphore(name))

    def op(self, eng, emit, reads=(), writes=(), dma_key=None):
        is_dma = dma_key is not None
        dsem = None
        if is_dma:
            if dma_key not in self.dma_sems:
                self.dma_sems[dma_key] = [self.new_sem("d_" + dma_key), 0]
                self.key_bufs[dma_key] = Buf("k_" + dma_key)
            dsem = self.dma_sems[dma_key]
            writes = list(writes) + [self.key_bufs[dma_key]]
        o = Op(eng, emit, is_dma, dsem)
        deps = []

        def add(p, kind):
            if p is None or p is o:
                return
            if not p.is_dma and not is_dma and p.eng == eng:
                if eng == PE or kind == "war":
                    return
            deps.append(p)

        for b in reads:
            add(b.last_w, "raw")
        for b in writes:
            add(b.last_w, "waw")
            for r in b.readers:
                add(r, "war")
        for b in reads:
            b.readers.append(o)
        for b in writes:
            b.last_w = o
            b.readers = []
        seen = set()
        for p in deps:
            if id(p) not in seen:
                seen.add(id(p))
                p.need_sig = True
                o.deps.append(p)
        self.ops[eng].append(o)
        return o

    def emit_all(self):
        nc = self.nc
        for e in ENGINES:
            cnt = 0
            sems = []
            for o in self.ops[e]:
                if o.is_dma:
                    o.dsem[1] += 16
                    o.sig = (o.dsem[0], o.dsem[1], 16)
                elif o.need_sig:
                    ep = cnt // EPOCH
                    if ep >= len(sems):
                        sems.append(self.new_sem(f"s_{e}_{ep}"))
                    o.sig = (sems[ep], cnt % EPOCH + 1, 1)
                    cnt += 1
        final_waits = self.final_waits

        def run(e, h):
            waited = {}
            for o in self.ops[e]:
                for p in o.deps:
                    sem, val, _ = p.sig
                    k = id(sem)
                    if waited.get(k, 0) < val:
                        h.wait_ge(sem, val)
                        waited[k] = val
                ins = o.emit(h)
                if o.need_sig:
                    ins.then_inc(o.sig[0], o.sig[2])
            if e == SP:
                for o in final_waits:
                    sem, val, _ = o.sig
                    if waited.get(id(sem), 0) < val:
                        h.wait_ge(sem, val)
                        waited[id(sem)] = val

        with nc.Block() as block:
            @block.tensor
            def _(h):
                run(PE, h)

            @block.scalar
            def _(h):
                run(ACT, h)

            @block.vector
            def _(h):
                run(DVE, h)

            @block.gpsimd
            def _(h):
                run(POOL, h)

            @block.sync
            def _(h):
                run(SP, h)


class Tl:
    __slots__ = ("t", "bufs")

    def __init__(self, t, bufs):
        self.t = t
        self.bufs = bufs


def build_program(T, stage=3, debug=False):
    NT = T // TT
    nc = bass.Bass("TRN2", target_bir_lowering=False)

    def din(name, shape, dt=F32):
        return nc.dram_tensor(name, list(shape), dt, kind="ExternalInput").ap()

    def dscr(name, shape, dt):
        return nc.dram_tensor(name, list(shape), dt, kind="Internal").ap()

    xT = din("xT", [D, T])
    cT = din("cT", [128, NKC])
    w_ada = din("w_ada", [D, 9 * D])
    b_ada = din("b_ada", [128, 144])
    npre = din("npre", [128, 48])
    npost = din("npost", [128, 48])
    wsrc = {}
    for nm, shp in [("wg1", [D, FF]), ("wu1", [D, FF]), ("wd1", [FF, D]), ("win", [D, INW]), ("wout", [D, D]),
                    ("wg2", [D, FF]), ("wu2", [D, FF]), ("wd2", [FF, D])]:
        wsrc[nm] = (din(nm, shp), dscr(nm + "_b", shp, BF16), shp)
    muT = din("muT", [128, 43])
    poolw = din("poolw", [128, 4 * 128])
    pscale = din("pscale", [128, 4])
    vecs = din("vecs", [128, 7 * NPAIR])
    wa2 = din("wa2", [128, RW])
    g2a = din("g2a", [128, RW])
    g2b = din("g2b", [96, RW])
    outT = nc.dram_tensor("outT", [D, T], F32, kind="ExternalOutput").ap()
    x1T = dscr("x1T", [D, T], F32)
    x2T = dscr("x2T", [D, T], F32)

    with ExitStack() as st:
        P = Prog(nc, st)

        def sb(name, shape, dt=F32):
            return st.enter_context(nc.sbuf_tensor(name, list(shape), dt))

        def tile(name, shape, dt=F32):
            return Tl(sb(name, shape, dt), [Buf(name)])

        PSB = []
        for b in range(7):
            PSB.append(Tl(st.enter_context(nc.psum_tensor(f"ps{b}", [128, 512], F32)), [Buf(f"ps{b}")]))
        PST = Tl(st.enter_context(nc.psum_tensor("pst", [128, 1024], BF16)), [Buf("pst")])
        rr = {"i": 0}
        WORK = [0, 1, 2, 3, 4, 6]

        def nextbank():
            b = WORK[rr["i"] % len(WORK)]
            rr["i"] += 1
            return PSB[b]

        NSLAB = 44
        arena = sb("arena", [128, NSLAB * 512], BF16)
        slabB = [Buf(f"slab{i}") for i in range(NSLAB)]
        arena_f32 = None

        def slab_bf(i, n=1):
            return Tl(arena[:, i * 512:(i + n) * 512], slabB[i:i + n])


        hT = tile("hT", [128, NKC, TT], BF16)
        fbuf = Tl(sb("fbuf", [128, NKC, TT], F32), [Buf(f"fbuf{i}") for i in range(NKC)])
        NXR = 3
        xr = [tile(f"xr{i}", [128, 2, TT], F32) for i in range(NXR)]
        xr_i = {"i": 0}
        sq = [tile(f"sq{i}", [128, TT], BF16) for i in range(2)]
        sq_i = {"i": 0}
        rstd = tile("rstd", [128, TT], F32)
        rstd2 = tile("rstd2", [128, TT], F32)
        lntmp = tile("lntmp", [128, TT], F32)
        t32 = [tile(f"t32_{i}", [128, TT], F32) for i in range(4)]
        t32_i = {"i": 0}

        def next_t32():
            t = t32[t32_i["i"] % len(t32)]
            t32_i["i"] += 1
            return t

        WGUF = [tile(f"wguslot{i}", [128, 2 * NKC * 256], BF16) for i in range(2)]
        WGU = [Tl(w.t[:, :].rearrange("p (a k c) -> p a k c", a=2, k=NKC), w.bufs) for w in WGUF]
        WGU4 = [Tl(w.t[:, :].rearrange("p (a k c) -> p a k c", a=4, k=NKC), w.bufs) for w in WGUF]
        WD = [tile(f"wdslot{i}", [128, 11, 512], BF16) for i in range(2)]

        ones_mean = tile("ones_mean", [128, 128], BF16)
        blk1 = tile("blk1", [128, 128], BF16)
        blkm = tile("blkm", [128, 128], F32)
        ident = tile("ident", [128, 128], BF16)
        MU2 = tile("MU2", [128, 256], BF16)
        ML = tile("ML", [128, 128], BF16)
        scanmask = tile("scanmask", [128, TT], F32)
        mod = tile("mod", [128, 144], F32)
        bada = tile("bada", [128, 144], F32)
        npre_t = tile("npre_t", [128, 48], F32)
        npost_t = tile("npost_t", [128, 48], F32)
        gA = tile("gA", [128, 48], F32)
        gB = tile("gB", [128, 48], F32)
        c_t = tile("c_t", [128, NKC], F32)
        cst = tile("cst", [128, 8], F32)
        sc_bf = tile("sc_bf", [128, NKC], BF16)

        def cv(h):
            h.memset(ones_mean.t[:], 1.0 / D)
            h.memset(blk1.t[:], 0.0)
            h.memset(blk1.t[0:64, 0:64], 1.0)
            h.memset(blk1.t[64:128, 64:128], 1.0)
            h.memset(blkm.t[:], 0.0)
            h.memset(blkm.t[0:64, 0:64], 1.0 / 64)
            h.memset(blkm.t[64:128, 64:128], 1.0 / 64)
            h.memset(scanmask.t[:], 1.0)
            for i_, v_ in enumerate([1.0, -0.5, 1e-18, LNX_EPS, NORM_EPS]):
                h.memset(cst.t[:, i_:i_ + 1], float(v_))
            ins = None
            for c in range(TT // CH):
                ins = h.memset(scanmask.t[:, c * CH:c * CH + 1], 0.0)
            return ins
        P.op(DVE, cv, writes=ones_mean.bufs + blk1.bufs + blkm.bufs + scanmask.bufs + cst.bufs)

        def cp(h):
            h.memset(ident.t[:], 1.0)
            h.affine_select(out=ident.t[:], in_=ident.t[:], pattern=[[1, 128]], compare_op=ALU.is_equal,
                            fill=0.0, base=0, channel_multiplier=-1)
            h.memset(MU2.t[:], 1.0)
            h.affine_select(out=MU2.t[:, 0:128], in_=MU2.t[:, 0:128], pattern=[[1, 128]], compare_op=ALU.is_gt,
                            fill=0.0, base=0, channel_multiplier=-1)
            h.affine_select(out=MU2.t[:, 128:256], in_=MU2.t[:, 128:256], pattern=[[1, 128]], compare_op=ALU.is_ge,
                            fill=0.0, base=0, channel_multiplier=-1)
            h.memset(ML.t[:], 1.0)
            return h.affine_select(out=ML.t[:], in_=ML.t[:], pattern=[[-1, 128]], compare_op=ALU.is_gt,
                                   fill=0.0, base=0, channel_multiplier=1)
        P.op(POOL, cp, writes=ident.bufs + MU2.bufs + ML.bufs)

        wbuf = {}
        conv_i = {"i": 0}

        def convert(nm):
            src, dst, shp = wsrc[nm]
            rows = shp[0]
            blk = 256 if shp[1] > 2048 else 512
            bufs = []
            for r0 in range(0, rows, blk):
                b = Buf(f"{nm}_{r0}")
                bufs.append(b)
                key = f"cv{conv_i['i'] % 16}"
                conv_i["i"] += 1
                P.op(POOL, (lambda h, r0=r0, blk=blk: h.dma_start(out=dst[r0:r0 + blk, :], in_=src[r0:r0 + blk, :])),
                     writes=[b], dma_key=key)
            wbuf[nm] = bufs

        def load_small(dst_tile, src_ap, eng=SP):
            P.op(eng, lambda h: h.dma_start(out=dst_tile.t[:], in_=src_ap), writes=dst_tile.bufs, dma_key="par")

        load_small(c_t, cT[:, :])
        load_small(bada, b_ada[:, :])
        load_small(npre_t, npre[:, :])
        load_small(npost_t, npost[:, :])

        convert("wg1")
        convert("wu1")

        P.op(ACT, lambda h: h.activation(out=sc_bf.t[:], in_=c_t.t[:], func=AF.Silu), reads=c_t.bufs, writes=sc_bf.bufs)
        ADA_SL = [Tl(arena[:, i * 16 * 512:(i + 1) * 16 * 512].rearrange("p (k n) -> p k n", k=NKC),
                     slabB[i * 16:(i + 1) * 16]) for i in range(2)]
        ps_mod = PSB[5]
        for sl in range(36):
            slot = ADA_SL[sl % 2]
            n0 = sl * 512
            P.op(POOL, (lambda h, slot=slot, n0=n0: h.dma_start(
                out=slot.t, in_=w_ada[:, n0:n0 + 512].rearrange("(k p) n -> p k n", p=128))),
                writes=slot.bufs, dma_key=f"ada{sl % 2}")

            def mm(h, slot=slot, sl=sl):
                ins = None
                for jj in range(4):
                    j = sl * 4 + jj
                    for kc in range(NKC):
                        ins = h.matmul(ps_mod.t[:, j:j + 1], lhsT=slot.t[:, kc, jj * 128:(jj + 1) * 128],
                                       rhs=sc_bf.t[:, kc:kc + 1], start=(kc == 0), stop=(kc == NKC - 1))
                return ins
            P.op(PE, mm, reads=slot.bufs + sc_bf.bufs, writes=ps_mod.bufs)
        P.op(DVE, lambda h: h.tensor_tensor(out=mod.t[:], in0=ps_mod.t[:, 0:144], in1=bada.t[:], op=ALU.add),
             reads=ps_mod.bufs + bada.bufs, writes=mod.bufs)

        def modcol(s, m):
            return mod.t[:, (s * 3 + m) * 16:(s * 3 + m) * 16 + 16]

        def mkvec(h):
            ins = None
            for s in range(3):
                wt = 1.0 if s == 1 else 0.5
                h.scalar_tensor_tensor(out=gA.t[:, s * 16:(s + 1) * 16], in0=modcol(s, 1), scalar=1.0,
                                       in1=npre_t.t[:, s * 16:(s + 1) * 16], op0=ALU.add, op1=ALU.mult)
                ins = h.scalar_tensor_tensor(out=gB.t[:, s * 16:(s + 1) * 16], in0=modcol(s, 2), scalar=1.0,
                                             in1=npost_t.t[:, s * 16:(s + 1) * 16], op0=ALU.add, op1=ALU.mult)
            return ins
        P.op(DVE, mkvec, reads=mod.bufs + npre_t.bufs + npost_t.bufs, writes=gA.bufs + gB.bufs)

        def mkvec2(h):
            h.tensor_scalar(out=gB.t[:, 0:16], in0=gB.t[:, 0:16], scalar1=0.5, scalar2=None, op0=ALU.mult)
            return h.tensor_scalar(out=gB.t[:, 32:48], in0=gB.t[:, 32:48], scalar1=0.5, scalar2=None, op0=ALU.mult)
        P.op(DVE, mkvec2, reads=gB.bufs, writes=gB.bufs)

        convert("wd1")
        if stage >= 2:
            convert("win")
            convert("wout")
        if stage >= 3:
            convert("wg2")
            convert("wu2")
            convert("wd2")

        xbufs = {"xT": [Buf(f"xT{i}") for i in range(NT)], "x1T": [Buf(f"x1T{i}") for i in range(NT)],
                 "x2T": [Buf(f"x2T{i}") for i in range(NT)], "outT": [Buf(f"outT{i}") for i in range(NT)]}
        xaps = {"xT": xT, "x1T": x1T, "x2T": x2T, "outT": outT}

        def next_xr():
            t = xr[xr_i["i"] % NXR]
            k = xr_i["i"] % NXR
            xr_i["i"] += 1
            return t, k

        def load_x(src, ti, kc0):
            t, k = next_xr()
            ap = xaps[src][kc0 * 128:(kc0 + 2) * 128, ti * TT:(ti + 1) * TT].rearrange("(k p) t -> p k t", p=128)
            P.op(SP, lambda h: h.dma_start(out=t.t[:], in_=ap), reads=[xbufs[src][ti]], writes=t.bufs, dma_key=f"xr{k}")
            return t, k

        def rsqrt_from_psum(ps, out_t, eps):
            P.op(ACT, lambda h: h.activation(out=lntmp.t[:], in_=ps.t[:], func=AF.Ln, bias=cst.t[:, 4:5], scale=1.0),
                 reads=ps.bufs + cst.bufs, writes=lntmp.bufs)
            P.op(ACT, lambda h: h.activation(out=out_t.t[:], in_=lntmp.t[:], func=AF.Exp, scale=-0.5),
                 reads=lntmp.bufs, writes=out_t.bufs)

        def prenorm_p1(src, ti):
            ps = PSB[5]
            for kc0 in range(0, NKC, 2):
                t, _ = load_x(src, ti, kc0)
                for q in range(2):
                    kc = kc0 + q
                    s_ = sq[sq_i["i"] % 2]
                    sq_i["i"] += 1
                    P.op(ACT, (lambda h, t=t, q=q, s_=s_: h.activation(out=s_.t[:], in_=t.t[:, q, :], func=AF.Square)),
                         reads=t.bufs, writes=s_.bufs)
                    P.op(PE, (lambda h, s_=s_, kc=kc: h.matmul(ps.t[:], lhsT=ones_mean.t[:], rhs=s_.t[:],
                                                                start=(kc == 0), stop=(kc == NKC - 1))),
                         reads=s_.bufs + ones_mean.bufs, writes=ps.bufs)
            rsqrt_from_psum(ps, rstd, NORM_EPS)

        def prenorm_p2(src, ti, s):
            for kc0 in range(0, NKC, 2):
                t, _ = load_x(src, ti, kc0)
                for q in range(2):
                    kc = kc0 + q
                    tmp = next_t32()
                    P.op(DVE, (lambda h, t=t, q=q, tmp=tmp, kc=kc: h.scalar_tensor_tensor(
                        out=tmp.t[:], in0=t.t[:, q, :], scalar=gA.t[:, s * 16 + kc:s * 16 + kc + 1], in1=rstd.t[:],
                        op0=ALU.mult, op1=ALU.mult)), reads=t.bufs + gA.bufs + rstd.bufs, writes=tmp.bufs)
                    P.op(ACT, (lambda h, tmp=tmp, kc=kc: h.activation(
                        out=hT.t[:, kc, :], in_=tmp.t[:], func=AF.Identity,
                        bias=mod.t[:, (s * 3) * 16 + kc:(s * 3) * 16 + kc + 1], scale=1.0)),
                        reads=tmp.bufs + mod.bufs, writes=hT.bufs)

        def evac_f(ps, dc, ps2):
            P.op(ACT, lambda h: h.activation(out=fbuf.t[:, dc, :], in_=ps.t[:], func=AF.Copy),
                 reads=ps.bufs, writes=[fbuf.bufs[dc]])
            s_ = sq[sq_i["i"] % 2]
            sq_i["i"] += 1
            P.op(ACT, lambda h: h.activation(out=s_.t[:], in_=ps.t[:], func=AF.Square), reads=ps.bufs, writes=s_.bufs)
            P.op(PE, lambda h: h.matmul(ps2.t[:], lhsT=ones_mean.t[:], rhs=s_.t[:], start=(dc == 0), stop=(dc == NKC - 1)),
                 reads=s_.bufs + ones_mean.bufs, writes=ps2.bufs)

        def postnorm(src, dst, ti, s, ps2, final):
            rsqrt_from_psum(ps2, rstd2, NORM_EPS)
            for kc0 in range(0, NKC, 2):
                t, k = load_x(src, ti, kc0)
                for q in range(2):
                    kc = kc0 + q
                    tmp = next_t32()
                    P.op(DVE, (lambda h, tmp=tmp, kc=kc: h.scalar_tensor_tensor(
                        out=tmp.t[:], in0=fbuf.t[:, kc, :], scalar=gB.t[:, s * 16 + kc:s * 16 + kc + 1], in1=rstd2.t[:],
                        op0=ALU.mult, op1=ALU.mult)), reads=[fbuf.bufs[kc]] + gB.bufs + rstd2.bufs, writes=tmp.bufs)
                    P.op(DVE, (lambda h, t=t, q=q, tmp=tmp: h.tensor_tensor(out=t.t[:, q, :], in0=t.t[:, q, :], in1=tmp.t[:],
                                                                            op=ALU.add)),
                         reads=tmp.bufs + t.bufs, writes=t.bufs)
                ap = xaps[dst][kc0 * 128:(kc0 + 2) * 128, ti * TT:(ti + 1) * TT].rearrange("(k p) t -> p k t", p=128)
                o = P.op(SP, (lambda h, t=t, ap=ap: h.dma_start(out=ap, in_=t.t[:])), reads=t.bufs,
                         writes=[xbufs[dst][ti]], dma_key=f"st{k}")
                if final:
                    P.final_waits.append(o)

        def ffn_phase(s, src, dst, wg, wu, wd, final):
            wgb, wub, wdb = wsrc[wg][1], wsrc[wu][1], wsrc[wd][1]
            UT = [slab_bf(i) for i in range(NFC)]
            NGU = NT * (NFC // 2)
            NWD = NT * 16

            def gu_load(idx):
                if idx >= NGU:
                    return
                fp = idx % (NFC // 2)
                k = idx % 2
                slot = WGU[k]
                c0 = fp * 256
                P.op(SP, lambda h: h.dma_start(out=slot.t[:, 0, :, :], in_=wgb[:, c0:c0 + 256].rearrange("(k p) c -> p k c", p=128)),
                     reads=wbuf[wg], writes=slot.bufs, dma_key=f"wg{k}")
                P.op(SP, lambda h: h.dma_start(out=slot.t[:, 1, :, :], in_=wub[:, c0:c0 + 256].rearrange("(k p) c -> p k c", p=128)),
                     reads=wbuf[wu], writes=slot.bufs, dma_key=f"wu{k}")

            def wd_load(idx):
                if idx >= NWD:
                    return
                g = (idx % 16) // 4
                qq = idx % 4
                k = idx % 2
                slot = WD[k]
                r0 = qq * 11 * 128
                P.op(SP, lambda h: h.dma_start(
                    out=slot.t[:], in_=wdb[r0:r0 + 11 * 128, g * 512:(g + 1) * 512].rearrange("(f p) c -> p f c", p=128)),
                    reads=wbuf[wd], writes=slot.bufs, dma_key=f"wd{k}")

            def gu_step(idx):
                fp = idx % (NFC // 2)
                slot = WGU[idx % 2]
                for q in range(2):
                    gu_chunk(slot, fp * 2 + q, q)
                gu_load(idx + 2)

            def gu_chunk(slot, fc, q):
                pg = PSB[fc % 2]
                pu = PSB[2 + fc % 2]

                def mm(h):
                    ins = None
                    for kc in range(NKC):
                        h.matmul(pg.t[:], lhsT=slot.t[:, 0, kc, q * 128:(q + 1) * 128], rhs=hT.t[:, kc, :],
                                 start=(kc == 0), stop=(kc == NKC - 1))
                    for kc in range(NKC):
                        ins = h.matmul(pu.t[:], lhsT=slot.t[:, 1, kc, q * 128:(q + 1) * 128], rhs=hT.t[:, kc, :],
                                       start=(kc == 0), stop=(kc == NKC - 1))
                    return ins
                P.op(PE, mm, reads=slot.bufs + hT.bufs, writes=pg.bufs + pu.bufs)
                sg = next_t32()
                P.op(ACT, lambda h: h.activation(out=sg.t[:], in_=pg.t[:], func=AF.Silu), reads=pg.bufs, writes=sg.bufs)
                u = UT[fc]
                P.op(DVE, lambda h: h.tensor_tensor(out=u.t, in0=sg.t[:], in1=pu.t[:], op=ALU.mult),
                     reads=sg.bufs + pu.bufs, writes=u.bufs)

            def wd_step(idx):
                qq = idx % 4
                slot = WD[idx % 2]

                def mm(h):
                    ins = None
                    for fl in range(11):
                        fc = qq * 11 + fl
                        for d_ in range(4):
                            ins = h.matmul(PSB[d_].t[:], lhsT=slot.t[:, fl, d_ * 128:(d_ + 1) * 128], rhs=UT[fc].t,
                                           start=(fc == 0), stop=(fc == NFC - 1))
                    return ins
                rd = list(slot.bufs)
                for fl in range(11):
                    rd = rd + UT[qq * 11 + fl].bufs
                P.op(PE, mm, reads=rd, writes=PSB[0].bufs + PSB[1].bufs + PSB[2].bufs + PSB[3].bufs)
                wd_load(idx + 2)

            prenorm_p1(src, 0)
            prenorm_p2(src, 0, s)
            gu_load(0)
            gu_load(1)
            for ti in range(NT):
                for fp in range(NFC // 2):
                    gu_step(ti * (NFC // 2) + fp)
                    if ti == 0 and fp == 11:
                        wd_load(0)
                        wd_load(1)
                if ti + 1 < NT:
                    prenorm_p1(src, ti + 1)
                ps2 = PSB[4]
                for g in range(4):
                    for qq in range(4):
                        wd_step(ti * 16 + g * 4 + qq)
                    for d_ in range(4):
                        evac_f(PSB[d_], g * 4 + d_, ps2)
                    if g == 0 and ti + 1 < NT:
                        prenorm_p2(src, ti + 1, s)
                postnorm(src, dst, ti, s, ps2, final)

        ffn_phase(0, "xT", "x1T" if stage > 1 else "outT", "wg1", "wu1", "wd1", final=(stage == 1))

        if stage >= 2:
            mixer_phase(nc, P, locals())
        if stage >= 3:
            ffn_phase(2, "x2T", "outT", "wg2", "wu2", "wd2", final=True)

        P.emit_all()
    return nc


def mixer_phase(nc, P, E):
    sb = E["sb"]; tile = E["tile"]; PSB = E["PSB"]; PST = E["PST"]; nextbank = E["nextbank"]
    hT = E["hT"]; fbuf = E["fbuf"]; WGU4 = E["WGU4"]; WD = E["WD"]; wsrc = E["wsrc"]; wbuf = E["wbuf"]
    NT = E["NT"]; stage = E["stage"]; arena = E["arena"]; slabB = E["slabB"]; t32 = E["t32"]
    blk1 = E["blk1"]; blkm = E["blkm"]; ident = E["ident"]; MU2 = E["MU2"]; ML = E["ML"]; scanmask = E["scanmask"]
    prenorm_p1 = E["prenorm_p1"]; prenorm_p2 = E["prenorm_p2"]; postnorm = E["postnorm"]; evac_f = E["evac_f"]
    cst = E["cst"]; muT = E["muT"]; poolw = E["poolw"]; pscale = E["pscale"]; vecs = E["vecs"]; wa2 = E["wa2"]; g2a = E["g2a"]; g2b = E["g2b"]
    winb = wsrc["win"][1]; woutb = wsrc["wout"][1]

    def ft(i):
        return Tl(fbuf.t[:, i, :], [fbuf.bufs[i]])
    (I_RS, I_KS, I_VS, I_E1, I_L1, I_NL, I_A, I_G, I_KK, I_K2, I_BP, I_BONUS, I_CWN, I_EA, I_ER, I_EK) = range(16)
    F = [ft(i) for i in range(16)]
    I_CWX, I_Y, I_YC = I_NL, I_EA, I_ER

    def slab(i, n=1):
        return Tl(arena[:, i * 512:(i + n) * 512], slabB[i:i + n])
    catT = [slab(i) for i in range(16)]
    lora_in = slab(16)
    sg1 = slab(17)
    sg2 = slab(18)
    AR = Tl(arena[:, 19 * 512:21 * 512].rearrange("p (c t i) -> p c t i", c=4, t=2), slabB[19:21])
    Kt = slab(21)
    Bt = slab(22)
    Vt = slab(23)
    sqb = slab(24)
    pooled = slab(25)
    tok3 = [Tl(arena[:, (26 + i) * 512:(26 + i) * 512 + 384], [slabB[26 + i]]) for i in range(2)]

    def mat(sl, q, nm):
        return Tl(arena[:, sl * 512 + q * 128: sl * 512 + (q + 1) * 128], [Buf(nm)])
    G1M = [Tl(arena[:, 28 * 512 + hh * 256:28 * 512 + (hh + 1) * 256], [Buf(f"G1M{hh}")]) for hh in range(2)]
    G2M = [Tl(arena[:, 29 * 512 + hh * 256:29 * 512 + (hh + 1) * 256], [Buf(f"G2M{hh}")]) for hh in range(2)]
    PP = [[mat(30 + hh, pp, f"P{hh}{pp}") for pp in range(2)] for hh in range(2)]
    PPT = [[mat(30 + hh, 2 + pp, f"PT{hh}{pp}") for pp in range(2)] for hh in range(2)]
    TTm = [[mat(32, hh * 2 + pp, f"TT{hh}{pp}") for pp in range(2)] for hh in range(2)]
    Xbf = mat(33, 0, "Xbf")
    Ubf = mat(33, 1, "Ubf")
    Hz = [Tl(arena[:, 33 * 512 + 256 + hh * 64: 33 * 512 + 256 + (hh + 1) * 64], [Buf(f"Hz{hh}")]) for hh in range(2)]
    poolw_bf = Tl(arena[:, 34 * 512:35 * 512].rearrange("p (g d) -> p g d", g=4), [slabB[34]])
    wa2_bf = Tl(arena[:, 35 * 512:38 * 512], slabB[35:38])
    g2a_bf = Tl(arena[:, 38 * 512:41 * 512], slabB[38:41])
    g2b_bf = Tl(arena[:, 41 * 512:44 * 512], slabB[41:44])

    Hst = [tile(f"H{j}", [128, 64], F32) for j in range(NPAIR)]
    H0p = tile("H0p", [128, 64], F32)
    Ee = tile("Ee", [128, 16 + TT], F32)
    pcar = [tile(f"pcar{g}", [128, 16], F32) for g in range(4)]
    wtmp = [tile(f"wtmp{i}", [128, 16 + TT], F32) for i in range(2)]
    carry = tile("carry", [128, 43], F32)
    mu_t = tile("mu_t", [128, 43], F32)
    omu_t = tile("omu_t", [128, 43], F32)
    vec_t = tile("vec_t", [128, 7 * NPAIR], F32)
    dvec = tile("dvec", [128, 3 * NPAIR], F32)
    pscale_t = tile("pscale_t", [128, 4], F32)
    invc = tile("invc", [128, 16], F32)
    ncw = tile("ncw", [128, 4], F32)
    wc = tile("wc", [128, 4], F32)
    stage32 = Tl(fbuf.t[:, 0:3, :].rearrange("p a t -> p (a t)"), fbuf.bufs[0:3])

    V_W0, V_A0, V_KK, V_KA, V_RK, V_LW, V_LB = range(7)

    def vcol(v, j):
        return vec_t.t[:, v * NPAIR + j:v * NPAIR + j + 1]

    def act(out_ap, in_ap, func, reads, writes, **kw):
        P.op(ACT, lambda h: h.activation(out=out_ap, in_=in_ap, func=func, **kw), reads=reads, writes=writes)

    def dve(fn, reads, writes):
        P.op(DVE, fn, reads=reads, writes=writes)

    def pool(fn, reads, writes):
        P.op(POOL, fn, reads=reads, writes=writes)

    def ld(dst_ap, src_ap, bufs):
        P.op(SP, lambda h: h.dma_start(out=dst_ap, in_=src_ap), writes=bufs, dma_key="par")
    ld(mu_t.t[:], muT[:, :], mu_t.bufs)
    ld(vec_t.t[:], vecs[:, :], vec_t.bufs)
    ld(pscale_t.t[:], pscale[:, :], pscale_t.bufs)

    def setup_v(h):
        h.tensor_scalar(out=omu_t.t[:], in0=mu_t.t[:], scalar1=-1.0, scalar2=1.0, op0=ALU.mult, op1=ALU.add)
        h.tensor_scalar(out=dvec.t[:, 0:2 * NPAIR], in0=vec_t.t[:, 0:2 * NPAIR], scalar1=-1.0, scalar2=None, op0=ALU.mult)
        h.tensor_scalar(out=dvec.t[:, 2 * NPAIR:3 * NPAIR], in0=vec_t.t[:, V_KA * NPAIR:(V_KA + 1) * NPAIR],
                        scalar1=-1.0, scalar2=1.0, op0=ALU.mult, op1=ALU.add)
        h.memset(carry.t[:], 0.0)
        for g in range(4):
            h.memset(pcar[g].t[:], 0.0)
        for j in range(NPAIR):
            h.memset(Hst[j].t[:], 0.0)
        h.memset(Hz[0].t, 0.0)
        h.memset(Hz[1].t, 0.0)
        ins = None
        for t_ in range(16):
            ins = h.memset(invc.t[:, t_:t_ + 1], 1.0 / (t_ + 1))
        return ins
    wr = omu_t.bufs + dvec.bufs + carry.bufs + invc.bufs + Hz[0].bufs + Hz[1].bufs
    for g in range(4):
        wr = wr + pcar[g].bufs
    for j in range(NPAIR):
        wr = wr + Hst[j].bufs
    P.op(DVE, setup_v, reads=mu_t.bufs + vec_t.bufs, writes=wr)

    def ld_cast(dst_tl, dst_ap, src_ap, rows, width):
        P.op(SP, lambda h: h.dma_start(out=stage32.t[0:rows, 0:width], in_=src_ap), writes=stage32.bufs, dma_key="par")
        P.op(DVE, lambda h: h.tensor_copy(dst_ap, stage32.t[0:rows, 0:width]), reads=stage32.bufs, writes=dst_tl.bufs)
    ld_cast(poolw_bf, arena[:, 34 * 512:35 * 512], poolw[:, :], 128, 512)
    ld_cast(wa2_bf, wa2_bf.t[:, :], wa2[:, :], 128, RW)
    ld_cast(g2a_bf, g2a_bf.t[:, :], g2a[:, :], 128, RW)
    ld_cast(g2b_bf, g2b_bf.t[0:96, :], g2b[:, :], 96, RW)

    slot_i = {"w": 0, "d": 0}
    tab = {"i": 0}

    def win_load(chunks):
        k = slot_i["w"] % 2
        slot = WGU4[k]
        slot_i["w"] += 1
        for i, c in enumerate(chunks):
            wdt = min(128, INW - c * 128)
            dst_ap = slot.t[:, i, :, 0:wdt]
            src_ap = winb[:, c * 128:c * 128 + wdt].rearrange("(k p) c -> p k c", p=128)
            P.op(SP, (lambda h, dst_ap=dst_ap, src_ap=src_ap: h.dma_start(out=dst_ap, in_=src_ap)),
                 reads=wbuf["win"], writes=slot.bufs, dma_key=(f"wg{k}" if i % 2 == 0 else f"wu{k}"))
        return slot

    def proj(slot, i, rows):
        ps = nextbank()

        def mm(h):
            ins = None
            for kc in range(NKC):
                ins = h.matmul(ps.t[0:rows, :], lhsT=slot.t[:, i, kc, 0:rows], rhs=hT.t[:, kc, :], start=(kc == 0), stop=(kc == NKC - 1))
            return ins
        P.op(PE, mm, reads=slot.bufs + hT.bufs, writes=ps.bufs)
        return ps

    def tshift(ps, c, out, rows=128):
        a_ = t32[2 * (tab["i"] % 2)]
        b_ = t32[2 * (tab["i"] % 2) + 1]
        tab["i"] += 1
        act(a_.t[0:rows, :], ps.t[0:rows, :], AF.Identity, ps.bufs + omu_t.bufs, a_.bufs, scale=omu_t.t[0:rows, c:c + 1])
        act(b_.t[0:rows, :], ps.t[0:rows, :], AF.Identity, ps.bufs + mu_t.bufs, b_.bufs, scale=mu_t.t[0:rows, c:c + 1])

        def f(h):
            h.tensor_tensor(out=out.t[0:rows, 1:TT], in0=a_.t[0:rows, 1:TT], in1=b_.t[0:rows, 0:TT - 1], op=ALU.add)
            return h.tensor_tensor(out=out.t[0:rows, 0:1], in0=a_.t[0:rows, 0:1], in1=carry.t[0:rows, c:c + 1], op=ALU.add)
        pool(f, a_.bufs + b_.bufs + carry.bufs, out.bufs)
        pool(lambda h: h.tensor_copy(carry.t[0:rows, c:c + 1], b_.t[0:rows, TT - 1:TT]), b_.bufs, carry.bufs)

    def sigmoid_to(out_ap, out_bufs, in_t, rows):
        e1, l1 = F[I_E1], F[I_L1]
        act(e1.t[0:rows, :], in_t.t[0:rows, :], AF.Exp, in_t.bufs, e1.bufs, scale=-1.0)
        act(l1.t[0:rows, :], e1.t[0:rows, :], AF.Ln, e1.bufs, l1.bufs, bias=cst.t[0:rows, 0:1], scale=1.0)
        act(out_ap, l1.t[0:rows, :], AF.Exp, l1.bufs, out_bufs, scale=-1.0)

    def v3(t_):
        return t_.t.rearrange("p (c i) -> p c i", c=4)

    def lora_inputs():
        sx40, sx41, sx42 = F[I_RS], F[I_KS], F[I_VS]
        slot = win_load([40, 41, 42])
        ps = proj(slot, 0, 128)
        tshift(ps, 40, sx40)
        ps = proj(slot, 1, 128)
        tshift(ps, 41, sx41)
        ps = proj(slot, 2, 96)
        tshift(ps, 42, sx42, rows=96)
        e1, l1 = F[I_E1], F[I_L1]
        act(e1.t[0:64, :], sx40.t[0:64, :], AF.Exp, sx40.bufs, e1.bufs, scale=-2.0)
        act(l1.t[0:64, :], e1.t[0:64, :], AF.Ln, e1.bufs, l1.bufs, bias=cst.t[0:64, 0:1], scale=1.0)
        act(e1.t[0:64, :], l1.t[0:64, :], AF.Exp, l1.bufs, e1.bufs, scale=-1.0)
        dve(lambda h: h.tensor_scalar(out=lora_in.t[0:64, :], in0=e1.t[0:64, :], scalar1=2.0, scalar2=-1.0, op0=ALU.mult, op1=ALU.add),
            e1.bufs, lora_in.bufs)
        act(lora_in.t[64:128, :], sx40.t[64:128, :], AF.Copy, sx40.bufs, lora_in.bufs)
        sigmoid_to(sg1.t[:, :], sg1.bufs, sx41, 128)
        sigmoid_to(sg2.t[0:96, :], sg2.bufs, sx42, 96)

    def pool_group(ti, slot, g):
        win = 2 << g
        L = g + 1
        ps = proj(slot, g, 128)
        pool(lambda h: h.tensor_copy(Ee.t[:, 0:16], pcar[g].t[:]), pcar[g].bufs, Ee.bufs)
        act(Ee.t[:, 16:16 + TT], ps.t[:], AF.Copy, ps.bufs, Ee.bufs)
        pool(lambda h: h.tensor_copy(pcar[g].t[:], Ee.t[:, TT:TT + 16]), Ee.bufs, pcar[g].bufs)
        cur, cur_lo = Ee, 0
        for k in range(1, L + 1):
            lo = 16 - win + (1 << k)
            n = 16 + TT - lo
            nxt = wtmp[k % 2]
            a0 = lo - cur_lo
            b0 = lo - (1 << (k - 1)) - cur_lo
            assert b0 >= 0

            def f(h, nxt=nxt, cur=cur, a0=a0, b0=b0, n=n):
                return h.tensor_tensor(out=nxt.t[:, 0:n], in0=cur.t[:, a0:a0 + n], in1=cur.t[:, b0:b0 + n], op=ALU.add)
            pool(f, cur.bufs, nxt.bufs)
            cur, cur_lo = nxt, lo
        assert cur_lo == 16
        wl = cur
        dve(lambda h: h.scalar_tensor_tensor(out=pooled.t[:, :], in0=wl.t[:, 0:TT], scalar=1.0 / win, in1=Ee.t[:, 16:16 + TT],
                                             op0=ALU.mult, op1=ALU.subtract), wl.bufs + Ee.bufs, pooled.bufs)
        if ti == 0:
            tmpc = t32[0]
            nfix = win - 1
            dve(lambda h: h.tensor_tensor(out=tmpc.t[:, 0:nfix], in0=wl.t[:, 0:nfix], in1=invc.t[:, 0:nfix], op=ALU.mult),
                wl.bufs + invc.bufs, tmpc.bufs)
            dve(lambda h: h.tensor_tensor(out=pooled.t[:, 0:nfix], in0=tmpc.t[:, 0:nfix], in1=Ee.t[:, 16:16 + nfix], op=ALU.subtract),
                tmpc.bufs + Ee.bufs, pooled.bufs)
        psm = nextbank()
        P.op(PE, lambda h: h.matmul(psm.t[:], lhsT=poolw_bf.t[:, g, :], rhs=pooled.t[:, :], start=True, stop=True),
             reads=poolw_bf.bufs + pooled.bufs, writes=psm.bufs)
        act(catT[g].t, psm.t[:], AF.Identity, psm.bufs + pscale_t.bufs, catT[g].bufs, scale=pscale_t.t[:, g:g + 1])

    def chunk(j, c, ysb):
        Hj = Hst[j]
        cs = slice(c * CH, (c + 1) * CH)
        tk = tok3[c % 2]

        def trp(h):
            h.transpose(PST.t[:, 0:128], Bt.t[:, cs], ident.t[:])
            h.transpose(PST.t[:, 128:256], Kt.t[:, cs], ident.t[:])
            return h.transpose(PST.t[:, 256:384], Vt.t[:, cs], ident.t[:])
        P.op(PE, trp, reads=Bt.bufs + Kt.bufs + Vt.bufs + ident.bufs, writes=PST.bufs)
        act(tk.t, PST.t[:, 0:384], AF.Copy, PST.bufs, tk.bufs)
        Btok = lambda hh: tk.t[:, hh * 64:(hh + 1) * 64]
        Ktok = lambda hh: tk.t[:, 128 + hh * 64:128 + (hh + 1) * 64]
        Vtok = lambda hh: tk.t[:, 256 + hh * 64:256 + (hh + 1) * 64]
        lv = {}

        def grams(hh):
            ph = slice(64 * hh, 64 * hh + 64)
            p1, p2, p3 = nextbank(), nextbank(), nextbank()
            arf = AR.t[ph, c, :, :].rearrange("p t i -> p (t i)")
            P.op(PE, lambda h: h.matmul(p1.t[:, 0:256], lhsT=Kt.t[ph, cs], rhs=arf, start=True, stop=True),
                 reads=Kt.bufs + AR.bufs, writes=p1.bufs)
            P.op(PE, lambda h: h.matmul(p2.t[:, 0:256], lhsT=Bt.t[ph, cs], rhs=arf, start=True, stop=True),
                 reads=Bt.bufs + AR.bufs, writes=p2.bufs)
            P.op(PE, lambda h: h.matmul(p3.t[:, 0:128], lhsT=AR.t[ph, c, 0, :], rhs=Bt.t[ph, cs], start=True, stop=True),
                 reads=Bt.bufs + AR.bufs, writes=p3.bufs)
            dve(lambda h: h.tensor_tensor(out=G1M[hh].t, in0=p1.t[:, 0:256], in1=MU2.t[:], op=ALU.mult), p1.bufs + MU2.bufs, G1M[hh].bufs)
            dve(lambda h: h.tensor_tensor(out=G2M[hh].t, in0=p2.t[:, 0:256], in1=MU2.t[:], op=ALU.mult), p2.bufs + MU2.bufs, G2M[hh].bufs)
            dve(lambda h: h.tensor_tensor(out=PPT[hh][0].t, in0=p3.t[:, 0:128], in1=ML.t[:], op=ALU.mult), p3.bufs + ML.bufs, PPT[hh][0].bufs)
            pool(lambda h: h.tensor_tensor(out=TTm[hh][0].t, in0=G2M[hh].t[:, 0:128], in1=ident.t[:], op=ALU.add),
                 G2M[hh].bufs + ident.bufs, TTm[hh][0].bufs)
            lv[hh] = {"P": Tl(G2M[hh].t[:, 0:128], G2M[hh].bufs), "PT": PPT[hh][0], "TT": TTm[hh][0], "pp": 0}
        grams(0)
        grams(1)

        def level(m, hh):
            L = lv[hh]
            npp = 1 - L["pp"] if m > 1 else 1
            Pn, PTn, TTn = PP[hh][npp], PPT[hh][npp], TTm[hh][npp]
            Pt_, PTt_, TTt_ = L["P"], L["PT"], L["TT"]
            if m < 6:
                pa = nextbank()
                P.op(PE, lambda h: h.matmul(pa.t[:, 0:128], lhsT=PTt_.t, rhs=Pt_.t, start=True, stop=True),
                     reads=PTt_.bufs + Pt_.bufs, writes=pa.bufs)
            pb = nextbank()
            P.op(PE, lambda h: h.matmul(pb.t[:, 0:128], lhsT=Pt_.t, rhs=PTt_.t, start=True, stop=True),
                 reads=PTt_.bufs + Pt_.bufs, writes=pb.bufs)
            if m < 6:
                act(Pn.t, pa.t[:, 0:128], AF.Copy, pa.bufs, Pn.bufs)
            act(PTn.t, pb.t[:, 0:128], AF.Copy, pb.bufs, PTn.bufs)
            pc = nextbank()
            P.op(PE, lambda h: h.matmul(pc.t[:, 0:128], lhsT=PTn.t, rhs=TTt_.t, start=True, stop=True),
                 reads=PTn.bufs + TTt_.bufs, writes=pc.bufs)
            dve(lambda h: h.tensor_tensor(out=TTn.t, in0=pc.t[:, 0:128], in1=TTt_.t, op=ALU.add), pc.bufs + TTt_.bufs, TTn.bufs)
            lv[hh] = {"P": Pn, "PT": PTn, "TT": TTn, "pp": npp}
        for m in range(1, 7):
            level(m, 0)
            level(m, 1)
        TTf = [lv[0]["TT"], lv[1]["TT"]]

        dve(lambda h: h.tensor_scalar(out=H0p.t[:], in0=Hj.t[:], scalar1=wc.t[:, c:c + 1], scalar2=None, op0=ALU.mult),
            Hj.bufs + wc.bufs, H0p.bufs)
        act(Hz[0].t[0:64, :], H0p.t[0:64, :], AF.Copy, H0p.bufs, Hz[0].bufs)
        act(Hz[1].t[64:128, :], H0p.t[64:128, :], AF.Copy, H0p.bufs, Hz[1].bufs)
        px = nextbank()

        def mx(h):
            ins = None
            for hh in range(2):
                h.matmul(px.t[:, hh * 64:(hh + 1) * 64], lhsT=AR.t[:, c, 0, :], rhs=Hz[hh].t, start=True, stop=False)
                ins = h.matmul(px.t[:, hh * 64:(hh + 1) * 64], lhsT=G1M[hh].t[:, 0:128], rhs=Vtok(hh), start=False, stop=True)
            return ins
        P.op(PE, mx, reads=AR.bufs + Hz[0].bufs + Hz[1].bufs + G1M[0].bufs + G1M[1].bufs + tk.bufs, writes=px.bufs)
        act(Xbf.t, px.t[:, 0:128], AF.Copy, px.bufs, Xbf.bufs)
        pu_ = nextbank()

        def mu_(h):
            ins = None
            for hh in range(2):
                ins = h.matmul(pu_.t[:, hh * 64:(hh + 1) * 64], lhsT=TTf[hh].t, rhs=Xbf.t[:, hh * 64:(hh + 1) * 64], start=True, stop=True)
            return ins
        P.op(PE, mu_, reads=TTf[0].bufs + TTf[1].bufs + Xbf.bufs, writes=pu_.bufs)
        act(Ubf.t, pu_.t[:, 0:128], AF.Copy, pu_.bufs, Ubf.bufs)
        py, psn = nextbank(), nextbank()

        def my(h):
            ins = None
            for hh in range(2):
                po = slice(64 * hh, 64 * hh + 64)
                h.matmul(py.t[po, 0:128], lhsT=Hz[hh].t, rhs=AR.t[:, c, 1, :], start=True, stop=False)
                h.matmul(py.t[po, 0:128], lhsT=Ubf.t[:, hh * 64:(hh + 1) * 64], rhs=G2M[hh].t[:, 128:256], start=False, stop=False)
                h.matmul(py.t[po, 0:128], lhsT=Vtok(hh), rhs=G1M[hh].t[:, 128:256], start=False, stop=True)
            for hh in range(2):
                po = slice(64 * hh, 64 * hh + 64)
                h.matmul(psn.t[po, 0:64], lhsT=Btok(hh), rhs=Ubf.t[:, hh * 64:(hh + 1) * 64], start=True, stop=False)
                ins = h.matmul(psn.t[po, 0:64], lhsT=Ktok(hh), rhs=Vtok(hh), start=False, stop=True)
            return ins
        P.op(PE, my, reads=Hz[0].bufs + Hz[1].bufs + AR.bufs + Ubf.bufs + G1M[0].bufs + G1M[1].bufs + G2M[0].bufs + G2M[1].bufs + tk.bufs,
             writes=py.bufs + psn.bufs)
        dve(lambda h: h.tensor_tensor(out=Hj.t[:], in0=psn.t[:, 0:64], in1=H0p.t[:], op=ALU.add), psn.bufs + H0p.bufs, Hj.bufs)
        act(ysb.t[:, cs], py.t[:, 0:128], AF.Copy, py.bufs, ysb.bufs)

    def pair(j):
        slot = win_load([4 + j, 16 + j, 28 + j])
        rs, ks, vs = F[I_RS], F[I_KS], F[I_VS]
        ps = proj(slot, 0, 128)
        tshift(ps, 4 + j, rs)
        ps = proj(slot, 1, 128)
        tshift(ps, 16 + j, ks)
        ps = proj(slot, 2, 128)
        tshift(ps, 28 + j, vs)
        pzw, pza, pg_ = nextbank(), nextbank(), nextbank()
        P.op(PE, lambda h: h.matmul(pzw.t[:], lhsT=wa2_bf.t[0:64, j * 128:(j + 1) * 128], rhs=lora_in.t[0:64, :], start=True, stop=True),
             reads=wa2_bf.bufs + lora_in.bufs, writes=pzw.bufs)
        P.op(PE, lambda h: h.matmul(pza.t[:], lhsT=wa2_bf.t[64:128, j * 128:(j + 1) * 128], rhs=lora_in.t[64:128, :], start=True, stop=True),
             reads=wa2_bf.bufs + lora_in.bufs, writes=pza.bufs)

        def mg(h):
            h.matmul(pg_.t[:], lhsT=g2a_bf.t[:, j * 128:(j + 1) * 128], rhs=sg1.t[:, :], start=True, stop=False)
            return h.matmul(pg_.t[:], lhsT=g2b_bf.t[0:96, j * 128:(j + 1) * 128], rhs=sg2.t[0:96, :], start=False, stop=True)
        P.op(PE, mg, reads=g2a_bf.bufs + g2b_bf.bufs + sg1.bufs + sg2.bufs, writes=pg_.bufs)
        e1, l1, nl, a_t, g_t = F[I_E1], F[I_L1], F[I_NL], F[I_A], F[I_G]
        act(e1.t[:, :], pzw.t[:], AF.Exp, pzw.bufs + dvec.bufs, e1.bufs, scale=-1.0, bias=dvec.t[:, j:j + 1])
        act(l1.t[:, :], e1.t[:, :], AF.Ln, e1.bufs, l1.bufs, bias=cst.t[:, 0:1], scale=1.0)
        act(nl.t[:, :], l1.t[:, :], AF.Exp, l1.bufs, nl.bufs, scale=-1.0, bias=cst.t[:, 1:2])
        act(e1.t[:, :], pza.t[:], AF.Exp, pza.bufs + dvec.bufs, e1.bufs, scale=-1.0, bias=dvec.t[:, NPAIR + j:NPAIR + j + 1])
        act(l1.t[:, :], e1.t[:, :], AF.Ln, e1.bufs, l1.bufs, bias=cst.t[:, 0:1], scale=1.0)
        act(a_t.t[:, :], l1.t[:, :], AF.Exp, l1.bufs, a_t.bufs, scale=-1.0)
        act(g_t.t[:, :], pg_.t[:], AF.Copy, pg_.bufs, g_t.bufs)
        kk, k2, bp, bonus = F[I_KK], F[I_K2], F[I_BP], F[I_BONUS]
        dve(lambda h: h.tensor_scalar(out=kk.t[:, :], in0=ks.t[:, :], scalar1=vcol(V_KK, j), scalar2=None, op0=ALU.mult),
            ks.bufs + vec_t.bufs, kk.bufs)
        act(sqb.t[:, :], kk.t[:, :], AF.Square, kk.bufs, sqb.bufs)
        pss = nextbank()
        P.op(PE, lambda h: h.matmul(pss.t[:], lhsT=blk1.t[:], rhs=sqb.t[:, :], start=True, stop=True),
             reads=blk1.bufs + sqb.bufs, writes=pss.bufs)
        act(l1.t[:, :], pss.t[:], AF.Ln, pss.bufs, l1.bufs, bias=cst.t[:, 2:3], scale=1.0)
        act(e1.t[:, :], l1.t[:, :], AF.Exp, l1.bufs, e1.bufs, scale=-0.5)
        dve(lambda h: h.tensor_tensor(out=kk.t[:, :], in0=kk.t[:, :], in1=e1.t[:, :], op=ALU.mult), kk.bufs + e1.bufs, kk.bufs)
        dve(lambda h: h.tensor_scalar(out=k2.t[:, :], in0=a_t.t[:, :], scalar1=vcol(V_KA, j), scalar2=dvec.t[:, 2 * NPAIR + j:2 * NPAIR + j + 1],
                                      op0=ALU.mult, op1=ALU.add), a_t.bufs + vec_t.bufs + dvec.bufs, k2.bufs)
        dve(lambda h: h.tensor_tensor(out=k2.t[:, :], in0=k2.t[:, :], in1=ks.t[:, :], op=ALU.mult), k2.bufs + ks.bufs, k2.bufs)
        dve(lambda h: h.tensor_tensor(out=bp.t[:, :], in0=kk.t[:, :], in1=a_t.t[:, :], op=ALU.mult), kk.bufs + a_t.bufs, bp.bufs)
        dve(lambda h: h.scalar_tensor_tensor(out=sqb.t[:, :], in0=rs.t[:, :], scalar=vcol(V_RK, j), in1=k2.t[:, :], op0=ALU.mult, op1=ALU.mult),
            rs.bufs + k2.bufs + vec_t.bufs, sqb.bufs)
        psr = nextbank()
        P.op(PE, lambda h: h.matmul(psr.t[:], lhsT=blk1.t[:], rhs=sqb.t[:, :], start=True, stop=True),
             reads=blk1.bufs + sqb.bufs, writes=psr.bufs)
        dve(lambda h: h.tensor_tensor(out=bonus.t[:, :], in0=psr.t[:], in1=vs.t[:, :], op=ALU.mult), psr.bufs + vs.bufs, bonus.bufs)
        cwn, cwx, eA, eR, eK = F[I_CWN], F[I_CWX], F[I_EA], F[I_ER], F[I_EK]
        dve(lambda h: h.tensor_tensor_scan(out=cwn.t[:, :], data0=scanmask.t[:], data1=nl.t[:, :], initial=0.0, op0=ALU.mult, op1=ALU.add),
            scanmask.bufs + nl.bufs, cwn.bufs)
        dve(lambda h: h.tensor_tensor(out=cwx.t[:, :], in0=cwn.t[:, :], in1=nl.t[:, :], op=ALU.subtract), cwn.bufs + nl.bufs, cwx.bufs)
        cend = v3(cwn)[:, :, CH - 1]
        dve(lambda h: h.tensor_scalar(out=ncw.t[:], in0=cend, scalar1=-1.0, scalar2=None, op0=ALU.mult), cwn.bufs, ncw.bufs)
        act(wc.t[:], cend, AF.Exp, cwn.bufs, wc.bufs, scale=-1.0)

        def exps(h):
            ins = None
            for c in range(4):
                cs = slice(c * CH, (c + 1) * CH)
                ce = cwn.t[:, c * CH + CH - 1:c * CH + CH]
                h.activation(out=eR.t[:, cs], in_=cwn.t[:, cs], func=AF.Exp, scale=-1.0, bias=ce)
                h.activation(out=eA.t[:, cs], in_=cwx.t[:, cs], func=AF.Exp, scale=-1.0, bias=ce)
                ins = h.activation(out=eK.t[:, cs], in_=cwn.t[:, cs], func=AF.Exp, scale=1.0, bias=ncw.t[:, c:c + 1])
            return ins
        P.op(ACT, exps, reads=cwn.bufs + cwx.bufs + ncw.bufs, writes=eR.bufs + eA.bufs + eK.bufs)
        dve(lambda h: h.scalar_tensor_tensor(out=AR.t[:, :, 0, :], in0=v3(kk), scalar=-1.0, in1=v3(eA), op0=ALU.mult, op1=ALU.mult),
            kk.bufs + eA.bufs, AR.bufs)
        dve(lambda h: h.tensor_tensor(out=AR.t[:, :, 1, :], in0=v3(rs), in1=v3(eR), op=ALU.mult), rs.bufs + eR.bufs, AR.bufs)
        dve(lambda h: h.tensor_tensor(out=Kt.t[:, :], in0=k2.t[:, :], in1=eK.t[:, :], op=ALU.mult), k2.bufs + eK.bufs, Kt.bufs)
        dve(lambda h: h.tensor_tensor(out=Bt.t[:, :], in0=bp.t[:, :], in1=eK.t[:, :], op=ALU.mult), bp.bufs + eK.bufs, Bt.bufs)
        pool(lambda h: h.tensor_copy(Vt.t[:, :], vs.t[:, :]), vs.bufs, Vt.bufs)

        ysb = F[I_Y]
        for c in range(4):
            chunk(j, c, ysb)

        yc = F[I_YC]
        pm = nextbank()
        P.op(PE, lambda h: h.matmul(pm.t[:], lhsT=blkm.t[:], rhs=ysb.t[:, :], start=True, stop=True),
             reads=blkm.bufs + ysb.bufs, writes=pm.bufs)
        dve(lambda h: h.tensor_tensor(out=yc.t[:, :], in0=ysb.t[:, :], in1=pm.t[:], op=ALU.subtract), ysb.bufs + pm.bufs, yc.bufs)
        act(ysb.t[:, :], yc.t[:, :], AF.Square, yc.bufs, ysb.bufs)
        pv = nextbank()
        P.op(PE, lambda h: h.matmul(pv.t[:], lhsT=blkm.t[:], rhs=ysb.t[:, :], start=True, stop=True),
             reads=blkm.bufs + ysb.bufs, writes=pv.bufs)
        act(l1.t[:, :], pv.t[:], AF.Ln, pv.bufs, l1.bufs, bias=cst.t[:, 3:4], scale=1.0)
        act(e1.t[:, :], l1.t[:, :], AF.Exp, l1.bufs, e1.bufs, scale=-0.5)
        dve(lambda h: h.tensor_tensor(out=yc.t[:, :], in0=yc.t[:, :], in1=e1.t[:, :], op=ALU.mult), yc.bufs + e1.bufs, yc.bufs)
        act(ysb.t[:, :], yc.t[:, :], AF.Identity, yc.bufs + vec_t.bufs, ysb.bufs, scale=vcol(V_LW, j), bias=vcol(V_LB, j))
        dve(lambda h: h.tensor_tensor(out=ysb.t[:, :], in0=ysb.t[:, :], in1=bonus.t[:, :], op=ALU.add), ysb.bufs + bonus.bufs, ysb.bufs)
        dve(lambda h: h.tensor_tensor(out=catT[4 + j].t, in0=ysb.t[:, :], in1=g_t.t[:, :], op=ALU.mult), ysb.bufs + g_t.bufs, catT[4 + j].bufs)

    def out_group(g, ps2):
        pacc = [nextbank() for _ in range(4)]
        wrb = []
        for d_ in range(4):
            wrb = wrb + pacc[d_].bufs

        def half_(half):
            k = slot_i["d"] % 2
            slot = WD[k]
            slot_i["d"] += 1
            r0 = half * 8 * 128
            P.op(SP, lambda h: h.dma_start(
                out=slot.t[:, 0:8, :], in_=woutb[r0:r0 + 8 * 128, g * 512:(g + 1) * 512].rearrange("(f p) c -> p f c", p=128)),
                reads=wbuf["wout"], writes=slot.bufs, dma_key=f"wd{k}")

            def mm(h):
                ins = None
                for fl in range(8):
                    cc = half * 8 + fl
                    for d_ in range(4):
                        ins = h.matmul(pacc[d_].t[:], lhsT=slot.t[:, fl, d_ * 128:(d_ + 1) * 128], rhs=catT[cc].t,
                                       start=(cc == 0), stop=(cc == 15))
                return ins
            rd = list(slot.bufs)
            for fl in range(8):
                rd = rd + catT[half * 8 + fl].bufs
            P.op(PE, mm, reads=rd, writes=wrb)
        half_(0)
        half_(1)
        for d_ in range(4):
            evac_f(pacc[d_], g * 4 + d_, ps2)

    src, dst = "x1T", ("x2T" if stage > 2 else "outT")
    final = stage == 2
    for ti in range(NT):
        prenorm_p1(src, ti)
        prenorm_p2(src, ti, 1)
        lora_inputs()
        slot = win_load([0, 1, 2, 3])
        for g in range(4):
            pool_group(ti, slot, g)
        for j in range(NPAIR):
            pair(j)
        ps2 = PSB[5]
        for g in range(4):
            out_group(g, ps2)
        postnorm(src, dst, ti, 1, ps2, final)


def prep_shared(inp):
    f = lambda a: np.ascontiguousarray(np.asarray(a, dtype=np.float32))
    sh = {}
    sh["w_ada"] = f(inp["w_ada"][0])
    sh["b_ada"] = f(np.asarray(inp["b_ada"][0]).reshape(144, 128).T)
    sh["npre"] = f(np.asarray(inp["norm_pre"][0]).reshape(3, 16, 128).transpose(2, 0, 1).reshape(128, 48))
    sh["npost"] = f(np.asarray(inp["norm_post"][0]).reshape(3, 16, 128).transpose(2, 0, 1).reshape(128, 48))
    sh["wg1"] = f(inp["ffn1_w_gate"][0]); sh["wu1"] = f(inp["ffn1_w_up"][0]); sh["wd1"] = f(inp["ffn1_w_down"][0])
    sh["wg2"] = f(inp["ffn2_w_gate"][0]); sh["wu2"] = f(inp["ffn2_w_up"][0]); sh["wd2"] = f(inp["ffn2_w_down"][0])
    sh["win"] = f(inp["w_in"][0]); sh["wout"] = f(inp["w_out"][0])
    mu = np.zeros(43 * 128, np.float32)
    mu[512:512 + 4960] = np.asarray(inp["mu_shift"][0])
    sh["muT"] = f(mu.reshape(43, 128).T)
    sh["poolw"] = f(np.asarray(inp["pool_w"][0]).transpose(1, 0, 2).reshape(128, 512))
    sh["pscale"] = f(np.asarray(inp["pool_scale"][0]).reshape(4, 128).T)
    vs = [inp["w0"][0], inp["a0"][0], inp["k_k"][0], inp["k_a"][0], np.asarray(inp["r_k"][0]).reshape(-1), inp["lnx_w"][0], inp["lnx_b"][0]]
    sh["vecs"] = f(np.concatenate([np.asarray(v).reshape(NPAIR, 128).T for v in vs], axis=1))
    sh["wa2"] = f(np.concatenate([np.asarray(inp["w2"][0]), np.asarray(inp["a2"][0])], axis=0))
    g2 = np.asarray(inp["g2"][0])
    sh["g2a"] = f(g2[0:128]); sh["g2b"] = f(g2[128:224])
    return sh


def prep_core(inp, b):
    return {"xT": np.ascontiguousarray(np.asarray(inp["x"][b], dtype=np.float32).T),
            "cT": np.ascontiguousarray(np.asarray(inp["c"][b], dtype=np.float32).reshape(16, 128).T)}


_NEEDED = {1: ["xT", "cT", "w_ada", "b_ada", "npre", "npost", "wg1", "wu1", "wd1"]}


def kernel(**inputs):
    x = np.asarray(inputs["x"])
    B, T, _ = x.shape
    nc = build_program(T, stage=3)
    sh = prep_shared(inputs)
    in_maps = []
    for b in range(B):
        m = dict(sh)
        m.update(prep_core(inputs, b))
        in_maps.append(m)
    res = run_bass_kernel_spmd(nc, in_maps, core_ids=list(range(B)))
    out = np.stack([np.ascontiguousarray(r["outT"].T) for r in res.results], axis=0)
    return out.astype(np.float32)
```

```python
import numpy as np
from contextlib import ExitStack
import concourse.bass as bass
import concourse.mybir as mybir
from concourse.bass_utils import run_bass_kernel_spmd

F32 = mybir.dt.float32
BF16 = mybir.dt.bfloat16
AF = mybir.ActivationFunctionType
ALU = mybir.AluOpType

PE, ACT, DVE, POOL, SP = "pe", "act", "dve", "pool", "sp"
ENGINES = [PE, ACT, DVE, POOL, SP]
EPOCH = 20000

D = 2048
FF = 5632
NKC = 16
NFC = 44
TT = 512
CH = 128
INW = 5472
RW = 1536
NPAIR = 12
NORM_EPS = 1e-6
LNX_EPS = 1e-5 * 64


class Buf:
    __slots__ = ("name", "last_w", "readers")

    def __init__(self, name):
        self.name = name
        self.last_w = None
        self.readers = []


class Op:
    __slots__ = ("eng", "emit", "deps", "is_dma", "dsem", "need_sig", "sig")

    def __init__(self, eng, emit, is_dma, dsem):
        self.eng = eng
        self.emit = emit
        self.deps = []
        self.is_dma = is_dma
        self.dsem = dsem
        self.need_sig = is_dma
        self.sig = None


class Prog:
    def __init__(self, nc, stack):
        self.nc = nc
        self.stack = stack
        self.ops = {e: [] for e in ENGINES}
        self.dma_sems = {}
        self.key_bufs = {}
        self.final_waits = []

    def new_sem(self, name):
        return self.stack.enter_context(self.nc.semaphore(name))

    def op(self, eng, emit, reads=(), writes=(), dma_key=None):
        is_dma = dma_key is not None
        dsem = None
        if is_dma:
            if dma_key not in self.dma_sems:
                self.dma_sems[dma_key] = [self.new_sem("d_" + dma_key), 0]
                self.key_bufs[dma_key] = Buf("k_" + dma_key)
            dsem = self.dma_sems[dma_key]
            writes = list(writes) + [self.key_bufs[dma_key]]
        o = Op(eng, emit, is_dma, dsem)
        deps = []

        def add(p, kind):
            if p is None or p is o:
                return
            if not p.is_dma and not is_dma and p.eng == eng:
                if eng == PE or kind == "war":
                    return
            deps.append(p)

        for b in reads:
            add(b.last_w, "raw")
        for b in writes:
            add(b.last_w, "waw")
            for r in b.readers:
                add(r, "war")
        for b in reads:
            b.readers.append(o)
        for b in writes:
            b.last_w = o
            b.readers = []
        seen = set()
        for p in deps:
            if id(p) not in seen:
                seen.add(id(p))
                p.need_sig = True
                o.deps.append(p)
        self.ops[eng].append(o)
        return o

    def emit_all(self):
        nc = self.nc
        for e in ENGINES:
            cnt = 0
            sems = []
            for o in self.ops[e]:
                if o.is_dma:
                    o.dsem[1] += 16
                    o.sig = (o.dsem[0], o.dsem[1], 16)
                elif o.need_sig:
                    ep = cnt // EPOCH
                    if ep >= len(sems):
                        sems.append(self.new_sem(f"s_{e}_{ep}"))
                    o.sig = (sems[ep], cnt % EPOCH + 1, 1)
                    cnt += 1
        final_waits = self.final_waits

        def run(e, h):
            waited = {}
            for o in self.ops[e]:
                for p in o.deps:
                    sem, val, _ = p.sig
                    k = id(sem)
                    if waited.get(k, 0) < val:
                        h.wait_ge(sem, val)
                        waited[k] = val
                ins = o.emit(h)
                if o.need_sig:
                    ins.then_inc(o.sig[0], o.sig[2])
            if e == SP:
                for o in final_waits:
                    sem, val, _ = o.sig
                    if waited.get(id(sem), 0) < val:
                        h.wait_ge(sem, val)
                        waited[id(sem)] = val

        with nc.Block() as block:
            @block.tensor
            def _(h):
                run(PE, h)

            @block.scalar
            def _(h):
                run(ACT, h)

            @block.vector
            def _(h):
                run(DVE, h)

            @block.gpsimd
            def _(h):
                run(POOL, h)

            @block.sync
            def _(h):
                run(SP, h)


class Tl:
    __slots__ = ("t", "bufs")

    def __init__(self, t, bufs):
        self.t = t
        self.bufs = bufs


def build_program(T, stage=3, debug=False):
    NT = T // TT
    nc = bass.Bass("TRN2", target_bir_lowering=False)

    def din(name, shape, dt=F32):
        return nc.dram_tensor(name, list(shape), dt, kind="ExternalInput").ap()

    def dscr(name, shape, dt):
        return nc.dram_tensor(name, list(shape), dt, kind="Internal").ap()

    xT = din("xT", [D, T])
    cT = din("cT", [128, NKC])
    w_ada = din("w_ada", [D, 9 * D])
    b_ada = din("b_ada", [128, 144])
    npre = din("npre", [128, 48])
    npost = din("npost", [128, 48])
    wsrc = {}
    for nm, shp in [("wg1", [D, FF]), ("wu1", [D, FF]), ("wd1", [FF, D]), ("win", [D, INW]), ("wout", [D, D]),
                    ("wg2", [D, FF]), ("wu2", [D, FF]), ("wd2", [FF, D])]:
        wsrc[nm] = (din(nm, shp), dscr(nm + "_b", shp, BF16), shp)
    muT = din("muT", [128, 43])
    poolw = din("poolw", [128, 4 * 128])
    pscale = din("pscale", [128, 4])
    vecs = din("vecs", [128, 7 * NPAIR])
    wa2 = din("wa2", [128, RW])
    g2a = din("g2a", [128, RW])
    g2b = din("g2b", [96, RW])
    outT = nc.dram_tensor("outT", [D, T], F32, kind="ExternalOutput").ap()
    x1T = dscr("x1T", [D, T], F32)
    x2T = dscr("x2T", [D, T], F32)

    with ExitStack() as st:
        P = Prog(nc, st)

        def sb(name, shape, dt=F32):
            return st.enter_context(nc.sbuf_tensor(name, list(shape), dt))

        def tile(name, shape, dt=F32):
            return Tl(sb(name, shape, dt), [Buf(name)])

        PSB = []
        for b in range(7):
            PSB.append(Tl(st.enter_context(nc.psum_tensor(f"ps{b}", [128, 512], F32)), [Buf(f"ps{b}")]))
        PST = Tl(st.enter_context(nc.psum_tensor("pst", [128, 1024], BF16)), [Buf("pst")])
        rr = {"i": 0}
        WORK = [0, 1, 2, 3, 4, 6]

        def nextbank():
            b = WORK[rr["i"] % len(WORK)]
            rr["i"] += 1
            return PSB[b]

        NSLAB = 44
        arena = sb("arena", [128, NSLAB * 512], BF16)
        slabB = [Buf(f"slab{i}") for i in range(NSLAB)]
        arena_f32 = None

        def slab_bf(i, n=1):
            return Tl(arena[:, i * 512:(i + n) * 512], slabB[i:i + n])


        hT = tile("hT", [128, NKC, TT], BF16)
        fbuf = Tl(sb("fbuf", [128, NKC, TT], F32), [Buf(f"fbuf{i}") for i in range(NKC)])
        NXR = 3
        xr = [tile(f"xr{i}", [128, 2, TT], F32) for i in range(NXR)]
        xr_i = {"i": 0}
        sq = [tile(f"sq{i}", [128, TT], BF16) for i in range(2)]
        sq_i = {"i": 0}
        rstd = tile("rstd", [128, TT], F32)
        rstd2 = tile("rstd2", [128, TT], F32)
        lntmp = tile("lntmp", [128, TT], F32)
        t32 = [tile(f"t32_{i}", [128, TT], F32) for i in range(4)]
        t32_i = {"i": 0}

        def next_t32():
            t = t32[t32_i["i"] % len(t32)]
            t32_i["i"] += 1
            return t

        WGUF = [tile(f"wguslot{i}", [128, 2 * NKC * 256], BF16) for i in range(2)]
        WGU = [Tl(w.t[:, :].rearrange("p (a k c) -> p a k c", a=2, k=NKC), w.bufs) for w in WGUF]
        WGU4 = [Tl(w.t[:, :].rearrange("p (a k c) -> p a k c", a=4, k=NKC), w.bufs) for w in WGUF]
        WD = [tile(f"wdslot{i}", [128, 11, 512], BF16) for i in range(2)]

        ones_mean = tile("ones_mean", [128, 128], BF16)
        blk1 = tile("blk1", [128, 128], BF16)
        blkm = tile("blkm", [128, 128], F32)
        ident = tile("ident", [128, 128], BF16)
        MU2 = tile("MU2", [128, 256], BF16)
        ML = tile("ML", [128, 128], BF16)
        scanmask = tile("scanmask", [128, TT], F32)
        mod = tile("mod", [128, 144], F32)
        bada = tile("bada", [128, 144], F32)
        npre_t = tile("npre_t", [128, 48], F32)
        npost_t = tile("npost_t", [128, 48], F32)
        gA = tile("gA", [128, 48], F32)
        gB = tile("gB", [128, 48], F32)
        c_t = tile("c_t", [128, NKC], F32)
        cst = tile("cst", [128, 8], F32)
        sc_bf = tile("sc_bf", [128, NKC], BF16)

        def cv(h):
            h.memset(ones_mean.t[:], 1.0 / D)
            h.memset(blk1.t[:], 0.0)
            h.memset(blk1.t[0:64, 0:64], 1.0)
            h.memset(blk1.t[64:128, 64:128], 1.0)
            h.memset(blkm.t[:], 0.0)
            h.memset(blkm.t[0:64, 0:64], 1.0 / 64)
            h.memset(blkm.t[64:128, 64:128], 1.0 / 64)
            h.memset(scanmask.t[:], 1.0)
            for i_, v_ in enumerate([1.0, -0.5, 1e-18, LNX_EPS, NORM_EPS]):
                h.memset(cst.t[:, i_:i_ + 1], float(v_))
            ins = None
            for c in range(TT // CH):
                ins = h.memset(scanmask.t[:, c * CH:c * CH + 1], 0.0)
            return ins
        P.op(DVE, cv, writes=ones_mean.bufs + blk1.bufs + blkm.bufs + scanmask.bufs + cst.bufs)

        def cp(h):
            h.memset(ident.t[:], 1.0)
            h.affine_select(out=ident.t[:], in_=ident.t[:], pattern=[[1, 128]], compare_op=ALU.is_equal,
                            fill=0.0, base=0, channel_multiplier=-1)
            h.memset(MU2.t[:], 1.0)
            h.affine_select(out=MU2.t[:, 0:128], in_=MU2.t[:, 0:128], pattern=[[1, 128]], compare_op=ALU.is_gt,
                            fill=0.0, base=0, channel_multiplier=-1)
            h.affine_select(out=MU2.t[:, 128:256], in_=MU2.t[:, 128:256], pattern=[[1, 128]], compare_op=ALU.is_ge,
                            fill=0.0, base=0, channel_multiplier=-1)
            h.memset(ML.t[:], 1.0)
            return h.affine_select(out=ML.t[:], in_=ML.t[:], pattern=[[-1, 128]], compare_op=ALU.is_gt,
                                   fill=0.0, base=0, channel_multiplier=1)
        P.op(POOL, cp, writes=ident.bufs + MU2.bufs + ML.bufs)

        wbuf = {}
        conv_i = {"i": 0}

        def convert(nm):
            src, dst, shp = wsrc[nm]
            rows = shp[0]
            blk = 256 if shp[1] > 2048 else 512
            bufs = []
            for r0 in range(0, rows, blk):
                b = Buf(f"{nm}_{r0}")
                bufs.append(b)
                key = f"cv{conv_i['i'] % 16}"
                conv_i["i"] += 1
                P.op(POOL, (lambda h, r0=r0, blk=blk: h.dma_start(out=dst[r0:r0 + blk, :], in_=src[r0:r0 + blk, :])),
                     writes=[b], dma_key=key)
            wbuf[nm] = bufs

        def load_small(dst_tile, src_ap, eng=SP):
            P.op(eng, lambda h: h.dma_start(out=dst_tile.t[:], in_=src_ap), writes=dst_tile.bufs, dma_key="par")

        load_small(c_t, cT[:, :])
        load_small(bada, b_ada[:, :])
        load_small(npre_t, npre[:, :])
        load_small(npost_t, npost[:, :])

        convert("wg1")
        convert("wu1")

        P.op(ACT, lambda h: h.activation(out=sc_bf.t[:], in_=c_t.t[:], func=AF.Silu), reads=c_t.bufs, writes=sc_bf.bufs)
        ADA_F = [Tl(fbuf.t[:, i * 8:(i + 1) * 8, :].rearrange("p a t -> p (a t)").rearrange("p (k n) -> p k n", k=NKC),
                    fbuf.bufs[i * 8:(i + 1) * 8]) for i in range(2)]
        ADA_B = [Tl(arena[:, i * 8 * 512:(i + 1) * 8 * 512].rearrange("p (k n) -> p k n", k=NKC),
                    slabB[i * 8:(i + 1) * 8]) for i in range(4)]
        ps_mod = PSB[5]
        for sl in range(72):
            stg = ADA_F[sl % 2]
            slot = ADA_B[sl % 4]
            n0 = sl * 256
            P.op(SP, (lambda h, stg=stg, n0=n0: h.dma_start(
                out=stg.t, in_=w_ada[:, n0:n0 + 256].rearrange("(k p) n -> p k n", p=128))),
                writes=stg.bufs, dma_key=f"ada{sl % 2}")
            if sl % 2 == 0:
                P.op(DVE, (lambda h, stg=stg, slot=slot: h.tensor_copy(slot.t, stg.t)), reads=stg.bufs, writes=slot.bufs)
            else:
                P.op(ACT, (lambda h, stg=stg, slot=slot: h.activation(out=slot.t, in_=stg.t, func=AF.Copy)),
                     reads=stg.bufs, writes=slot.bufs)

            def mm(h, slot=slot, sl=sl):
                ins = None
                for jj in range(2):
                    j = sl * 2 + jj
                    for kc in range(NKC):
                        ins = h.matmul(ps_mod.t[:, j:j + 1], lhsT=slot.t[:, kc, jj * 128:(jj + 1) * 128],
                                       rhs=sc_bf.t[:, kc:kc + 1], start=(kc == 0), stop=(kc == NKC - 1))
                return ins
            P.op(PE, mm, reads=slot.bufs + sc_bf.bufs, writes=ps_mod.bufs)
        P.op(DVE, lambda h: h.tensor_tensor(out=mod.t[:], in0=ps_mod.t[:, 0:144], in1=bada.t[:], op=ALU.add),
             reads=ps_mod.bufs + bada.bufs, writes=mod.bufs)

        def modcol(s, m):
            return mod.t[:, (s * 3 + m) * 16:(s * 3 + m) * 16 + 16]

        def mkvec(h):
            ins = None
            for s in range(3):
                wt = 1.0 if s == 1 else 0.5
                h.scalar_tensor_tensor(out=gA.t[:, s * 16:(s + 1) * 16], in0=modcol(s, 1), scalar=1.0,
                                       in1=npre_t.t[:, s * 16:(s + 1) * 16], op0=ALU.add, op1=ALU.mult)
                ins = h.scalar_tensor_tensor(out=gB.t[:, s * 16:(s + 1) * 16], in0=modcol(s, 2), scalar=1.0,
                                             in1=npost_t.t[:, s * 16:(s + 1) * 16], op0=ALU.add, op1=ALU.mult)
            return ins
        P.op(DVE, mkvec, reads=mod.bufs + npre_t.bufs + npost_t.bufs, writes=gA.bufs + gB.bufs)

        def mkvec2(h):
            h.tensor_scalar(out=gB.t[:, 0:16], in0=gB.t[:, 0:16], scalar1=0.5, scalar2=None, op0=ALU.mult)
            return h.tensor_scalar(out=gB.t[:, 32:48], in0=gB.t[:, 32:48], scalar1=0.5, scalar2=None, op0=ALU.mult)
        P.op(DVE, mkvec2, reads=gB.bufs, writes=gB.bufs)

        convert("wd1")
        if stage >= 2:
            convert("win")
            convert("wout")
        if stage >= 3:
            convert("wg2")
            convert("wu2")
            convert("wd2")

        xbufs = {"xT": [Buf(f"xT{i}") for i in range(NT)], "x1T": [Buf(f"x1T{i}") for i in range(NT)],
                 "x2T": [Buf(f"x2T{i}") for i in range(NT)], "outT": [Buf(f"outT{i}") for i in range(NT)]}
        xaps = {"xT": xT, "x1T": x1T, "x2T": x2T, "outT": outT}

        def next_xr():
            t = xr[xr_i["i"] % NXR]
            k = xr_i["i"] % NXR
            xr_i["i"] += 1
            return t, k

        def load_x(src, ti, kc0):
            t, k = next_xr()
            ap = xaps[src][kc0 * 128:(kc0 + 2) * 128, ti * TT:(ti + 1) * TT].rearrange("(k p) t -> p k t", p=128)
            P.op(SP, lambda h: h.dma_start(out=t.t[:], in_=ap), reads=[xbufs[src][ti]], writes=t.bufs, dma_key=f"xr{k}")
            return t, k

        def rsqrt_from_psum(ps, out_t, eps):
            P.op(ACT, lambda h: h.activation(out=lntmp.t[:], in_=ps.t[:], func=AF.Ln, bias=cst.t[:, 4:5], scale=1.0),
                 reads=ps.bufs + cst.bufs, writes=lntmp.bufs)
            P.op(ACT, lambda h: h.activation(out=out_t.t[:], in_=lntmp.t[:], func=AF.Exp, scale=-0.5),
                 reads=lntmp.bufs, writes=out_t.bufs)

        def prenorm_p1(src, ti):
            ps = PSB[5]
            for kc0 in range(0, NKC, 2):
                t, _ = load_x(src, ti, kc0)
                for q in range(2):
                    kc = kc0 + q
                    s_ = sq[sq_i["i"] % 2]
                    sq_i["i"] += 1
                    P.op(ACT, (lambda h, t=t, q=q, s_=s_: h.activation(out=s_.t[:], in_=t.t[:, q, :], func=AF.Square)),
                         reads=t.bufs, writes=s_.bufs)
                    P.op(PE, (lambda h, s_=s_, kc=kc: h.matmul(ps.t[:], lhsT=ones_mean.t[:], rhs=s_.t[:],
                                                                start=(kc == 0), stop=(kc == NKC - 1))),
                         reads=s_.bufs + ones_mean.bufs, writes=ps.bufs)
            rsqrt_from_psum(ps, rstd, NORM_EPS)

        def prenorm_p2(src, ti, s):
            for kc0 in range(0, NKC, 2):
                t, _ = load_x(src, ti, kc0)
                for q in range(2):
                    kc = kc0 + q
                    tmp = next_t32()
                    P.op(DVE, (lambda h, t=t, q=q, tmp=tmp, kc=kc: h.scalar_tensor_tensor(
                        out=tmp.t[:], in0=t.t[:, q, :], scalar=gA.t[:, s * 16 + kc:s * 16 + kc + 1], in1=rstd.t[:],
                        op0=ALU.mult, op1=ALU.mult)), reads=t.bufs + gA.bufs + rstd.bufs, writes=tmp.bufs)
                    P.op(ACT, (lambda h, tmp=tmp, kc=kc: h.activation(
                        out=hT.t[:, kc, :], in_=tmp.t[:], func=AF.Identity,
                        bias=mod.t[:, (s * 3) * 16 + kc:(s * 3) * 16 + kc + 1], scale=1.0)),
                        reads=tmp.bufs + mod.bufs, writes=hT.bufs)

        def evac_f(ps, dc, ps2):
            P.op(ACT, lambda h: h.activation(out=fbuf.t[:, dc, :], in_=ps.t[:], func=AF.Copy),
                 reads=ps.bufs, writes=[fbuf.bufs[dc]])
            s_ = sq[sq_i["i"] % 2]
            sq_i["i"] += 1
            P.op(ACT, lambda h: h.activation(out=s_.t[:], in_=ps.t[:], func=AF.Square), reads=ps.bufs, writes=s_.bufs)
            P.op(PE, lambda h: h.matmul(ps2.t[:], lhsT=ones_mean.t[:], rhs=s_.t[:], start=(dc == 0), stop=(dc == NKC - 1)),
                 reads=s_.bufs + ones_mean.bufs, writes=ps2.bufs)

        def postnorm(src, dst, ti, s, ps2, final):
            rsqrt_from_psum(ps2, rstd2, NORM_EPS)
            for kc0 in range(0, NKC, 2):
                t, k = load_x(src, ti, kc0)
                for q in range(2):
                    kc = kc0 + q
                    tmp = next_t32()
                    P.op(DVE, (lambda h, tmp=tmp, kc=kc: h.scalar_tensor_tensor(
                        out=tmp.t[:], in0=fbuf.t[:, kc, :], scalar=gB.t[:, s * 16 + kc:s * 16 + kc + 1], in1=rstd2.t[:],
                        op0=ALU.mult, op1=ALU.mult)), reads=[fbuf.bufs[kc]] + gB.bufs + rstd2.bufs, writes=tmp.bufs)
                    P.op(DVE, (lambda h, t=t, q=q, tmp=tmp: h.tensor_tensor(out=t.t[:, q, :], in0=t.t[:, q, :], in1=tmp.t[:],
                                                                            op=ALU.add)),
                         reads=tmp.bufs + t.bufs, writes=t.bufs)
                ap = xaps[dst][kc0 * 128:(kc0 + 2) * 128, ti * TT:(ti + 1) * TT].rearrange("(k p) t -> p k t", p=128)
                o = P.op(SP, (lambda h, t=t, ap=ap: h.dma_start(out=ap, in_=t.t[:])), reads=t.bufs,
                         writes=[xbufs[dst][ti]], dma_key=f"st{k}")
                if final:
                    P.final_waits.append(o)

        def ffn_phase(s, src, dst, wg, wu, wd, final):
            wgb, wub, wdb = wsrc[wg][1], wsrc[wu][1], wsrc[wd][1]
            UT = [slab_bf(i) for i in range(NFC)]
            NGU = NT * (NFC // 2)
            NWD = NT * 16

            def gu_load(idx):
                if idx >= NGU:
                    return
                fp = idx % (NFC // 2)
                k = idx % 2
                slot = WGU[k]
                c0 = fp * 256
                P.op(SP, lambda h: h.dma_start(out=slot.t[:, 0, :, :], in_=wgb[:, c0:c0 + 256].rearrange("(k p) c -> p k c", p=128)),
                     reads=wbuf[wg], writes=slot.bufs, dma_key=f"wg{k}")
                P.op(SP, lambda h: h.dma_start(out=slot.t[:, 1, :, :], in_=wub[:, c0:c0 + 256].rearrange("(k p) c -> p k c", p=128)),
                     reads=wbuf[wu], writes=slot.bufs, dma_key=f"wu{k}")

            def wd_load(idx):
                if idx >= NWD:
                    return
                g = (idx % 16) // 4
                qq = idx % 4
                k = idx % 2
                slot = WD[k]
                r0 = qq * 11 * 128
                P.op(SP, lambda h: h.dma_start(
                    out=slot.t[:], in_=wdb[r0:r0 + 11 * 128, g * 512:(g + 1) * 512].rearrange("(f p) c -> p f c", p=128)),
                    reads=wbuf[wd], writes=slot.bufs, dma_key=f"wd{k}")

            def gu_step(idx):
                fp = idx % (NFC // 2)
                slot = WGU[idx % 2]
                for q in range(2):
                    gu_chunk(slot, fp * 2 + q, q)
                gu_load(idx + 2)

            def gu_chunk(slot, fc, q):
                pg = PSB[fc % 2]
                pu = PSB[2 + fc % 2]

                def mm(h):
                    ins = None
                    for kc in range(NKC):
                        h.matmul(pg.t[:], lhsT=slot.t[:, 0, kc, q * 128:(q + 1) * 128], rhs=hT.t[:, kc, :],
                                 start=(kc == 0), stop=(kc == NKC - 1))
                    for kc in range(NKC):
                        ins = h.matmul(pu.t[:], lhsT=slot.t[:, 1, kc, q * 128:(q + 1) * 128], rhs=hT.t[:, kc, :],
                                       start=(kc == 0), stop=(kc == NKC - 1))
                    return ins
                P.op(PE, mm, reads=slot.bufs + hT.bufs, writes=pg.bufs + pu.bufs)
                sg = next_t32()
                P.op(ACT, lambda h: h.activation(out=sg.t[:], in_=pg.t[:], func=AF.Silu), reads=pg.bufs, writes=sg.bufs)
                u = UT[fc]
                P.op(DVE, lambda h: h.tensor_tensor(out=u.t, in0=sg.t[:], in1=pu.t[:], op=ALU.mult),
                     reads=sg.bufs + pu.bufs, writes=u.bufs)

            def wd_step(idx):
                qq = idx % 4
                slot = WD[idx % 2]

                def mm(h):
                    ins = None
                    for fl in range(11):
                        fc = qq * 11 + fl
                        for d_ in range(4):
                            ins = h.matmul(PSB[d_].t[:], lhsT=slot.t[:, fl, d_ * 128:(d_ + 1) * 128], rhs=UT[fc].t,
                                           start=(fc == 0), stop=(fc == NFC - 1))
                    return ins
                rd = list(slot.bufs)
                for fl in range(11):
                    rd = rd + UT[qq * 11 + fl].bufs
                P.op(PE, mm, reads=rd, writes=PSB[0].bufs + PSB[1].bufs + PSB[2].bufs + PSB[3].bufs)
                wd_load(idx + 2)

            prenorm_p1(src, 0)
            prenorm_p2(src, 0, s)
            gu_load(0)
            gu_load(1)
            for ti in range(NT):
                for fp in range(NFC // 2):
                    gu_step(ti * (NFC // 2) + fp)
                    if ti == 0 and fp == 11:
                        wd_load(0)
                        wd_load(1)
                if ti + 1 < NT:
                    prenorm_p1(src, ti + 1)
                ps2 = PSB[4]
                for g in range(4):
                    for qq in range(4):
                        wd_step(ti * 16 + g * 4 + qq)
                    for d_ in range(4):
                        evac_f(PSB[d_], g * 4 + d_, ps2)
                    if g == 0 and ti + 1 < NT:
                        prenorm_p2(src, ti + 1, s)
                postnorm(src, dst, ti, s, ps2, final)

        ffn_phase(0, "xT", "x1T" if stage > 1 else "outT", "wg1", "wu1", "wd1", final=(stage == 1))

        if stage >= 2:
            mixer_phase(nc, P, locals())
        if stage >= 3:
            ffn_phase(2, "x2T", "outT", "wg2", "wu2", "wd2", final=True)

        P.emit_all()
    return nc


def mixer_phase(nc, P, E):
    sb = E["sb"]; tile = E["tile"]; PSB = E["PSB"]; PST = E["PST"]; nextbank = E["nextbank"]
    hT = E["hT"]; fbuf = E["fbuf"]; WGU4 = E["WGU4"]; WD = E["WD"]; wsrc = E["wsrc"]; wbuf = E["wbuf"]
    NT = E["NT"]; stage = E["stage"]; arena = E["arena"]; slabB = E["slabB"]; t32 = E["t32"]
    blk1 = E["blk1"]; blkm = E["blkm"]; ident = E["ident"]; MU2 = E["MU2"]; ML = E["ML"]; scanmask = E["scanmask"]
    prenorm_p1 = E["prenorm_p1"]; prenorm_p2 = E["prenorm_p2"]; postnorm = E["postnorm"]; evac_f = E["evac_f"]
    cst = E["cst"]; muT = E["muT"]; poolw = E["poolw"]; pscale = E["pscale"]; vecs = E["vecs"]; wa2 = E["wa2"]; g2a = E["g2a"]; g2b = E["g2b"]
    winb = wsrc["win"][1]; woutb = wsrc["wout"][1]

    def ft(i):
        return Tl(fbuf.t[:, i, :], [fbuf.bufs[i]])
    (I_RS, I_KS, I_VS, I_E1, I_L1, I_NL, I_A, I_G, I_KK, I_K2, I_BP, I_BONUS, I_CWN, I_EA, I_ER, I_EK) = range(16)
    F = [ft(i) for i in range(16)]
    I_CWX, I_Y, I_YC = I_NL, I_EA, I_ER

    def slab(i, n=1):
        return Tl(arena[:, i * 512:(i + n) * 512], slabB[i:i + n])
    catT = [slab(i) for i in range(16)]
    lora_in = slab(16)
    sg1 = slab(17)
    sg2 = slab(18)
    AR = Tl(arena[:, 19 * 512:21 * 512].rearrange("p (c t i) -> p c t i", c=4, t=2), slabB[19:21])
    Kt = slab(21)
    Bt = slab(22)
    Vt = slab(23)
    sqb = slab(24)
    pooled = slab(25)
    tok3 = [Tl(arena[:, (26 + i) * 512:(26 + i) * 512 + 384], [slabB[26 + i]]) for i in range(2)]

    def mat(sl, q, nm):
        return Tl(arena[:, sl * 512 + q * 128: sl * 512 + (q + 1) * 128], [Buf(nm)])
    G1M = [Tl(arena[:, 28 * 512 + hh * 256:28 * 512 + (hh + 1) * 256], [Buf(f"G1M{hh}")]) for hh in range(2)]
    G2M = [Tl(arena[:, 29 * 512 + hh * 256:29 * 512 + (hh + 1) * 256], [Buf(f"G2M{hh}")]) for hh in range(2)]
    PP = [[mat(30 + hh, pp, f"P{hh}{pp}") for pp in range(2)] for hh in range(2)]
    PPT = [[mat(30 + hh, 2 + pp, f"PT{hh}{pp}") for pp in range(2)] for hh in range(2)]
    TTm = [[mat(32, hh * 2 + pp, f"TT{hh}{pp}") for pp in range(2)] for hh in range(2)]
    Xbf = mat(33, 0, "Xbf")
    Ubf = mat(33, 1, "Ubf")
    Hz = [Tl(arena[:, 33 * 512 + 256 + hh * 64: 33 * 512 + 256 + (hh + 1) * 64], [Buf(f"Hz{hh}")]) for hh in range(2)]
    poolw_bf = Tl(arena[:, 34 * 512:35 * 512].rearrange("p (g d) -> p g d", g=4), [slabB[34]])
    wa2_bf = Tl(arena[:, 35 * 512:38 * 512], slabB[35:38])
    g2a_bf = Tl(arena[:, 38 * 512:41 * 512], slabB[38:41])
    g2b_bf = Tl(arena[:, 41 * 512:44 * 512], slabB[41:44])

    Hst = [tile(f"H{j}", [128, 64], F32) for j in range(NPAIR)]
    H0p = tile("H0p", [128, 64], F32)
    Ee = tile("Ee", [128, 16 + TT], F32)
    pcar = [tile(f"pcar{g}", [128, 16], F32) for g in range(4)]
    wtmp = [tile(f"wtmp{i}", [128, 16 + TT], F32) for i in range(2)]
    carry = tile("carry", [128, 43], F32)
    mu_t = tile("mu_t", [128, 43], F32)
    omu_t = tile("omu_t", [128, 43], F32)
    vec_t = tile("vec_t", [128, 7 * NPAIR], F32)
    dvec = tile("dvec", [128, 3 * NPAIR], F32)
    pscale_t = tile("pscale_t", [128, 4], F32)
    invc = tile("invc", [128, 16], F32)
    ncw = tile("ncw", [128, 4], F32)
    wc = tile("wc", [128, 4], F32)
    stage32 = Tl(fbuf.t[:, 0:3, :].rearrange("p a t -> p (a t)"), fbuf.bufs[0:3])

    V_W0, V_A0, V_KK, V_KA, V_RK, V_LW, V_LB = range(7)

    def vcol(v, j):
        return vec_t.t[:, v * NPAIR + j:v * NPAIR + j + 1]

    def act(out_ap, in_ap, func, reads, writes, **kw):
        P.op(ACT, lambda h: h.activation(out=out_ap, in_=in_ap, func=func, **kw), reads=reads, writes=writes)

    def dve(fn, reads, writes):
        P.op(DVE, fn, reads=reads, writes=writes)

    def pool(fn, reads, writes):
        P.op(POOL, fn, reads=reads, writes=writes)

    def ld(dst_ap, src_ap, bufs):
        P.op(SP, lambda h: h.dma_start(out=dst_ap, in_=src_ap), writes=bufs, dma_key="par")
    ld(mu_t.t[:], muT[:, :], mu_t.bufs)
    ld(vec_t.t[:], vecs[:, :], vec_t.bufs)
    ld(pscale_t.t[:], pscale[:, :], pscale_t.bufs)

    def setup_v(h):
        h.tensor_scalar(out=omu_t.t[:], in0=mu_t.t[:], scalar1=-1.0, scalar2=1.0, op0=ALU.mult, op1=ALU.add)
        h.tensor_scalar(out=dvec.t[:, 0:2 * NPAIR], in0=vec_t.t[:, 0:2 * NPAIR], scalar1=-1.0, scalar2=None, op0=ALU.mult)
        h.tensor_scalar(out=dvec.t[:, 2 * NPAIR:3 * NPAIR], in0=vec_t.t[:, V_KA * NPAIR:(V_KA + 1) * NPAIR],
                        scalar1=-1.0, scalar2=1.0, op0=ALU.mult, op1=ALU.add)
        h.memset(carry.t[:], 0.0)
        for g in range(4):
            h.memset(pcar[g].t[:], 0.0)
        for j in range(NPAIR):
            h.memset(Hst[j].t[:], 0.0)
        h.memset(Hz[0].t, 0.0)
        h.memset(Hz[1].t, 0.0)
        ins = None
        for t_ in range(16):
            ins = h.memset(invc.t[:, t_:t_ + 1], 1.0 / (t_ + 1))
        return ins
    wr = omu_t.bufs + dvec.bufs + carry.bufs + invc.bufs + Hz[0].bufs + Hz[1].bufs
    for g in range(4):
        wr = wr + pcar[g].bufs
    for j in range(NPAIR):
        wr = wr + Hst[j].bufs
    P.op(DVE, setup_v, reads=mu_t.bufs + vec_t.bufs, writes=wr)

    def ld_cast(dst_tl, dst_ap, src_ap, rows, width):
        P.op(SP, lambda h: h.dma_start(out=stage32.t[0:rows, 0:width], in_=src_ap), writes=stage32.bufs, dma_key="par")
        P.op(DVE, lambda h: h.tensor_copy(dst_ap, stage32.t[0:rows, 0:width]), reads=stage32.bufs, writes=dst_tl.bufs)
    ld_cast(poolw_bf, arena[:, 34 * 512:35 * 512], poolw[:, :], 128, 512)
    ld_cast(wa2_bf, wa2_bf.t[:, :], wa2[:, :], 128, RW)
    ld_cast(g2a_bf, g2a_bf.t[:, :], g2a[:, :], 128, RW)
    ld_cast(g2b_bf, g2b_bf.t[0:96, :], g2b[:, :], 96, RW)

    slot_i = {"w": 0, "d": 0}
    tab = {"i": 0}

    def win_load(chunks):
        k = slot_i["w"] % 2
        slot = WGU4[k]
        slot_i["w"] += 1
        for i, c in enumerate(chunks):
            wdt = min(128, INW - c * 128)
            dst_ap = slot.t[:, i, :, 0:wdt]
            src_ap = winb[:, c * 128:c * 128 + wdt].rearrange("(k p) c -> p k c", p=128)
            P.op(SP, (lambda h, dst_ap=dst_ap, src_ap=src_ap: h.dma_start(out=dst_ap, in_=src_ap)),
                 reads=wbuf["win"], writes=slot.bufs, dma_key=(f"wg{k}" if i % 2 == 0 else f"wu{k}"))
        return slot

    def proj(slot, i, rows):
        ps = nextbank()

        def mm(h):
            ins = None
            for kc in range(NKC):
                ins = h.matmul(ps.t[0:rows, :], lhsT=slot.t[:, i, kc, 0:rows], rhs=hT.t[:, kc, :], start=(kc == 0), stop=(kc == NKC - 1))
            return ins
        P.op(PE, mm, reads=slot.bufs + hT.bufs, writes=ps.bufs)
        return ps

    def tshift(ps, c, out, rows=128):
        a_ = t32[2 * (tab["i"] % 2)]
        b_ = t32[2 * (tab["i"] % 2) + 1]
        tab["i"] += 1
        act(a_.t[0:rows, :], ps.t[0:rows, :], AF.Identity, ps.bufs + omu_t.bufs, a_.bufs, scale=omu_t.t[0:rows, c:c + 1])
        act(b_.t[0:rows, :], ps.t[0:rows, :], AF.Identity, ps.bufs + mu_t.bufs, b_.bufs, scale=mu_t.t[0:rows, c:c + 1])

        def f(h):
            h.tensor_tensor(out=out.t[0:rows, 1:TT], in0=a_.t[0:rows, 1:TT], in1=b_.t[0:rows, 0:TT - 1], op=ALU.add)
            return h.tensor_tensor(out=out.t[0:rows, 0:1], in0=a_.t[0:rows, 0:1], in1=carry.t[0:rows, c:c + 1], op=ALU.add)
        pool(f, a_.bufs + b_.bufs + carry.bufs, out.bufs)
        pool(lambda h: h.tensor_copy(carry.t[0:rows, c:c + 1], b_.t[0:rows, TT - 1:TT]), b_.bufs, carry.bufs)

    def sigmoid_to(out_ap, out_bufs, in_t, rows):
        e1, l1 = F[I_E1], F[I_L1]
        act(e1.t[0:rows, :], in_t.t[0:rows, :], AF.Exp, in_t.bufs, e1.bufs, scale=-1.0)
        act(l1.t[0:rows, :], e1.t[0:rows, :], AF.Ln, e1.bufs, l1.bufs, bias=cst.t[0:rows, 0:1], scale=1.0)
        act(out_ap, l1.t[0:rows, :], AF.Exp, l1.bufs, out_bufs, scale=-1.0)

    def v3(t_):
        return t_.t.rearrange("p (c i) -> p c i", c=4)

    def lora_inputs():
        sx40, sx41, sx42 = F[I_RS], F[I_KS], F[I_VS]
        slot = win_load([40, 41, 42])
        ps = proj(slot, 0, 128)
        tshift(ps, 40, sx40)
        ps = proj(slot, 1, 128)
        tshift(ps, 41, sx41)
        ps = proj(slot, 2, 96)
        tshift(ps, 42, sx42, rows=96)
        e1, l1 = F[I_E1], F[I_L1]
        act(e1.t[0:64, :], sx40.t[0:64, :], AF.Exp, sx40.bufs, e1.bufs, scale=-2.0)
        act(l1.t[0:64, :], e1.t[0:64, :], AF.Ln, e1.bufs, l1.bufs, bias=cst.t[0:64, 0:1], scale=1.0)
        act(e1.t[0:64, :], l1.t[0:64, :], AF.Exp, l1.bufs, e1.bufs, scale=-1.0)
        dve(lambda h: h.tensor_scalar(out=lora_in.t[0:64, :], in0=e1.t[0:64, :], scalar1=2.0, scalar2=-1.0, op0=ALU.mult, op1=ALU.add),
            e1.bufs, lora_in.bufs)
        act(lora_in.t[64:128, :], sx40.t[64:128, :], AF.Copy, sx40.bufs, lora_in.bufs)
        sigmoid_to(sg1.t[:, :], sg1.bufs, sx41, 128)
        sigmoid_to(sg2.t[0:96, :], sg2.bufs, sx42, 96)

    def pool_group(ti, slot, g):
        win = 2 << g
        L = g + 1
        ps = proj(slot, g, 128)
        pool(lambda h: h.tensor_copy(Ee.t[:, 0:16], pcar[g].t[:]), pcar[g].bufs, Ee.bufs)
        act(Ee.t[:, 16:16 + TT], ps.t[:], AF.Copy, ps.bufs, Ee.bufs)
        pool(lambda h: h.tensor_copy(pcar[g].t[:], Ee.t[:, TT:TT + 16]), Ee.bufs, pcar[g].bufs)
        cur, cur_lo = Ee, 0
        for k in range(1, L + 1):
            lo = 16 - win + (1 << k)
            n = 16 + TT - lo
            nxt = wtmp[k % 2]
            a0 = lo - cur_lo
            b0 = lo - (1 << (k - 1)) - cur_lo
            assert b0 >= 0

            def f(h, nxt=nxt, cur=cur, a0=a0, b0=b0, n=n):
                return h.tensor_tensor(out=nxt.t[:, 0:n], in0=cur.t[:, a0:a0 + n], in1=cur.t[:, b0:b0 + n], op=ALU.add)
            pool(f, cur.bufs, nxt.bufs)
            cur, cur_lo = nxt, lo
        assert cur_lo == 16
        wl = cur
        dve(lambda h: h.scalar_tensor_tensor(out=pooled.t[:, :], in0=wl.t[:, 0:TT], scalar=1.0 / win, in1=Ee.t[:, 16:16 + TT],
                                             op0=ALU.mult, op1=ALU.subtract), wl.bufs + Ee.bufs, pooled.bufs)
        if ti == 0:
            tmpc = t32[0]
            nfix = win - 1
            dve(lambda h: h.tensor_tensor(out=tmpc.t[:, 0:nfix], in0=wl.t[:, 0:nfix], in1=invc.t[:, 0:nfix], op=ALU.mult),
                wl.bufs + invc.bufs, tmpc.bufs)
            dve(lambda h: h.tensor_tensor(out=pooled.t[:, 0:nfix], in0=tmpc.t[:, 0:nfix], in1=Ee.t[:, 16:16 + nfix], op=ALU.subtract),
                tmpc.bufs + Ee.bufs, pooled.bufs)
        psm = nextbank()
        P.op(PE, lambda h: h.matmul(psm.t[:], lhsT=poolw_bf.t[:, g, :], rhs=pooled.t[:, :], start=True, stop=True),
             reads=poolw_bf.bufs + pooled.bufs, writes=psm.bufs)
        act(catT[g].t, psm.t[:], AF.Identity, psm.bufs + pscale_t.bufs, catT[g].bufs, scale=pscale_t.t[:, g:g + 1])

    chn = sb("chn", [128, 8 * 256 + 8 * 128 * 2 + 4 * 384], BF16)
    G1A = arena[:, 26 * 512:30 * 512].rearrange("p (q n) -> p q n", q=8)
    PA = arena[:, 30 * 512:32 * 512].rearrange("p (q n) -> p q n", q=8)
    o = 0
    G2A = chn[:, o:o + 8 * 256].rearrange("p (q n) -> p q n", q=8); o += 8 * 256
    PTA = chn[:, o:o + 8 * 128].rearrange("p (q n) -> p q n", q=8); o += 8 * 128
    TTA = chn[:, o:o + 8 * 128].rearrange("p (q n) -> p q n", q=8); o += 8 * 128
    TOK = chn[:, o:o + 4 * 384].rearrange("p (c n) -> p c n", c=4); o += 4 * 384
    G1B = [slabB[26 + 2 * h] for h in range(2)]
    G1Bx = [[slabB[26 + 2 * h], slabB[27 + 2 * h]] for h in range(2)]
    G2B = [Buf(f"G2h{h}") for h in range(2)]
    PB = [slabB[30 + h] for h in range(2)]
    PTB = [Buf(f"PTh{h}") for h in range(2)]
    TTB = [Buf(f"TTh{h}") for h in range(2)]
    TOKB = [Buf(f"tok{i}") for i in range(2)]
    MU2x2 = Tl(arena[:, 32 * 512:33 * 512].rearrange("p (r n) -> p r n", r=2), [slabB[32]])
    MLx4 = tile("MLx4", [128, 4, 128], BF16)
    IDx4 = tile("IDx4", [128, 4, 128], BF16)

    def mkc(h):
        ins = None
        for r in range(2):
            pass
        for r in range(4):
            h.tensor_copy(MLx4.t[:, r, :], ML.t[:])
            ins = h.tensor_copy(IDx4.t[:, r, :], ident.t[:])
        return ins
    P.op(DVE, mkc, reads=MU2.bufs + ML.bufs + ident.bufs, writes=MLx4.bufs + IDx4.bufs)

    def chains(j):
        for half in range(2):
            def trp(h, half=half):
                ins = None
                for cc in range(2):
                    c = half * 2 + cc
                    cs = slice(c * CH, (c + 1) * CH)
                    h.transpose(PST.t[:, cc * 384:cc * 384 + 128], Bt.t[:, cs], ident.t[:])
                    h.transpose(PST.t[:, cc * 384 + 128:cc * 384 + 256], Kt.t[:, cs], ident.t[:])
                    ins = h.transpose(PST.t[:, cc * 384 + 256:cc * 384 + 384], Vt.t[:, cs], ident.t[:])
                return ins
            P.op(PE, trp, reads=Bt.bufs + Kt.bufs + Vt.bufs + ident.bufs, writes=PST.bufs)
            act(TOK[:, half * 2:half * 2 + 2, :], PST.t[:, 0:768].rearrange("p (c n) -> p c n", c=2), AF.Copy, PST.bufs, [TOKB[half]])
        for hh in range(2):
            ph = slice(64 * hh, 64 * hh + 64)
            for cp in range(2):
                b1, b2 = nextbank(), nextbank()

                def g12(h, b1=b1, b2=b2, ph=ph, cp=cp):
                    ins = None
                    for cc in range(2):
                        c = cp * 2 + cc
                        cs = slice(c * CH, (c + 1) * CH)
                        arf = AR.t[ph, c, :, :].rearrange("p t i -> p (t i)")
                        h.matmul(b1.t[:, cc * 256:(cc + 1) * 256], lhsT=Kt.t[ph, cs], rhs=arf, start=True, stop=True)
                        ins = h.matmul(b2.t[:, cc * 256:(cc + 1) * 256], lhsT=Bt.t[ph, cs], rhs=arf, start=True, stop=True)
                    return ins
                P.op(PE, g12, reads=Kt.bufs + Bt.bufs + AR.bufs, writes=b1.bufs + b2.bufs)
                q0 = hh * 4 + cp * 2
                dve(lambda h, b1=b1, q0=q0: h.tensor_tensor(out=G1A[:, q0:q0 + 2, :], in0=b1.t[:].rearrange("p (c n) -> p c n", c=2),
                                                           in1=MU2x2.t, op=ALU.mult), b1.bufs + MU2x2.bufs, G1Bx[hh])
                dve(lambda h, b2=b2, q0=q0: h.tensor_tensor(out=G2A[:, q0:q0 + 2, :], in0=b2.t[:].rearrange("p (c n) -> p c n", c=2),
                                                           in1=MU2x2.t, op=ALU.mult), b2.bufs + MU2x2.bufs, [G2B[hh]])
            b3 = nextbank()

            def g3(h, b3=b3, ph=ph):
                ins = None
                for c in range(4):
                    cs = slice(c * CH, (c + 1) * CH)
                    ins = h.matmul(b3.t[:, c * 128:(c + 1) * 128], lhsT=AR.t[ph, c, 0, :], rhs=Bt.t[ph, cs], start=True, stop=True)
                return ins
            P.op(PE, g3, reads=Bt.bufs + AR.bufs, writes=b3.bufs)
            dve(lambda h, b3=b3, hh=hh: h.tensor_tensor(out=PTA[:, hh * 4:hh * 4 + 4, :], in0=b3.t[:].rearrange("p (c n) -> p c n", c=4),
                                                       in1=MLx4.t[:], op=ALU.mult), b3.bufs + MLx4.bufs, [PTB[hh]])
            pool(lambda h, hh=hh: h.tensor_tensor(out=TTA[:, hh * 4:hh * 4 + 4, :], in0=G2A[:, hh * 4:hh * 4 + 4, 0:128], in1=IDx4.t[:], op=ALU.add),
                 [G2B[hh]] + IDx4.bufs, [TTB[hh]])
        for m in range(1, 7):
            pcs = []
            for hh in range(2):
                qs = range(hh * 4, hh * 4 + 4)
                Pin = (lambda q: G2A[:, q, 0:128]) if m == 1 else (lambda q: PA[:, q, :])
                Prd = [G2B[hh]] if m == 1 else [PB[hh]]
                if m < 6:
                    ba = nextbank()

                    def sqa(h, ba=ba, qs=qs, Pin=Pin):
                        ins = None
                        for i, q in enumerate(qs):
                            ins = h.matmul(ba.t[:, i * 128:(i + 1) * 128], lhsT=PTA[:, q, :], rhs=Pin(q), start=True, stop=True)
                        return ins
                    P.op(PE, sqa, reads=Prd + [PTB[hh]], writes=ba.bufs)
                bb = nextbank()

                def sqb_(h, bb=bb, qs=qs, Pin=Pin):
                    ins = None
                    for i, q in enumerate(qs):
                        ins = h.matmul(bb.t[:, i * 128:(i + 1) * 128], lhsT=Pin(q), rhs=PTA[:, q, :], start=True, stop=True)
                    return ins
                P.op(PE, sqb_, reads=Prd + [PTB[hh]], writes=bb.bufs)
                if hh == 0:
                    if m < 6:
                        act(PA[:, hh * 4:hh * 4 + 4, :], ba.t[:].rearrange("p (c n) -> p c n", c=4), AF.Copy, ba.bufs, [PB[hh]])
                    dve(lambda h, bb=bb, hh=hh: h.tensor_copy(PTA[:, hh * 4:hh * 4 + 4, :], bb.t[:].rearrange("p (c n) -> p c n", c=4)),
                        bb.bufs, [PTB[hh]])
                else:
                    act(PTA[:, hh * 4:hh * 4 + 4, :], bb.t[:].rearrange("p (c n) -> p c n", c=4), AF.Copy, bb.bufs, [PTB[hh]])
                    if m < 6:
                        act(PA[:, hh * 4:hh * 4 + 4, :], ba.t[:].rearrange("p (c n) -> p c n", c=4), AF.Copy, ba.bufs, [PB[hh]])
            for hh in range(2):
                qs = range(hh * 4, hh * 4 + 4)
                bc = nextbank()

                def ttu(h, bc=bc, qs=qs):
                    ins = None
                    for i, q in enumerate(qs):
                        ins = h.matmul(bc.t[:, i * 128:(i + 1) * 128], lhsT=PTA[:, q, :], rhs=TTA[:, q, :], start=True, stop=True)
                    return ins
                P.op(PE, ttu, reads=[PTB[hh], TTB[hh]], writes=bc.bufs)
                dve(lambda h, bc=bc, hh=hh: h.tensor_tensor(out=TTA[:, hh * 4:hh * 4 + 4, :], in0=bc.t[:].rearrange("p (c n) -> p c n", c=4),
                                                           in1=TTA[:, hh * 4:hh * 4 + 4, :], op=ALU.add), bc.bufs + [TTB[hh]], [TTB[hh]])

    def chunk(j, c, ysb):
        Hj = Hst[j]
        cs = slice(c * CH, (c + 1) * CH)
        tkb = [TOKB[c // 2]]
        Btok = lambda hh: TOK[:, c, hh * 64:(hh + 1) * 64]
        Ktok = lambda hh: TOK[:, c, 128 + hh * 64:128 + (hh + 1) * 64]
        Vtok = lambda hh: TOK[:, c, 256 + hh * 64:256 + (hh + 1) * 64]
        q_ = lambda hh: hh * 4 + c
        dve(lambda h: h.tensor_scalar(out=H0p.t[:], in0=Hj.t[:], scalar1=wc.t[:, c:c + 1], scalar2=None, op0=ALU.mult),
            Hj.bufs + wc.bufs, H0p.bufs)
        dve(lambda h: h.tensor_scalar(out=Hz[0].t[0:64, :], in0=Hj.t[0:64, :], scalar1=wc.t[0:64, c:c + 1], scalar2=None, op0=ALU.mult),
            Hj.bufs + wc.bufs, Hz[0].bufs)
        dve(lambda h: h.tensor_scalar(out=Hz[1].t[64:128, :], in0=Hj.t[64:128, :], scalar1=wc.t[64:128, c:c + 1], scalar2=None, op0=ALU.mult),
            Hj.bufs + wc.bufs, Hz[1].bufs)
        px = nextbank()

        def mx(h):
            ins = None
            for hh in range(2):
                h.matmul(px.t[:, hh * 64:(hh + 1) * 64], lhsT=AR.t[:, c, 0, :], rhs=Hz[hh].t, start=True, stop=False)
                ins = h.matmul(px.t[:, hh * 64:(hh + 1) * 64], lhsT=G1A[:, q_(hh), 0:128], rhs=Vtok(hh), start=False, stop=True)
            return ins
        P.op(PE, mx, reads=AR.bufs + Hz[0].bufs + Hz[1].bufs + G1Bx[0] + G1Bx[1] + tkb, writes=px.bufs)
        act(Xbf.t, px.t[:, 0:128], AF.Copy, px.bufs, Xbf.bufs)
        pu_ = nextbank()

        def mu_(h):
            ins = None
            for hh in range(2):
                ins = h.matmul(pu_.t[:, hh * 64:(hh + 1) * 64], lhsT=TTA[:, q_(hh), :], rhs=Xbf.t[:, hh * 64:(hh + 1) * 64], start=True, stop=True)
            return ins
        P.op(PE, mu_, reads=TTB + Xbf.bufs, writes=pu_.bufs)
        act(Ubf.t, pu_.t[:, 0:128], AF.Copy, pu_.bufs, Ubf.bufs)
        py, psn = nextbank(), nextbank()

        def my(h):
            ins = None
            for hh in range(2):
                po = slice(64 * hh, 64 * hh + 64)
                h.matmul(py.t[po, 0:128], lhsT=Hz[hh].t, rhs=AR.t[:, c, 1, :], start=True, stop=False)
                h.matmul(py.t[po, 0:128], lhsT=Ubf.t[:, hh * 64:(hh + 1) * 64], rhs=G2A[:, q_(hh), 128:256], start=False, stop=False)
                h.matmul(py.t[po, 0:128], lhsT=Vtok(hh), rhs=G1A[:, q_(hh), 128:256], start=False, stop=True)
            for hh in range(2):
                po = slice(64 * hh, 64 * hh + 64)
                h.matmul(psn.t[po, 0:64], lhsT=Btok(hh), rhs=Ubf.t[:, hh * 64:(hh + 1) * 64], start=True, stop=False)
                ins = h.matmul(psn.t[po, 0:64], lhsT=Ktok(hh), rhs=Vtok(hh), start=False, stop=True)
            return ins
        P.op(PE, my, reads=Hz[0].bufs + Hz[1].bufs + AR.bufs + Ubf.bufs + G1Bx[0] + G1Bx[1] + G2B + tkb, writes=py.bufs + psn.bufs)
        dve(lambda h: h.tensor_tensor(out=Hj.t[:], in0=psn.t[:, 0:64], in1=H0p.t[:], op=ALU.add), psn.bufs + H0p.bufs, Hj.bufs)
        act(ysb.t[:, cs], py.t[:, 0:128], AF.Copy, py.bufs, ysb.bufs)

    def pair(j):
        slot = win_load([4 + j, 16 + j, 28 + j])
        rs, ks, vs = F[I_RS], F[I_KS], F[I_VS]
        e1, l1, nl, a_t, g_t = F[I_E1], F[I_L1], F[I_NL], F[I_A], F[I_G]
        kk, k2, bp, bonus = F[I_KK], F[I_K2], F[I_BP], F[I_BONUS]
        cwn, cwx, eA, eR, eK = F[I_CWN], F[I_CWX], F[I_EA], F[I_ER], F[I_EK]
        pzw, pza, pg_ = nextbank(), nextbank(), nextbank()
        P.op(PE, lambda h: h.matmul(pzw.t[:], lhsT=wa2_bf.t[0:64, j * 128:(j + 1) * 128], rhs=lora_in.t[0:64, :], start=True, stop=True),
             reads=wa2_bf.bufs + lora_in.bufs, writes=pzw.bufs)
        P.op(PE, lambda h: h.matmul(pza.t[:], lhsT=wa2_bf.t[64:128, j * 128:(j + 1) * 128], rhs=lora_in.t[64:128, :], start=True, stop=True),
             reads=wa2_bf.bufs + lora_in.bufs, writes=pza.bufs)

        def mg(h):
            h.matmul(pg_.t[:], lhsT=g2a_bf.t[:, j * 128:(j + 1) * 128], rhs=sg1.t[:, :], start=True, stop=False)
            return h.matmul(pg_.t[:], lhsT=g2b_bf.t[0:96, j * 128:(j + 1) * 128], rhs=sg2.t[0:96, :], start=False, stop=True)
        P.op(PE, mg, reads=g2a_bf.bufs + g2b_bf.bufs + sg1.bufs + sg2.bufs, writes=pg_.bufs)
        act(e1.t[:, :], pzw.t[:], AF.Exp, pzw.bufs + dvec.bufs, e1.bufs, scale=-1.0, bias=dvec.t[:, j:j + 1])
        act(l1.t[:, :], e1.t[:, :], AF.Ln, e1.bufs, l1.bufs, bias=cst.t[:, 0:1], scale=1.0)
        act(nl.t[:, :], l1.t[:, :], AF.Exp, l1.bufs, nl.bufs, scale=-1.0, bias=cst.t[:, 1:2])
        dve(lambda h: h.tensor_tensor_scan(out=cwn.t[:, :], data0=scanmask.t[:], data1=nl.t[:, :], initial=0.0, op0=ALU.mult, op1=ALU.add),
            scanmask.bufs + nl.bufs, cwn.bufs)
        dve(lambda h: h.tensor_tensor(out=cwx.t[:, :], in0=cwn.t[:, :], in1=nl.t[:, :], op=ALU.subtract), cwn.bufs + nl.bufs, cwx.bufs)
        cend = v3(cwn)[:, :, CH - 1]
        dve(lambda h: h.tensor_scalar(out=ncw.t[:], in0=cend, scalar1=-1.0, scalar2=None, op0=ALU.mult), cwn.bufs, ncw.bufs)
        act(e1.t[:, :], pza.t[:], AF.Exp, pza.bufs + dvec.bufs, e1.bufs, scale=-1.0, bias=dvec.t[:, NPAIR + j:NPAIR + j + 1])
        act(l1.t[:, :], e1.t[:, :], AF.Ln, e1.bufs, l1.bufs, bias=cst.t[:, 0:1], scale=1.0)
        act(a_t.t[:, :], l1.t[:, :], AF.Exp, l1.bufs, a_t.bufs, scale=-1.0)
        act(wc.t[:], cend, AF.Exp, cwn.bufs, wc.bufs, scale=-1.0)

        def exps(h):
            ins = None
            for c in range(4):
                cs = slice(c * CH, (c + 1) * CH)
                ce = cwn.t[:, c * CH + CH - 1:c * CH + CH]
                h.activation(out=eR.t[:, cs], in_=cwn.t[:, cs], func=AF.Exp, scale=-1.0, bias=ce)
                h.activation(out=eA.t[:, cs], in_=cwx.t[:, cs], func=AF.Exp, scale=-1.0, bias=ce)
                ins = h.activation(out=eK.t[:, cs], in_=cwn.t[:, cs], func=AF.Exp, scale=1.0, bias=ncw.t[:, c:c + 1])
            return ins
        P.op(ACT, exps, reads=cwn.bufs + cwx.bufs + ncw.bufs, writes=eR.bufs + eA.bufs + eK.bufs)
        ps = proj(slot, 0, 128)
        tshift(ps, 4 + j, rs)
        ps = proj(slot, 1, 128)
        tshift(ps, 16 + j, ks)
        ps = proj(slot, 2, 128)
        tshift(ps, 28 + j, vs)
        act(g_t.t[:, :], pg_.t[:], AF.Copy, pg_.bufs, g_t.bufs)
        pool(lambda h: h.tensor_copy(Vt.t[:, :], vs.t[:, :]), vs.bufs, Vt.bufs)
        dve(lambda h: h.tensor_tensor(out=AR.t[:, :, 1, :], in0=v3(rs), in1=v3(eR), op=ALU.mult), rs.bufs + eR.bufs, AR.bufs)
        dve(lambda h: h.tensor_scalar(out=kk.t[:, :], in0=ks.t[:, :], scalar1=vcol(V_KK, j), scalar2=None, op0=ALU.mult),
            ks.bufs + vec_t.bufs, kk.bufs)
        act(sqb.t[:, :], kk.t[:, :], AF.Square, kk.bufs, sqb.bufs)
        pss = nextbank()
        P.op(PE, lambda h: h.matmul(pss.t[:], lhsT=blk1.t[:], rhs=sqb.t[:, :], start=True, stop=True),
             reads=blk1.bufs + sqb.bufs, writes=pss.bufs)
        dve(lambda h: h.tensor_scalar(out=k2.t[:, :], in0=a_t.t[:, :], scalar1=vcol(V_KA, j), scalar2=dvec.t[:, 2 * NPAIR + j:2 * NPAIR + j + 1],
                                      op0=ALU.mult, op1=ALU.add), a_t.bufs + vec_t.bufs + dvec.bufs, k2.bufs)
        dve(lambda h: h.tensor_tensor(out=k2.t[:, :], in0=k2.t[:, :], in1=ks.t[:, :], op=ALU.mult), k2.bufs + ks.bufs, k2.bufs)
        dve(lambda h: h.tensor_tensor(out=Kt.t[:, :], in0=k2.t[:, :], in1=eK.t[:, :], op=ALU.mult), k2.bufs + eK.bufs, Kt.bufs)
        act(l1.t[:, :], pss.t[:], AF.Ln, pss.bufs, l1.bufs, bias=cst.t[:, 2:3], scale=1.0)
        act(e1.t[:, :], l1.t[:, :], AF.Exp, l1.bufs, e1.bufs, scale=-0.5)
        dve(lambda h: h.tensor_tensor(out=kk.t[:, :], in0=kk.t[:, :], in1=e1.t[:, :], op=ALU.mult), kk.bufs + e1.bufs, kk.bufs)
        dve(lambda h: h.scalar_tensor_tensor(out=AR.t[:, :, 0, :], in0=v3(kk), scalar=-1.0, in1=v3(eA), op0=ALU.mult, op1=ALU.mult),
            kk.bufs + eA.bufs, AR.bufs)
        dve(lambda h: h.tensor_tensor(out=bp.t[:, :], in0=kk.t[:, :], in1=a_t.t[:, :], op=ALU.mult), kk.bufs + a_t.bufs, bp.bufs)
        dve(lambda h: h.tensor_tensor(out=Bt.t[:, :], in0=bp.t[:, :], in1=eK.t[:, :], op=ALU.mult), bp.bufs + eK.bufs, Bt.bufs)
        chains(j)
        dve(lambda h: h.scalar_tensor_tensor(out=sqb.t[:, :], in0=rs.t[:, :], scalar=vcol(V_RK, j), in1=k2.t[:, :], op0=ALU.mult, op1=ALU.mult),
            rs.bufs + k2.bufs + vec_t.bufs, sqb.bufs)
        psr = nextbank()
        P.op(PE, lambda h: h.matmul(psr.t[:], lhsT=blk1.t[:], rhs=sqb.t[:, :], start=True, stop=True),
             reads=blk1.bufs + sqb.bufs, writes=psr.bufs)
        dve(lambda h: h.tensor_tensor(out=bonus.t[:, :], in0=psr.t[:], in1=vs.t[:, :], op=ALU.mult), psr.bufs + vs.bufs, bonus.bufs)

        ysb = F[I_Y]
        for c in range(4):
            chunk(j, c, ysb)

        yc = F[I_YC]
        pm = nextbank()
        P.op(PE, lambda h: h.matmul(pm.t[:], lhsT=blkm.t[:], rhs=ysb.t[:, :], start=True, stop=True),
             reads=blkm.bufs + ysb.bufs, writes=pm.bufs)
        dve(lambda h: h.tensor_tensor(out=yc.t[:, :], in0=ysb.t[:, :], in1=pm.t[:], op=ALU.subtract), ysb.bufs + pm.bufs, yc.bufs)
        act(ysb.t[:, :], yc.t[:, :], AF.Square, yc.bufs, ysb.bufs)
        pv = nextbank()
        P.op(PE, lambda h: h.matmul(pv.t[:], lhsT=blkm.t[:], rhs=ysb.t[:, :], start=True, stop=True),
             reads=blkm.bufs + ysb.bufs, writes=pv.bufs)
        act(l1.t[:, :], pv.t[:], AF.Ln, pv.bufs, l1.bufs, bias=cst.t[:, 3:4], scale=1.0)
        act(e1.t[:, :], l1.t[:, :], AF.Exp, l1.bufs, e1.bufs, scale=-0.5)
        dve(lambda h: h.tensor_tensor(out=yc.t[:, :], in0=yc.t[:, :], in1=e1.t[:, :], op=ALU.mult), yc.bufs + e1.bufs, yc.bufs)
        act(ysb.t[:, :], yc.t[:, :], AF.Identity, yc.bufs + vec_t.bufs, ysb.bufs, scale=vcol(V_LW, j), bias=vcol(V_LB, j))
        dve(lambda h: h.tensor_tensor(out=ysb.t[:, :], in0=ysb.t[:, :], in1=bonus.t[:, :], op=ALU.add), ysb.bufs + bonus.bufs, ysb.bufs)
        dve(lambda h: h.tensor_tensor(out=catT[4 + j].t, in0=ysb.t[:, :], in1=g_t.t[:, :], op=ALU.mult), ysb.bufs + g_t.bufs, catT[4 + j].bufs)

    def out_group(g, ps2):
        pacc = [nextbank() for _ in range(4)]
        wrb = []
        for d_ in range(4):
            wrb = wrb + pacc[d_].bufs

        def half_(half):
            k = slot_i["d"] % 2
            slot = WD[k]
            slot_i["d"] += 1
            r0 = half * 8 * 128
            P.op(SP, lambda h: h.dma_start(
                out=slot.t[:, 0:8, :], in_=woutb[r0:r0 + 8 * 128, g * 512:(g + 1) * 512].rearrange("(f p) c -> p f c", p=128)),
                reads=wbuf["wout"], writes=slot.bufs, dma_key=f"wd{k}")

            def mm(h):
                ins = None
                for fl in range(8):
                    cc = half * 8 + fl
                    for d_ in range(4):
                        ins = h.matmul(pacc[d_].t[:], lhsT=slot.t[:, fl, d_ * 128:(d_ + 1) * 128], rhs=catT[cc].t,
                                       start=(cc == 0), stop=(cc == 15))
                return ins
            rd = list(slot.bufs)
            for fl in range(8):
                rd = rd + catT[half * 8 + fl].bufs
            P.op(PE, mm, reads=rd, writes=wrb)
        half_(0)
        half_(1)
        for d_ in range(4):
            evac_f(pacc[d_], g * 4 + d_, ps2)

    src, dst = "x1T", ("x2T" if stage > 2 else "outT")
    final = stage == 2
    def mk_mu(h):
        h.tensor_copy(MU2x2.t[:, 0, :], MU2.t[:])
        return h.tensor_copy(MU2x2.t[:, 1, :], MU2.t[:])
    P.op(DVE, mk_mu, reads=MU2.bufs, writes=MU2x2.bufs)
    for ti in range(NT):
        prenorm_p1(src, ti)
        prenorm_p2(src, ti, 1)
        lora_inputs()
        slot = win_load([0, 1, 2, 3])
        for g in range(4):
            pool_group(ti, slot, g)
        for j in range(NPAIR):
            pair(j)
        ps2 = PSB[5]
        for g in range(4):
            out_group(g, ps2)
        postnorm(src, dst, ti, 1, ps2, final)


def prep_shared(inp):
    f = lambda a: np.ascontiguousarray(np.asarray(a, dtype=np.float32))
    sh = {}
    sh["w_ada"] = f(inp["w_ada"][0])
    sh["b_ada"] = f(np.asarray(inp["b_ada"][0]).reshape(144, 128).T)
    sh["npre"] = f(np.asarray(inp["norm_pre"][0]).reshape(3, 16, 128).transpose(2, 0, 1).reshape(128, 48))
    sh["npost"] = f(np.asarray(inp["norm_post"][0]).reshape(3, 16, 128).transpose(2, 0, 1).reshape(128, 48))
    sh["wg1"] = f(inp["ffn1_w_gate"][0]); sh["wu1"] = f(inp["ffn1_w_up"][0]); sh["wd1"] = f(inp["ffn1_w_down"][0])
    sh["wg2"] = f(inp["ffn2_w_gate"][0]); sh["wu2"] = f(inp["ffn2_w_up"][0]); sh["wd2"] = f(inp["ffn2_w_down"][0])
    sh["win"] = f(inp["w_in"][0]); sh["wout"] = f(inp["w_out"][0])
    mu = np.zeros(43 * 128, np.float32)
    mu[512:512 + 4960] = np.asarray(inp["mu_shift"][0])
    sh["muT"] = f(mu.reshape(43, 128).T)
    sh["poolw"] = f(np.asarray(inp["pool_w"][0]).transpose(1, 0, 2).reshape(128, 512))
    sh["pscale"] = f(np.asarray(inp["pool_scale"][0]).reshape(4, 128).T)
    vs = [inp["w0"][0], inp["a0"][0], inp["k_k"][0], inp["k_a"][0], np.asarray(inp["r_k"][0]).reshape(-1), inp["lnx_w"][0], inp["lnx_b"][0]]
    sh["vecs"] = f(np.concatenate([np.asarray(v).reshape(NPAIR, 128).T for v in vs], axis=1))
    sh["wa2"] = f(np.concatenate([np.asarray(inp["w2"][0]), np.asarray(inp["a2"][0])], axis=0))
    g2 = np.asarray(inp["g2"][0])
    sh["g2a"] = f(g2[0:128]); sh["g2b"] = f(g2[128:224])
    return sh


def prep_core(inp, b):
    return {"xT": np.ascontiguousarray(np.asarray(inp["x"][b], dtype=np.float32).T),
            "cT": np.ascontiguousarray(np.asarray(inp["c"][b], dtype=np.float32).reshape(16, 128).T)}


_NEEDED = {1: ["xT", "cT", "w_ada", "b_ada", "npre", "npost", "wg1", "wu1", "wd1"]}


def kernel(**inputs):
    x = np.asarray(inputs["x"])
    B, T, _ = x.shape
    nc = build_program(T, stage=3)
    sh = prep_shared(inputs)
    in_maps = []
    for b in range(B):
        m = dict(sh)
        m.update(prep_core(inputs, b))
        in_maps.append(m)
    res = run_bass_kernel_spmd(nc, in_maps, core_ids=list(range(B)))
    out = np.stack([np.ascontiguousarray(r["outT"].T) for r in res.results], axis=0)
    return out.astype(np.float32)
```

```python
import numpy as np
from contextlib import ExitStack
import concourse.bass as bass
import concourse.mybir as mybir
from concourse.bass_utils import run_bass_kernel_spmd

F32 = mybir.dt.float32
BF16 = mybir.dt.bfloat16
AF = mybir.ActivationFunctionType
ALU = mybir.AluOpType

PE, ACT, DVE, POOL, SP = "pe", "act", "dve", "pool", "sp"
ENGINES = [PE, ACT, DVE, POOL, SP]
EPOCH = 20000

D = 2048
FF = 5632
NKC = 16
NFC = 44
TT = 512
CH = 128
INW = 5472
RW = 1536
NPAIR = 12
NORM_EPS = 1e-6
LNX_EPS = 1e-5 * 64


class Buf:
    __slots__ = ("name", "last_w", "readers")

    def __init__(self, name):
        self.name = name
        self.last_w = None
        self.readers = []


class Op:
    __slots__ = ("eng", "emit", "deps", "is_dma", "dsem", "need_sig", "sig")

    def __init__(self, eng, emit, is_dma, dsem):
        self.eng = eng
        self.emit = emit
        self.deps = []
        self.is_dma = is_dma
        self.dsem = dsem
        self.need_sig = is_dma
        self.sig = None


class Prog:
    def __init__(self, nc, stack):
        self.nc = nc
        self.stack = stack
        self.ops = {e: [] for e in ENGINES}
        self.dma_sems = {}
        self.key_bufs = {}
        self.final_waits = []

    def new_sem(self, name):
        return self.stack.enter_context(self.nc.semaphore(name))

    def op(self, eng, emit, reads=(), writes=(), dma_key=None):
        is_dma = dma_key is not None
        dsem = None
        if is_dma:
            if dma_key not in self.dma_sems:
                self.dma_sems[dma_key] = [self.new_sem("d_" + dma_key), 0]
                self.key_bufs[dma_key] = Buf("k_" + dma_key)
            dsem = self.dma_sems[dma_key]
            writes = list(writes) + [self.key_bufs[dma_key]]
        o = Op(eng, emit, is_dma, dsem)
        deps = []

        def add(p, kind):
            if p is None or p is o:
                return
            if not p.is_dma and not is_dma and p.eng == eng:
                if eng == PE or kind == "war":
                    return
            deps.append(p)

        for b in reads:
            add(b.last_w, "raw")
        for b in writes:
            add(b.last_w, "waw")
            for r in b.readers:
                add(r, "war")
        for b in reads:
            b.readers.append(o)
        for b in writes:
            b.last_w = o
            b.readers = []
        seen = set()
        for p in deps:
            if id(p) not in seen:
                seen.add(id(p))
                p.need_sig = True
                o.deps.append(p)
        self.ops[eng].append(o)
        return o

    def emit_all(self):
        nc = self.nc
        for e in ENGINES:
            cnt = 0
            sems = []
            for o in self.ops[e]:
                if o.is_dma:
                    o.dsem[1] += 16
                    o.sig = (o.dsem[0], o.dsem[1], 16)
                elif o.need_sig:
                    ep = cnt // EPOCH
                    if ep >= len(sems):
                        sems.append(self.new_sem(f"s_{e}_{ep}"))
                    o.sig = (sems[ep], cnt % EPOCH + 1, 1)
                    cnt += 1
        final_waits = self.final_waits

        def run(e, h):
            waited = {}
            for o in self.ops[e]:
                for p in o.deps:
                    sem, val, _ = p.sig
                    k = id(sem)
                    if waited.get(k, 0) < val:
                        h.wait_ge(sem, val)
                        waited[k] = val
                ins = o.emit(h)
                if o.need_sig:
                    ins.then_inc(o.sig[0], o.sig[2])
            if e == SP:
                for o in final_waits:
                    sem, val, _ = o.sig
                    if waited.get(id(sem), 0) < val:
                        h.wait_ge(sem, val)
                        waited[id(sem)] = val

        with nc.Block() as block:
            @block.tensor
            def _(h):
                run(PE, h)

            @block.scalar
            def _(h):
                run(ACT, h)

            @block.vector
            def _(h):
                run(DVE, h)

            @block.gpsimd
            def _(h):
                run(POOL, h)

            @block.sync
            def _(h):
                run(SP, h)


class Tl:
    __slots__ = ("t", "bufs")

    def __init__(self, t, bufs):
        self.t = t
        self.bufs = bufs


def build_program(T, stage=3, debug=False):
    NT = T // TT
    nc = bass.Bass("TRN2", target_bir_lowering=False)

    def din(name, shape, dt=F32):
        return nc.dram_tensor(name, list(shape), dt, kind="ExternalInput").ap()

    def dscr(name, shape, dt):
        return nc.dram_tensor(name, list(shape), dt, kind="Internal").ap()

    xT = din("xT", [D, T])
    cT = din("cT", [128, NKC])
    w_ada = din("w_ada", [D, 9 * D])
    b_ada = din("b_ada", [128, 144])
    npre = din("npre", [128, 48])
    npost = din("npost", [128, 48])
    wsrc = {}
    for nm, shp in [("wg1", [D, FF]), ("wu1", [D, FF]), ("wd1", [FF, D]), ("win", [D, INW]), ("wout", [D, D]),
                    ("wg2", [D, FF]), ("wu2", [D, FF]), ("wd2", [FF, D])]:
        wsrc[nm] = (din(nm, shp), dscr(nm + "_b", shp, BF16), shp)
    muT = din("muT", [128, 43])
    poolw = din("poolw", [128, 4 * 128])
    pscale = din("pscale", [128, 4])
    vecs = din("vecs", [128, 7 * NPAIR])
    wa2 = din("wa2", [128, RW])
    g2a = din("g2a", [128, RW])
    g2b = din("g2b", [96, RW])
    outT = nc.dram_tensor("outT", [D, T], F32, kind="ExternalOutput").ap()
    x1T = dscr("x1T", [D, T], F32)
    x2T = dscr("x2T", [D, T], F32)

    with ExitStack() as st:
        P = Prog(nc, st)

        def sb(name, shape, dt=F32):
            return st.enter_context(nc.sbuf_tensor(name, list(shape), dt))

        def tile(name, shape, dt=F32):
            return Tl(sb(name, shape, dt), [Buf(name)])

        PSB = []
        for b in range(7):
            PSB.append(Tl(st.enter_context(nc.psum_tensor(f"ps{b}", [128, 512], F32)), [Buf(f"ps{b}")]))
        PST = Tl(st.enter_context(nc.psum_tensor("pst", [128, 1024], BF16)), [Buf("pst")])
        rr = {"i": 0}
        WORK = [0, 1, 2, 3, 4, 6]

        def nextbank():
            b = WORK[rr["i"] % len(WORK)]
            rr["i"] += 1
            return PSB[b]

        NSLAB = 44
        arena = sb("arena", [128, NSLAB * 512], BF16)
        slabB = [Buf(f"slab{i}") for i in range(NSLAB)]
        arena_f32 = None

        def slab_bf(i, n=1):
            return Tl(arena[:, i * 512:(i + n) * 512], slabB[i:i + n])


        hT = tile("hT", [128, NKC, TT], BF16)
        fbuf = Tl(sb("fbuf", [128, NKC, TT], F32), [Buf(f"fbuf{i}") for i in range(NKC)])
        NXR = 3
        xr = [tile(f"xr{i}", [128, 2, TT], F32) for i in range(NXR)]
        xr_i = {"i": 0}
        sq = [tile(f"sq{i}", [128, TT], BF16) for i in range(2)]
        sq_i = {"i": 0}
        rstd = tile("rstd", [128, TT], F32)
        rstd2 = tile("rstd2", [128, TT], F32)
        sgt = [tile(f"sgt{i}", [128, TT], F32) for i in range(2)]
        sgt_i = {"i": 0}
        t32 = [tile(f"t32_{i}", [128, TT], F32) for i in range(4)]
        t32_i = {"i": 0}

        def next_t32():
            t = t32[t32_i["i"] % len(t32)]
            t32_i["i"] += 1
            return t

        WGUF = [tile(f"wguslot{i}", [128, 2 * NKC * 256], BF16) for i in range(2)]
        WGU = [Tl(w.t[:, :].rearrange("p (a k c) -> p a k c", a=2, k=NKC), w.bufs) for w in WGUF]
        WGU4 = [Tl(w.t[:, :].rearrange("p (a k c) -> p a k c", a=4, k=NKC), w.bufs) for w in WGUF]
        WD = [tile(f"wdslot{i}", [128, 11, 512], BF16) for i in range(2)]

        ones_mean = tile("ones_mean", [128, 128], BF16)
        blk1 = tile("blk1", [128, 128], BF16)
        blkm = tile("blkm", [128, 128], F32)
        ident = tile("ident", [128, 128], BF16)
        MU2 = tile("MU2", [128, 256], BF16)
        ML = tile("ML", [128, 128], BF16)
        scanmask = tile("scanmask", [128, TT], F32)
        mod = tile("mod", [128, 144], F32)
        bada = tile("bada", [128, 144], F32)
        npre_t = tile("npre_t", [128, 48], F32)
        npost_t = tile("npost_t", [128, 48], F32)
        gA = tile("gA", [128, 48], F32)
        gB = tile("gB", [128, 48], F32)
        c_t = tile("c_t", [128, NKC], F32)
        cst = tile("cst", [128, 8], F32)
        sc_bf = tile("sc_bf", [128, NKC], BF16)

        def cv(h):
            h.memset(ones_mean.t[:], 1.0 / D)
            h.memset(blk1.t[:], 0.0)
            h.memset(blk1.t[0:64, 0:64], 1.0)
            h.memset(blk1.t[64:128, 64:128], 1.0)
            h.memset(blkm.t[:], 0.0)
            h.memset(blkm.t[0:64, 0:64], 1.0 / 64)
            h.memset(blkm.t[64:128, 64:128], 1.0 / 64)
            h.memset(scanmask.t[:], 1.0)
            for i_, v_ in enumerate([1.0, -0.5, 1e-18, LNX_EPS, NORM_EPS]):
                h.memset(cst.t[:, i_:i_ + 1], float(v_))
            ins = None
            for c in range(TT // CH):
                ins = h.memset(scanmask.t[:, c * CH:c * CH + 1], 0.0)
            return ins
        P.op(DVE, cv, writes=ones_mean.bufs + blk1.bufs + blkm.bufs + scanmask.bufs + cst.bufs)

        def cp(h):
            h.memset(ident.t[:], 1.0)
            h.affine_select(out=ident.t[:], in_=ident.t[:], pattern=[[1, 128]], compare_op=ALU.is_equal,
                            fill=0.0, base=0, channel_multiplier=-1)
            h.memset(MU2.t[:], 1.0)
            h.affine_select(out=MU2.t[:, 0:128], in_=MU2.t[:, 0:128], pattern=[[1, 128]], compare_op=ALU.is_gt,
                            fill=0.0, base=0, channel_multiplier=-1)
            h.affine_select(out=MU2.t[:, 128:256], in_=MU2.t[:, 128:256], pattern=[[1, 128]], compare_op=ALU.is_ge,
                            fill=0.0, base=0, channel_multiplier=-1)
            h.memset(ML.t[:], 1.0)
            return h.affine_select(out=ML.t[:], in_=ML.t[:], pattern=[[-1, 128]], compare_op=ALU.is_gt,
                                   fill=0.0, base=0, channel_multiplier=1)
        P.op(POOL, cp, writes=ident.bufs + MU2.bufs + ML.bufs)

        wbuf = {}
        conv_i = {"i": 0}

        def convert(nm):
            src, dst, shp = wsrc[nm]
            rows = shp[0]
            blk = 256 if shp[1] > 2048 else 512
            bufs = []
            for r0 in range(0, rows, blk):
                b = Buf(f"{nm}_{r0}")
                bufs.append(b)
                key = f"cv{conv_i['i'] % 16}"
                conv_i["i"] += 1
                P.op(POOL, (lambda h, r0=r0, blk=blk: h.dma_start(out=dst[r0:r0 + blk, :], in_=src[r0:r0 + blk, :])),
                     writes=[b], dma_key=key)
            wbuf[nm] = bufs

        def load_small(dst_tile, src_ap, eng=SP):
            P.op(eng, lambda h: h.dma_start(out=dst_tile.t[:], in_=src_ap), writes=dst_tile.bufs, dma_key="par")

        load_small(c_t, cT[:, :])
        load_small(bada, b_ada[:, :])
        load_small(npre_t, npre[:, :])
        load_small(npost_t, npost[:, :])

        convert("wg1")
        convert("wu1")

        P.op(ACT, lambda h: h.activation(out=sc_bf.t[:], in_=c_t.t[:], func=AF.Silu), reads=c_t.bufs, writes=sc_bf.bufs)
        ADA_F = [Tl(fbuf.t[:, i * 8:(i + 1) * 8, :].rearrange("p a t -> p (a t)").rearrange("p (k n) -> p k n", k=NKC),
                    fbuf.bufs[i * 8:(i + 1) * 8]) for i in range(2)]
        ADA_B = [Tl(arena[:, i * 8 * 512:(i + 1) * 8 * 512].rearrange("p (k n) -> p k n", k=NKC),
                    slabB[i * 8:(i + 1) * 8]) for i in range(4)]
        ps_mod = PSB[5]
        for sl in range(72):
            stg = ADA_F[sl % 2]
            slot = ADA_B[sl % 4]
            n0 = sl * 256
            P.op(SP, (lambda h, stg=stg, n0=n0: h.dma_start(
                out=stg.t, in_=w_ada[:, n0:n0 + 256].rearrange("(k p) n -> p k n", p=128))),
                writes=stg.bufs, dma_key=f"ada{sl % 2}")
            if sl % 2 == 0:
                P.op(DVE, (lambda h, stg=stg, slot=slot: h.tensor_copy(slot.t, stg.t)), reads=stg.bufs, writes=slot.bufs)
            else:
                P.op(ACT, (lambda h, stg=stg, slot=slot: h.activation(out=slot.t, in_=stg.t, func=AF.Copy)),
                     reads=stg.bufs, writes=slot.bufs)

            def mm(h, slot=slot, sl=sl):
                ins = None
                for jj in range(2):
                    j = sl * 2 + jj
                    for kc in range(NKC):
                        ins = h.matmul(ps_mod.t[:, j:j + 1], lhsT=slot.t[:, kc, jj * 128:(jj + 1) * 128],
                                       rhs=sc_bf.t[:, kc:kc + 1], start=(kc == 0), stop=(kc == NKC - 1))
                return ins
            P.op(PE, mm, reads=slot.bufs + sc_bf.bufs, writes=ps_mod.bufs)
        P.op(DVE, lambda h: h.tensor_tensor(out=mod.t[:], in0=ps_mod.t[:, 0:144], in1=bada.t[:], op=ALU.add),
             reads=ps_mod.bufs + bada.bufs, writes=mod.bufs)

        def modcol(s, m):
            return mod.t[:, (s * 3 + m) * 16:(s * 3 + m) * 16 + 16]

        def mkvec(h):
            ins = None
            for s in range(3):
                wt = 1.0 if s == 1 else 0.5
                h.scalar_tensor_tensor(out=gA.t[:, s * 16:(s + 1) * 16], in0=modcol(s, 1), scalar=1.0,
                                       in1=npre_t.t[:, s * 16:(s + 1) * 16], op0=ALU.add, op1=ALU.mult)
                ins = h.scalar_tensor_tensor(out=gB.t[:, s * 16:(s + 1) * 16], in0=modcol(s, 2), scalar=1.0,
                                             in1=npost_t.t[:, s * 16:(s + 1) * 16], op0=ALU.add, op1=ALU.mult)
            return ins
        P.op(DVE, mkvec, reads=mod.bufs + npre_t.bufs + npost_t.bufs, writes=gA.bufs + gB.bufs)

        def mkvec2(h):
            h.tensor_scalar(out=gB.t[:, 0:16], in0=gB.t[:, 0:16], scalar1=0.5, scalar2=None, op0=ALU.mult)
            return h.tensor_scalar(out=gB.t[:, 32:48], in0=gB.t[:, 32:48], scalar1=0.5, scalar2=None, op0=ALU.mult)
        P.op(DVE, mkvec2, reads=gB.bufs, writes=gB.bufs)

        convert("wd1")
        if stage >= 2:
            convert("win")
            convert("wout")
        if stage >= 3:
            convert("wg2")
            convert("wu2")
            convert("wd2")

        xbufs = {"xT": [Buf(f"xT{i}") for i in range(NT)], "x1T": [Buf(f"x1T{i}") for i in range(NT)],
                 "x2T": [Buf(f"x2T{i}") for i in range(NT)], "outT": [Buf(f"outT{i}") for i in range(NT)]}
        xaps = {"xT": xT, "x1T": x1T, "x2T": x2T, "outT": outT}

        def next_xr():
            t = xr[xr_i["i"] % NXR]
            k = xr_i["i"] % NXR
            xr_i["i"] += 1
            return t, k

        def load_x(src, ti, kc0):
            t, k = next_xr()
            ap = xaps[src][kc0 * 128:(kc0 + 2) * 128, ti * TT:(ti + 1) * TT].rearrange("(k p) t -> p k t", p=128)
            P.op(SP, lambda h: h.dma_start(out=t.t[:], in_=ap), reads=[xbufs[src][ti]], writes=t.bufs, dma_key=f"xr{k}")
            return t, k

        def rsqrt_from_psum(ps, out_t, eps):
            P.op(ACT, lambda h: h.activation(out=out_t.t[:], in_=ps.t[:], func=AF.Ln, bias=cst.t[:, 4:5], scale=1.0),
                 reads=ps.bufs + cst.bufs, writes=out_t.bufs)
            P.op(ACT, lambda h: h.activation(out=out_t.t[:], in_=out_t.t[:], func=AF.Exp, scale=-0.5),
                 reads=out_t.bufs, writes=out_t.bufs)

        def prenorm_p1(src, ti):
            ps = PSB[5]
            for kc0 in range(0, NKC, 2):
                t, _ = load_x(src, ti, kc0)
                for q in range(2):
                    kc = kc0 + q
                    s_ = sq[sq_i["i"] % 2]
                    sq_i["i"] += 1
                    P.op(ACT, (lambda h, t=t, q=q, s_=s_: h.activation(out=s_.t[:], in_=t.t[:, q, :], func=AF.Square)),
                         reads=t.bufs, writes=s_.bufs)
                    P.op(PE, (lambda h, s_=s_, kc=kc: h.matmul(ps.t[:], lhsT=ones_mean.t[:], rhs=s_.t[:],
                                                                start=(kc == 0), stop=(kc == NKC - 1))),
                         reads=s_.bufs + ones_mean.bufs, writes=ps.bufs)
            rsqrt_from_psum(ps, rstd, NORM_EPS)

        def prenorm_p2(src, ti, s):
            for kc0 in range(0, NKC, 2):
                t, _ = load_x(src, ti, kc0)
                for q in range(2):
                    kc = kc0 + q
                    tmp = next_t32()
                    P.op(DVE, (lambda h, t=t, q=q, tmp=tmp, kc=kc: h.scalar_tensor_tensor(
                        out=tmp.t[:], in0=t.t[:, q, :], scalar=gA.t[:, s * 16 + kc:s * 16 + kc + 1], in1=rstd.t[:],
                        op0=ALU.mult, op1=ALU.mult)), reads=t.bufs + gA.bufs + rstd.bufs, writes=tmp.bufs)
                    P.op(ACT, (lambda h, tmp=tmp, kc=kc: h.activation(
                        out=hT.t[:, kc, :], in_=tmp.t[:], func=AF.Identity,
                        bias=mod.t[:, (s * 3) * 16 + kc:(s * 3) * 16 + kc + 1], scale=1.0)),
                        reads=tmp.bufs + mod.bufs, writes=hT.bufs)

        def evac_f(ps, dc, ps2):
            P.op(ACT, lambda h: h.activation(out=fbuf.t[:, dc, :], in_=ps.t[:], func=AF.Copy),
                 reads=ps.bufs, writes=[fbuf.bufs[dc]])
            s_ = sq[sq_i["i"] % 2]
            sq_i["i"] += 1
            P.op(ACT, lambda h: h.activation(out=s_.t[:], in_=ps.t[:], func=AF.Square), reads=ps.bufs, writes=s_.bufs)
            P.op(PE, lambda h: h.matmul(ps2.t[:], lhsT=ones_mean.t[:], rhs=s_.t[:], start=(dc == 0), stop=(dc == NKC - 1)),
                 reads=s_.bufs + ones_mean.bufs, writes=ps2.bufs)

        def postnorm(src, dst, ti, s, ps2, final):
            rsqrt_from_psum(ps2, rstd2, NORM_EPS)
            for kc0 in range(0, NKC, 2):
                t, k = load_x(src, ti, kc0)
                for q in range(2):
                    kc = kc0 + q
                    tmp = next_t32()
                    P.op(DVE, (lambda h, tmp=tmp, kc=kc: h.scalar_tensor_tensor(
                        out=tmp.t[:], in0=fbuf.t[:, kc, :], scalar=gB.t[:, s * 16 + kc:s * 16 + kc + 1], in1=rstd2.t[:],
                        op0=ALU.mult, op1=ALU.mult)), reads=[fbuf.bufs[kc]] + gB.bufs + rstd2.bufs, writes=tmp.bufs)
                    P.op(DVE, (lambda h, t=t, q=q, tmp=tmp: h.tensor_tensor(out=t.t[:, q, :], in0=t.t[:, q, :], in1=tmp.t[:],
                                                                            op=ALU.add)),
                         reads=tmp.bufs + t.bufs, writes=t.bufs)
                ap = xaps[dst][kc0 * 128:(kc0 + 2) * 128, ti * TT:(ti + 1) * TT].rearrange("(k p) t -> p k t", p=128)
                o = P.op(SP, (lambda h, t=t, ap=ap: h.dma_start(out=ap, in_=t.t[:])), reads=t.bufs,
                         writes=[xbufs[dst][ti]], dma_key=f"st{k}")
                if final:
                    P.final_waits.append(o)

        def ffn_phase(s, src, dst, wg, wu, wd, final):
            wgb, wub, wdb = wsrc[wg][1], wsrc[wu][1], wsrc[wd][1]
            UT = [slab_bf(i) for i in range(NFC)]
            NGU = NT * (NFC // 2)
            NWD = NT * 16

            def gu_load(idx):
                if idx >= NGU:
                    return
                fp = idx % (NFC // 2)
                k = idx % 2
                slot = WGU[k]
                c0 = fp * 256
                P.op(SP, lambda h: h.dma_start(out=slot.t[:, 0, :, :], in_=wgb[:, c0:c0 + 256].rearrange("(k p) c -> p k c", p=128)),
                     reads=wbuf[wg], writes=slot.bufs, dma_key=f"wg{k}")
                P.op(SP, lambda h: h.dma_start(out=slot.t[:, 1, :, :], in_=wub[:, c0:c0 + 256].rearrange("(k p) c -> p k c", p=128)),
                     reads=wbuf[wu], writes=slot.bufs, dma_key=f"wu{k}")

            def wd_load(idx):
                if idx >= NWD:
                    return
                g = (idx % 16) // 4
                qq = idx % 4
                k = idx % 2
                slot = WD[k]
                r0 = qq * 11 * 128
                P.op(SP, lambda h: h.dma_start(
                    out=slot.t[:], in_=wdb[r0:r0 + 11 * 128, g * 512:(g + 1) * 512].rearrange("(f p) c -> p f c", p=128)),
                    reads=wbuf[wd], writes=slot.bufs, dma_key=f"wd{k}")

            def gu_step(idx):
                fp = idx % (NFC // 2)
                slot = WGU[idx % 2]
                for q in range(2):
                    gu_chunk(slot, fp * 2 + q, q)
                gu_load(idx + 2)

            def gu_chunk(slot, fc, q):
                pg = PSB[fc % 2]
                pu = PSB[2 + fc % 2]

                def mm(h):
                    ins = None
                    for kc in range(NKC):
                        h.matmul(pg.t[:], lhsT=slot.t[:, 0, kc, q * 128:(q + 1) * 128], rhs=hT.t[:, kc, :],
                                 start=(kc == 0), stop=(kc == NKC - 1))
                    for kc in range(NKC):
                        ins = h.matmul(pu.t[:], lhsT=slot.t[:, 1, kc, q * 128:(q + 1) * 128], rhs=hT.t[:, kc, :],
                                       start=(kc == 0), stop=(kc == NKC - 1))
                    return ins
                P.op(PE, mm, reads=slot.bufs + hT.bufs, writes=pg.bufs + pu.bufs)
                sg = sgt[sgt_i["i"] % 2]
                sgt_i["i"] += 1
                P.op(ACT, lambda h: h.activation(out=sg.t[:], in_=pg.t[:], func=AF.Silu), reads=pg.bufs, writes=sg.bufs)
                u = UT[fc]
                P.op(DVE, lambda h: h.tensor_tensor(out=u.t, in0=sg.t[:], in1=pu.t[:], op=ALU.mult),
                     reads=sg.bufs + pu.bufs, writes=u.bufs)

            def wd_step(idx):
                qq = idx % 4
                slot = WD[idx % 2]

                def mm(h):
                    ins = None
                    for fl in range(11):
                        fc = qq * 11 + fl
                        for d_ in range(4):
                            ins = h.matmul(PSB[d_].t[:], lhsT=slot.t[:, fl, d_ * 128:(d_ + 1) * 128], rhs=UT[fc].t,
                                           start=(fc == 0), stop=(fc == NFC - 1))
                    return ins
                rd = list(slot.bufs)
                for fl in range(11):
                    rd = rd + UT[qq * 11 + fl].bufs
                P.op(PE, mm, reads=rd, writes=PSB[0].bufs + PSB[1].bufs + PSB[2].bufs + PSB[3].bufs)
                wd_load(idx + 2)

            prenorm_p1(src, 0)
            prenorm_p2(src, 0, s)
            gu_load(0)
            gu_load(1)
            for ti in range(NT):
                for fp in range(NFC // 2):
                    gu_step(ti * (NFC // 2) + fp)
                    if ti == 0 and fp == 11:
                        wd_load(0)
                        wd_load(1)
                if ti + 1 < NT:
                    prenorm_p1(src, ti + 1)
                ps2 = PSB[4]
                for g in range(4):
                    for qq in range(4):
                        wd_step(ti * 16 + g * 4 + qq)
                    for d_ in range(4):
                        evac_f(PSB[d_], g * 4 + d_, ps2)
                    if g == 0 and ti + 1 < NT:
                        prenorm_p2(src, ti + 1, s)
                postnorm(src, dst, ti, s, ps2, final)

        ffn_phase(0, "xT", "x1T" if stage > 1 else "outT", "wg1", "wu1", "wd1", final=(stage == 1))

        if stage >= 2:
            mixer_phase(nc, P, locals())
        if stage >= 3:
            ffn_phase(2, "x2T", "outT", "wg2", "wu2", "wd2", final=True)

        P.emit_all()
    return nc


def mixer_phase(nc, P, E):
    sb = E["sb"]; tile = E["tile"]; PSB = E["PSB"]; PST = E["PST"]; nextbank = E["nextbank"]
    hT = E["hT"]; fbuf = E["fbuf"]; WGU4 = E["WGU4"]; WD = E["WD"]; wsrc = E["wsrc"]; wbuf = E["wbuf"]
    NT = E["NT"]; stage = E["stage"]; arena = E["arena"]; slabB = E["slabB"]; t32 = E["t32"]
    blk1 = E["blk1"]; blkm = E["blkm"]; ident = E["ident"]; MU2 = E["MU2"]; ML = E["ML"]; scanmask = E["scanmask"]
    prenorm_p1 = E["prenorm_p1"]; prenorm_p2 = E["prenorm_p2"]; postnorm = E["postnorm"]; evac_f = E["evac_f"]
    cst = E["cst"]; muT = E["muT"]; poolw = E["poolw"]; pscale = E["pscale"]; vecs = E["vecs"]; wa2 = E["wa2"]; g2a = E["g2a"]; g2b = E["g2b"]
    winb = wsrc["win"][1]; woutb = wsrc["wout"][1]

    def ft(i):
        return Tl(fbuf.t[:, i, :], [fbuf.bufs[i]])
    (I_RS, I_KS, I_VS, I_E1, I_L1, I_NL, I_A, I_G, I_KK, I_K2, I_BP, I_BONUS, I_CWN, I_EA, I_ER, I_EK) = range(16)
    F = [ft(i) for i in range(16)]
    I_CWX, I_Y, I_YC = I_NL, I_EA, I_ER

    def slab(i, n=1):
        return Tl(arena[:, i * 512:(i + n) * 512], slabB[i:i + n])
    catT = [slab(i) for i in range(16)]
    lora_in = slab(16)
    sg1 = slab(17)
    sg2 = slab(18)
    AR = Tl(arena[:, 19 * 512:21 * 512].rearrange("p (c t i) -> p c t i", c=4, t=2), slabB[19:21])
    Kt = slab(21)
    Bt = slab(22)
    Vt = slab(23)
    sqb = slab(24)
    pooled = slab(25)
    tok3 = [Tl(arena[:, (26 + i) * 512:(26 + i) * 512 + 384], [slabB[26 + i]]) for i in range(2)]

    def mat(sl, q, nm):
        return Tl(arena[:, sl * 512 + q * 128: sl * 512 + (q + 1) * 128], [Buf(nm)])
    G1M = [Tl(arena[:, 28 * 512 + hh * 256:28 * 512 + (hh + 1) * 256], [Buf(f"G1M{hh}")]) for hh in range(2)]
    G2M = [Tl(arena[:, 29 * 512 + hh * 256:29 * 512 + (hh + 1) * 256], [Buf(f"G2M{hh}")]) for hh in range(2)]
    PP = [[mat(30 + hh, pp, f"P{hh}{pp}") for pp in range(2)] for hh in range(2)]
    PPT = [[mat(30 + hh, 2 + pp, f"PT{hh}{pp}") for pp in range(2)] for hh in range(2)]
    TTm = [[mat(32, hh * 2 + pp, f"TT{hh}{pp}") for pp in range(2)] for hh in range(2)]
    Xbf = mat(33, 0, "Xbf")
    Ubf = mat(33, 1, "Ubf")
    Hz = [Tl(arena[:, 33 * 512 + 256 + hh * 64: 33 * 512 + 256 + (hh + 1) * 64], [Buf(f"Hz{hh}")]) for hh in range(2)]
    poolw_bf = Tl(arena[:, 34 * 512:35 * 512].rearrange("p (g d) -> p g d", g=4), [slabB[34]])
    wa2_bf = Tl(arena[:, 35 * 512:38 * 512], slabB[35:38])
    g2a_bf = Tl(arena[:, 38 * 512:41 * 512], slabB[38:41])
    g2b_bf = Tl(arena[:, 41 * 512:44 * 512], slabB[41:44])

    Hst = [tile(f"H{j}", [128, 64], F32) for j in range(NPAIR)]
    H0p = tile("H0p", [128, 64], F32)
    Ee = tile("Ee", [128, 16 + TT], F32)
    pcar = [tile(f"pcar{g}", [128, 16], F32) for g in range(4)]
    wtmp = [tile(f"wtmp{i}", [128, 16 + TT], F32) for i in range(2)]
    carry = tile("carry", [128, 43], F32)
    mu_t = tile("mu_t", [128, 43], F32)
    omu_t = tile("omu_t", [128, 43], F32)
    vec_t = tile("vec_t", [128, 7 * NPAIR], F32)
    dvec = tile("dvec", [128, 3 * NPAIR], F32)
    pscale_t = tile("pscale_t", [128, 4], F32)
    invc = tile("invc", [128, 16], F32)
    ncw = tile("ncw", [128, 4], F32)
    wc = tile("wc", [128, 4], F32)
    stage32 = Tl(fbuf.t[:, 0:3, :].rearrange("p a t -> p (a t)"), fbuf.bufs[0:3])

    V_W0, V_A0, V_KK, V_KA, V_RK, V_LW, V_LB = range(7)

    def vcol(v, j):
        return vec_t.t[:, v * NPAIR + j:v * NPAIR + j + 1]

    def act(out_ap, in_ap, func, reads, writes, **kw):
        P.op(ACT, lambda h: h.activation(out=out_ap, in_=in_ap, func=func, **kw), reads=reads, writes=writes)

    def dve(fn, reads, writes):
        P.op(DVE, fn, reads=reads, writes=writes)

    def pool(fn, reads, writes):
        P.op(POOL, fn, reads=reads, writes=writes)

    def ld(dst_ap, src_ap, bufs):
        P.op(SP, lambda h: h.dma_start(out=dst_ap, in_=src_ap), writes=bufs, dma_key="par")
    ld(mu_t.t[:], muT[:, :], mu_t.bufs)
    ld(vec_t.t[:], vecs[:, :], vec_t.bufs)
    ld(pscale_t.t[:], pscale[:, :], pscale_t.bufs)

    def setup_v(h):
        h.tensor_scalar(out=omu_t.t[:], in0=mu_t.t[:], scalar1=-1.0, scalar2=1.0, op0=ALU.mult, op1=ALU.add)
        h.tensor_scalar(out=dvec.t[:, 0:2 * NPAIR], in0=vec_t.t[:, 0:2 * NPAIR], scalar1=-1.0, scalar2=None, op0=ALU.mult)
        h.tensor_scalar(out=dvec.t[:, 2 * NPAIR:3 * NPAIR], in0=vec_t.t[:, V_KA * NPAIR:(V_KA + 1) * NPAIR],
                        scalar1=-1.0, scalar2=1.0, op0=ALU.mult, op1=ALU.add)
        h.memset(carry.t[:], 0.0)
        for g in range(4):
            h.memset(pcar[g].t[:], 0.0)
        for j in range(NPAIR):
            h.memset(Hst[j].t[:], 0.0)
        h.memset(Hz[0].t, 0.0)
        h.memset(Hz[1].t, 0.0)
        ins = None
        for t_ in range(16):
            ins = h.memset(invc.t[:, t_:t_ + 1], 1.0 / (t_ + 1))
        return ins
    wr = omu_t.bufs + dvec.bufs + carry.bufs + invc.bufs + Hz[0].bufs + Hz[1].bufs
    for g in range(4):
        wr = wr + pcar[g].bufs
    for j in range(NPAIR):
        wr = wr + Hst[j].bufs
    P.op(DVE, setup_v, reads=mu_t.bufs + vec_t.bufs, writes=wr)

    def ld_cast(dst_tl, dst_ap, src_ap, rows, width):
        P.op(SP, lambda h: h.dma_start(out=stage32.t[0:rows, 0:width], in_=src_ap), writes=stage32.bufs, dma_key="par")
        P.op(DVE, lambda h: h.tensor_copy(dst_ap, stage32.t[0:rows, 0:width]), reads=stage32.bufs, writes=dst_tl.bufs)
    ld_cast(poolw_bf, arena[:, 34 * 512:35 * 512], poolw[:, :], 128, 512)
    ld_cast(wa2_bf, wa2_bf.t[:, :], wa2[:, :], 128, RW)
    ld_cast(g2a_bf, g2a_bf.t[:, :], g2a[:, :], 128, RW)
    ld_cast(g2b_bf, g2b_bf.t[0:96, :], g2b[:, :], 96, RW)

    slot_i = {"w": 0, "d": 0}
    tab = {"i": 0}

    def win_load(chunks):
        k = slot_i["w"] % 2
        slot = WGU4[k]
        slot_i["w"] += 1
        for i, c in enumerate(chunks):
            wdt = min(128, INW - c * 128)
            dst_ap = slot.t[:, i, :, 0:wdt]
            src_ap = winb[:, c * 128:c * 128 + wdt].rearrange("(k p) c -> p k c", p=128)
            P.op(SP, (lambda h, dst_ap=dst_ap, src_ap=src_ap: h.dma_start(out=dst_ap, in_=src_ap)),
                 reads=wbuf["win"], writes=slot.bufs, dma_key=(f"wg{k}" if i % 2 == 0 else f"wu{k}"))
        return slot

    def proj(slot, i, rows):
        ps = nextbank()

        def mm(h):
            ins = None
            for kc in range(NKC):
                ins = h.matmul(ps.t[0:rows, :], lhsT=slot.t[:, i, kc, 0:rows], rhs=hT.t[:, kc, :], start=(kc == 0), stop=(kc == NKC - 1))
            return ins
        P.op(PE, mm, reads=slot.bufs + hT.bufs, writes=ps.bufs)
        return ps

    def tshift(ps, c, out, rows=128):
        a_ = t32[2 * (tab["i"] % 2)]
        b_ = t32[2 * (tab["i"] % 2) + 1]
        tab["i"] += 1
        act(a_.t[0:rows, :], ps.t[0:rows, :], AF.Identity, ps.bufs + omu_t.bufs, a_.bufs, scale=omu_t.t[0:rows, c:c + 1])
        act(b_.t[0:rows, :], ps.t[0:rows, :], AF.Identity, ps.bufs + mu_t.bufs, b_.bufs, scale=mu_t.t[0:rows, c:c + 1])

        def f(h):
            h.tensor_tensor(out=out.t[0:rows, 1:TT], in0=a_.t[0:rows, 1:TT], in1=b_.t[0:rows, 0:TT - 1], op=ALU.add)
            return h.tensor_tensor(out=out.t[0:rows, 0:1], in0=a_.t[0:rows, 0:1], in1=carry.t[0:rows, c:c + 1], op=ALU.add)
        pool(f, a_.bufs + b_.bufs + carry.bufs, out.bufs)
        pool(lambda h: h.tensor_copy(carry.t[0:rows, c:c + 1], b_.t[0:rows, TT - 1:TT]), b_.bufs, carry.bufs)

    def sigmoid_to(out_ap, out_bufs, in_t, rows):
        e1, l1 = F[I_E1], F[I_L1]
        act(e1.t[0:rows, :], in_t.t[0:rows, :], AF.Exp, in_t.bufs, e1.bufs, scale=-1.0)
        act(l1.t[0:rows, :], e1.t[0:rows, :], AF.Ln, e1.bufs, l1.bufs, bias=cst.t[0:rows, 0:1], scale=1.0)
        act(out_ap, l1.t[0:rows, :], AF.Exp, l1.bufs, out_bufs, scale=-1.0)

    def v3(t_):
        return t_.t.rearrange("p (c i) -> p c i", c=4)

    def lora_inputs():
        sx40, sx41, sx42 = F[I_RS], F[I_KS], F[I_VS]
        slot = win_load([40, 41, 42])
        ps = proj(slot, 0, 128)
        tshift(ps, 40, sx40)
        ps = proj(slot, 1, 128)
        tshift(ps, 41, sx41)
        ps = proj(slot, 2, 96)
        tshift(ps, 42, sx42, rows=96)
        e1, l1 = F[I_E1], F[I_L1]
        act(e1.t[0:64, :], sx40.t[0:64, :], AF.Exp, sx40.bufs, e1.bufs, scale=-2.0)
        act(l1.t[0:64, :], e1.t[0:64, :], AF.Ln, e1.bufs, l1.bufs, bias=cst.t[0:64, 0:1], scale=1.0)
        act(e1.t[0:64, :], l1.t[0:64, :], AF.Exp, l1.bufs, e1.bufs, scale=-1.0)
        dve(lambda h: h.tensor_scalar(out=lora_in.t[0:64, :], in0=e1.t[0:64, :], scalar1=2.0, scalar2=-1.0, op0=ALU.mult, op1=ALU.add),
            e1.bufs, lora_in.bufs)
        act(lora_in.t[64:128, :], sx40.t[64:128, :], AF.Copy, sx40.bufs, lora_in.bufs)
        sigmoid_to(sg1.t[:, :], sg1.bufs, sx41, 128)
        sigmoid_to(sg2.t[0:96, :], sg2.bufs, sx42, 96)

    def pool_group(ti, slot, g):
        win = 2 << g
        L = g + 1
        ps = proj(slot, g, 128)
        pool(lambda h: h.tensor_copy(Ee.t[:, 0:16], pcar[g].t[:]), pcar[g].bufs, Ee.bufs)
        act(Ee.t[:, 16:16 + TT], ps.t[:], AF.Copy, ps.bufs, Ee.bufs)
        pool(lambda h: h.tensor_copy(pcar[g].t[:], Ee.t[:, TT:TT + 16]), Ee.bufs, pcar[g].bufs)
        cur, cur_lo = Ee, 0
        for k in range(1, L + 1):
            lo = 16 - win + (1 << k)
            n = 16 + TT - lo
            nxt = wtmp[k % 2]
            a0 = lo - cur_lo
            b0 = lo - (1 << (k - 1)) - cur_lo
            assert b0 >= 0

            def f(h, nxt=nxt, cur=cur, a0=a0, b0=b0, n=n):
                return h.tensor_tensor(out=nxt.t[:, 0:n], in0=cur.t[:, a0:a0 + n], in1=cur.t[:, b0:b0 + n], op=ALU.add)
            pool(f, cur.bufs, nxt.bufs)
            cur, cur_lo = nxt, lo
        assert cur_lo == 16
        wl = cur
        dve(lambda h: h.scalar_tensor_tensor(out=pooled.t[:, :], in0=wl.t[:, 0:TT], scalar=1.0 / win, in1=Ee.t[:, 16:16 + TT],
                                             op0=ALU.mult, op1=ALU.subtract), wl.bufs + Ee.bufs, pooled.bufs)
        if ti == 0:
            tmpc = t32[0]
            nfix = win - 1
            dve(lambda h: h.tensor_tensor(out=tmpc.t[:, 0:nfix], in0=wl.t[:, 0:nfix], in1=invc.t[:, 0:nfix], op=ALU.mult),
                wl.bufs + invc.bufs, tmpc.bufs)
            dve(lambda h: h.tensor_tensor(out=pooled.t[:, 0:nfix], in0=tmpc.t[:, 0:nfix], in1=Ee.t[:, 16:16 + nfix], op=ALU.subtract),
                tmpc.bufs + Ee.bufs, pooled.bufs)
        psm = nextbank()
        P.op(PE, lambda h: h.matmul(psm.t[:], lhsT=poolw_bf.t[:, g, :], rhs=pooled.t[:, :], start=True, stop=True),
             reads=poolw_bf.bufs + pooled.bufs, writes=psm.bufs)
        act(catT[g].t, psm.t[:], AF.Identity, psm.bufs + pscale_t.bufs, catT[g].bufs, scale=pscale_t.t[:, g:g + 1])

    chn = sb("chn", [128, 8 * 256 + 8 * 128 * 2 + 4 * 384], BF16)
    G1A = arena[:, 26 * 512:30 * 512].rearrange("p (q n) -> p q n", q=8)
    PA = arena[:, 30 * 512:32 * 512].rearrange("p (q n) -> p q n", q=8)
    o = 0
    G2A = chn[:, o:o + 8 * 256].rearrange("p (q n) -> p q n", q=8); o += 8 * 256
    PTA = chn[:, o:o + 8 * 128].rearrange("p (q n) -> p q n", q=8); o += 8 * 128
    TTA = chn[:, o:o + 8 * 128].rearrange("p (q n) -> p q n", q=8); o += 8 * 128
    TOK = chn[:, o:o + 4 * 384].rearrange("p (c n) -> p c n", c=4); o += 4 * 384
    G1B = [slabB[26 + 2 * h] for h in range(2)]
    G1Bx = [[slabB[26 + 2 * h], slabB[27 + 2 * h]] for h in range(2)]
    G2B = [Buf(f"G2h{h}") for h in range(2)]
    PB = [slabB[30 + h] for h in range(2)]
    PTB = [Buf(f"PTh{h}") for h in range(2)]
    TTB = [Buf(f"TTh{h}") for h in range(2)]
    TOKB = [Buf(f"tok{i}") for i in range(2)]
    MU2x2 = Tl(arena[:, 32 * 512:33 * 512].rearrange("p (r n) -> p r n", r=2), [slabB[32]])
    MLx4 = tile("MLx4", [128, 4, 128], BF16)
    IDx4 = tile("IDx4", [128, 4, 128], BF16)

    def mkc(h):
        ins = None
        for r in range(2):
            pass
        for r in range(4):
            h.tensor_copy(MLx4.t[:, r, :], ML.t[:])
            ins = h.tensor_copy(IDx4.t[:, r, :], ident.t[:])
        return ins
    P.op(DVE, mkc, reads=MU2.bufs + ML.bufs + ident.bufs, writes=MLx4.bufs + IDx4.bufs)

    def chains(j):
        for half in range(2):
            def trp(h, half=half):
                ins = None
                for cc in range(2):
                    c = half * 2 + cc
                    cs = slice(c * CH, (c + 1) * CH)
                    h.transpose(PST.t[:, cc * 384:cc * 384 + 128], Bt.t[:, cs], ident.t[:])
                    h.transpose(PST.t[:, cc * 384 + 128:cc * 384 + 256], Kt.t[:, cs], ident.t[:])
                    ins = h.transpose(PST.t[:, cc * 384 + 256:cc * 384 + 384], Vt.t[:, cs], ident.t[:])
                return ins
            P.op(PE, trp, reads=Bt.bufs + Kt.bufs + Vt.bufs + ident.bufs, writes=PST.bufs)
            act(TOK[:, half * 2:half * 2 + 2, :], PST.t[:, 0:768].rearrange("p (c n) -> p c n", c=2), AF.Copy, PST.bufs, [TOKB[half]])
        for hh in range(2):
            ph = slice(64 * hh, 64 * hh + 64)
            for cp in range(2):
                b1, b2 = nextbank(), nextbank()

                def g12(h, b1=b1, b2=b2, ph=ph, cp=cp):
                    ins = None
                    for cc in range(2):
                        c = cp * 2 + cc
                        cs = slice(c * CH, (c + 1) * CH)
                        arf = AR.t[ph, c, :, :].rearrange("p t i -> p (t i)")
                        h.matmul(b1.t[:, cc * 256:(cc + 1) * 256], lhsT=Kt.t[ph, cs], rhs=arf, start=True, stop=True)
                        ins = h.matmul(b2.t[:, cc * 256:(cc + 1) * 256], lhsT=Bt.t[ph, cs], rhs=arf, start=True, stop=True)
                    return ins
                P.op(PE, g12, reads=Kt.bufs + Bt.bufs + AR.bufs, writes=b1.bufs + b2.bufs)
                q0 = hh * 4 + cp * 2
                dve(lambda h, b1=b1, q0=q0: h.tensor_tensor(out=G1A[:, q0:q0 + 2, :], in0=b1.t[:].rearrange("p (c n) -> p c n", c=2),
                                                           in1=MU2x2.t, op=ALU.mult), b1.bufs + MU2x2.bufs, G1Bx[hh])
                dve(lambda h, b2=b2, q0=q0: h.tensor_tensor(out=G2A[:, q0:q0 + 2, :], in0=b2.t[:].rearrange("p (c n) -> p c n", c=2),
                                                           in1=MU2x2.t, op=ALU.mult), b2.bufs + MU2x2.bufs, [G2B[hh]])
            b3 = nextbank()

            def g3(h, b3=b3, ph=ph):
                ins = None
                for c in range(4):
                    cs = slice(c * CH, (c + 1) * CH)
                    ins = h.matmul(b3.t[:, c * 128:(c + 1) * 128], lhsT=AR.t[ph, c, 0, :], rhs=Bt.t[ph, cs], start=True, stop=True)
                return ins
            P.op(PE, g3, reads=Bt.bufs + AR.bufs, writes=b3.bufs)
            dve(lambda h, b3=b3, hh=hh: h.tensor_tensor(out=PTA[:, hh * 4:hh * 4 + 4, :], in0=b3.t[:].rearrange("p (c n) -> p c n", c=4),
                                                       in1=MLx4.t[:], op=ALU.mult), b3.bufs + MLx4.bufs, [PTB[hh]])
            pool(lambda h, hh=hh: h.tensor_tensor(out=TTA[:, hh * 4:hh * 4 + 4, :], in0=G2A[:, hh * 4:hh * 4 + 4, 0:128], in1=IDx4.t[:], op=ALU.add),
                 [G2B[hh]] + IDx4.bufs, [TTB[hh]])
        for m in range(1, 7):
            pcs = []
            for hh in range(2):
                qs = range(hh * 4, hh * 4 + 4)
                Pin = (lambda q: G2A[:, q, 0:128]) if m == 1 else (lambda q: PA[:, q, :])
                Prd = [G2B[hh]] if m == 1 else [PB[hh]]
                if m < 6:
                    ba = nextbank()

                    def sqa(h, ba=ba, qs=qs, Pin=Pin):
                        ins = None
                        for i, q in enumerate(qs):
                            ins = h.matmul(ba.t[:, i * 128:(i + 1) * 128], lhsT=PTA[:, q, :], rhs=Pin(q), start=True, stop=True)
                        return ins
                    P.op(PE, sqa, reads=Prd + [PTB[hh]], writes=ba.bufs)
                bb = nextbank()

                def sqb_(h, bb=bb, qs=qs, Pin=Pin):
                    ins = None
                    for i, q in enumerate(qs):
                        ins = h.matmul(bb.t[:, i * 128:(i + 1) * 128], lhsT=Pin(q), rhs=PTA[:, q, :], start=True, stop=True)
                    return ins
                P.op(PE, sqb_, reads=Prd + [PTB[hh]], writes=bb.bufs)
                if hh == 0:
                    if m < 6:
                        act(PA[:, hh * 4:hh * 4 + 4, :], ba.t[:].rearrange("p (c n) -> p c n", c=4), AF.Copy, ba.bufs, [PB[hh]])
                    dve(lambda h, bb=bb, hh=hh: h.tensor_copy(PTA[:, hh * 4:hh * 4 + 4, :], bb.t[:].rearrange("p (c n) -> p c n", c=4)),
                        bb.bufs, [PTB[hh]])
                else:
                    act(PTA[:, hh * 4:hh * 4 + 4, :], bb.t[:].rearrange("p (c n) -> p c n", c=4), AF.Copy, bb.bufs, [PTB[hh]])
                    if m < 6:
                        act(PA[:, hh * 4:hh * 4 + 4, :], ba.t[:].rearrange("p (c n) -> p c n", c=4), AF.Copy, ba.bufs, [PB[hh]])
            for hh in range(2):
                qs = range(hh * 4, hh * 4 + 4)
                bc = nextbank()

                def ttu(h, bc=bc, qs=qs):
                    ins = None
                    for i, q in enumerate(qs):
                        ins = h.matmul(bc.t[:, i * 128:(i + 1) * 128], lhsT=PTA[:, q, :], rhs=TTA[:, q, :], start=True, stop=True)
                    return ins
                P.op(PE, ttu, reads=[PTB[hh], TTB[hh]], writes=bc.bufs)
                dve(lambda h, bc=bc, hh=hh: h.tensor_tensor(out=TTA[:, hh * 4:hh * 4 + 4, :], in0=bc.t[:].rearrange("p (c n) -> p c n", c=4),
                                                           in1=TTA[:, hh * 4:hh * 4 + 4, :], op=ALU.add), bc.bufs + [TTB[hh]], [TTB[hh]])

    def chunk(j, c, ysb):
        Hj = Hst[j]
        cs = slice(c * CH, (c + 1) * CH)
        tkb = [TOKB[c // 2]]
        Btok = lambda hh: TOK[:, c, hh * 64:(hh + 1) * 64]
        Ktok = lambda hh: TOK[:, c, 128 + hh * 64:128 + (hh + 1) * 64]
        Vtok = lambda hh: TOK[:, c, 256 + hh * 64:256 + (hh + 1) * 64]
        q_ = lambda hh: hh * 4 + c
        dve(lambda h: h.tensor_scalar(out=H0p.t[:], in0=Hj.t[:], scalar1=wc.t[:, c:c + 1], scalar2=None, op0=ALU.mult),
            Hj.bufs + wc.bufs, H0p.bufs)
        dve(lambda h: h.tensor_scalar(out=Hz[0].t[0:64, :], in0=Hj.t[0:64, :], scalar1=wc.t[0:64, c:c + 1], scalar2=None, op0=ALU.mult),
            Hj.bufs + wc.bufs, Hz[0].bufs)
        dve(lambda h: h.tensor_scalar(out=Hz[1].t[64:128, :], in0=Hj.t[64:128, :], scalar1=wc.t[64:128, c:c + 1], scalar2=None, op0=ALU.mult),
            Hj.bufs + wc.bufs, Hz[1].bufs)
        px = nextbank()

        def mx(h):
            ins = None
            for hh in range(2):
                h.matmul(px.t[:, hh * 64:(hh + 1) * 64], lhsT=AR.t[:, c, 0, :], rhs=Hz[hh].t, start=True, stop=False)
                ins = h.matmul(px.t[:, hh * 64:(hh + 1) * 64], lhsT=G1A[:, q_(hh), 0:128], rhs=Vtok(hh), start=False, stop=True)
            return ins
        P.op(PE, mx, reads=AR.bufs + Hz[0].bufs + Hz[1].bufs + G1Bx[0] + G1Bx[1] + tkb, writes=px.bufs)
        act(Xbf.t, px.t[:, 0:128], AF.Copy, px.bufs, Xbf.bufs)
        pu_ = nextbank()

        def mu_(h):
            ins = None
            for hh in range(2):
                ins = h.matmul(pu_.t[:, hh * 64:(hh + 1) * 64], lhsT=TTA[:, q_(hh), :], rhs=Xbf.t[:, hh * 64:(hh + 1) * 64], start=True, stop=True)
            return ins
        P.op(PE, mu_, reads=TTB + Xbf.bufs, writes=pu_.bufs)
        act(Ubf.t, pu_.t[:, 0:128], AF.Copy, pu_.bufs, Ubf.bufs)
        py, psn = nextbank(), nextbank()

        def my(h):
            ins = None
            for hh in range(2):
                po = slice(64 * hh, 64 * hh + 64)
                h.matmul(py.t[po, 0:128], lhsT=Hz[hh].t, rhs=AR.t[:, c, 1, :], start=True, stop=False)
                h.matmul(py.t[po, 0:128], lhsT=Ubf.t[:, hh * 64:(hh + 1) * 64], rhs=G2A[:, q_(hh), 128:256], start=False, stop=False)
                h.matmul(py.t[po, 0:128], lhsT=Vtok(hh), rhs=G1A[:, q_(hh), 128:256], start=False, stop=True)
            for hh in range(2):
                po = slice(64 * hh, 64 * hh + 64)
                h.matmul(psn.t[po, 0:64], lhsT=Btok(hh), rhs=Ubf.t[:, hh * 64:(hh + 1) * 64], start=True, stop=False)
                ins = h.matmul(psn.t[po, 0:64], lhsT=Ktok(hh), rhs=Vtok(hh), start=False, stop=True)
            return ins
        P.op(PE, my, reads=Hz[0].bufs + Hz[1].bufs + AR.bufs + Ubf.bufs + G1Bx[0] + G1Bx[1] + G2B + tkb, writes=py.bufs + psn.bufs)
        dve(lambda h: h.tensor_tensor(out=Hj.t[:], in0=psn.t[:, 0:64], in1=H0p.t[:], op=ALU.add), psn.bufs + H0p.bufs, Hj.bufs)
        act(ysb.t[:, cs], py.t[:, 0:128], AF.Copy, py.bufs, ysb.bufs)

    def pair(j, mid=None):
        slot = win_load([4 + j, 16 + j, 28 + j])
        rs, ks, vs = F[I_RS], F[I_KS], F[I_VS]
        e1, l1, nl, a_t, g_t = F[I_E1], F[I_L1], F[I_NL], F[I_A], F[I_G]
        kk, k2, bp, bonus = F[I_KK], F[I_K2], F[I_BP], F[I_BONUS]
        cwn, cwx, eA, eR, eK = F[I_CWN], F[I_CWX], F[I_EA], F[I_ER], F[I_EK]
        pzw, pza, pg_ = nextbank(), nextbank(), nextbank()
        P.op(PE, lambda h: h.matmul(pzw.t[:], lhsT=wa2_bf.t[0:64, j * 128:(j + 1) * 128], rhs=lora_in.t[0:64, :], start=True, stop=True),
             reads=wa2_bf.bufs + lora_in.bufs, writes=pzw.bufs)
        P.op(PE, lambda h: h.matmul(pza.t[:], lhsT=wa2_bf.t[64:128, j * 128:(j + 1) * 128], rhs=lora_in.t[64:128, :], start=True, stop=True),
             reads=wa2_bf.bufs + lora_in.bufs, writes=pza.bufs)

        def mg(h):
            h.matmul(pg_.t[:], lhsT=g2a_bf.t[:, j * 128:(j + 1) * 128], rhs=sg1.t[:, :], start=True, stop=False)
            return h.matmul(pg_.t[:], lhsT=g2b_bf.t[0:96, j * 128:(j + 1) * 128], rhs=sg2.t[0:96, :], start=False, stop=True)
        P.op(PE, mg, reads=g2a_bf.bufs + g2b_bf.bufs + sg1.bufs + sg2.bufs, writes=pg_.bufs)
        act(e1.t[:, :], pzw.t[:], AF.Exp, pzw.bufs + dvec.bufs, e1.bufs, scale=-1.0, bias=dvec.t[:, j:j + 1])
        act(l1.t[:, :], e1.t[:, :], AF.Ln, e1.bufs, l1.bufs, bias=cst.t[:, 0:1], scale=1.0)
        act(nl.t[:, :], l1.t[:, :], AF.Exp, l1.bufs, nl.bufs, scale=-1.0, bias=cst.t[:, 1:2])
        dve(lambda h: h.tensor_tensor_scan(out=cwn.t[:, :], data0=scanmask.t[:], data1=nl.t[:, :], initial=0.0, op0=ALU.mult, op1=ALU.add),
            scanmask.bufs + nl.bufs, cwn.bufs)
        dve(lambda h: h.tensor_tensor(out=cwx.t[:, :], in0=cwn.t[:, :], in1=nl.t[:, :], op=ALU.subtract), cwn.bufs + nl.bufs, cwx.bufs)
        cend = v3(cwn)[:, :, CH - 1]
        dve(lambda h: h.tensor_scalar(out=ncw.t[:], in0=cend, scalar1=-1.0, scalar2=None, op0=ALU.mult), cwn.bufs, ncw.bufs)
        act(e1.t[:, :], pza.t[:], AF.Exp, pza.bufs + dvec.bufs, e1.bufs, scale=-1.0, bias=dvec.t[:, NPAIR + j:NPAIR + j + 1])
        act(l1.t[:, :], e1.t[:, :], AF.Ln, e1.bufs, l1.bufs, bias=cst.t[:, 0:1], scale=1.0)
        act(a_t.t[:, :], l1.t[:, :], AF.Exp, l1.bufs, a_t.bufs, scale=-1.0)
        act(wc.t[:], cend, AF.Exp, cwn.bufs, wc.bufs, scale=-1.0)

        def exps(h):
            ins = None
            for c in range(4):
                cs = slice(c * CH, (c + 1) * CH)
                ce = cwn.t[:, c * CH + CH - 1:c * CH + CH]
                h.activation(out=eR.t[:, cs], in_=cwn.t[:, cs], func=AF.Exp, scale=-1.0, bias=ce)
                h.activation(out=eA.t[:, cs], in_=cwx.t[:, cs], func=AF.Exp, scale=-1.0, bias=ce)
                ins = h.activation(out=eK.t[:, cs], in_=cwn.t[:, cs], func=AF.Exp, scale=1.0, bias=ncw.t[:, c:c + 1])
            return ins
        P.op(ACT, exps, reads=cwn.bufs + cwx.bufs + ncw.bufs, writes=eR.bufs + eA.bufs + eK.bufs)
        ps = proj(slot, 0, 128)
        tshift(ps, 4 + j, rs)
        ps = proj(slot, 1, 128)
        tshift(ps, 16 + j, ks)
        ps = proj(slot, 2, 128)
        tshift(ps, 28 + j, vs)
        act(g_t.t[:, :], pg_.t[:], AF.Copy, pg_.bufs, g_t.bufs)
        pool(lambda h: h.tensor_copy(Vt.t[:, :], vs.t[:, :]), vs.bufs, Vt.bufs)
        dve(lambda h: h.tensor_tensor(out=AR.t[:, :, 1, :], in0=v3(rs), in1=v3(eR), op=ALU.mult), rs.bufs + eR.bufs, AR.bufs)
        dve(lambda h: h.tensor_scalar(out=kk.t[:, :], in0=ks.t[:, :], scalar1=vcol(V_KK, j), scalar2=None, op0=ALU.mult),
            ks.bufs + vec_t.bufs, kk.bufs)
        act(sqb.t[:, :], kk.t[:, :], AF.Square, kk.bufs, sqb.bufs)
        pss = nextbank()
        P.op(PE, lambda h: h.matmul(pss.t[:], lhsT=blk1.t[:], rhs=sqb.t[:, :], start=True, stop=True),
             reads=blk1.bufs + sqb.bufs, writes=pss.bufs)
        dve(lambda h: h.tensor_scalar(out=k2.t[:, :], in0=a_t.t[:, :], scalar1=vcol(V_KA, j), scalar2=dvec.t[:, 2 * NPAIR + j:2 * NPAIR + j + 1],
                                      op0=ALU.mult, op1=ALU.add), a_t.bufs + vec_t.bufs + dvec.bufs, k2.bufs)
        dve(lambda h: h.tensor_tensor(out=k2.t[:, :], in0=k2.t[:, :], in1=ks.t[:, :], op=ALU.mult), k2.bufs + ks.bufs, k2.bufs)
        dve(lambda h: h.tensor_tensor(out=Kt.t[:, :], in0=k2.t[:, :], in1=eK.t[:, :], op=ALU.mult), k2.bufs + eK.bufs, Kt.bufs)
        act(l1.t[:, :], pss.t[:], AF.Ln, pss.bufs, l1.bufs, bias=cst.t[:, 2:3], scale=1.0)
        act(e1.t[:, :], l1.t[:, :], AF.Exp, l1.bufs, e1.bufs, scale=-0.5)
        dve(lambda h: h.tensor_tensor(out=kk.t[:, :], in0=kk.t[:, :], in1=e1.t[:, :], op=ALU.mult), kk.bufs + e1.bufs, kk.bufs)
        dve(lambda h: h.scalar_tensor_tensor(out=AR.t[:, :, 0, :], in0=v3(kk), scalar=-1.0, in1=v3(eA), op0=ALU.mult, op1=ALU.mult),
            kk.bufs + eA.bufs, AR.bufs)
        dve(lambda h: h.tensor_tensor(out=bp.t[:, :], in0=kk.t[:, :], in1=a_t.t[:, :], op=ALU.mult), kk.bufs + a_t.bufs, bp.bufs)
        dve(lambda h: h.tensor_tensor(out=Bt.t[:, :], in0=bp.t[:, :], in1=eK.t[:, :], op=ALU.mult), bp.bufs + eK.bufs, Bt.bufs)
        if mid is not None:
            mid()
        chains(j)
        dve(lambda h: h.scalar_tensor_tensor(out=sqb.t[:, :], in0=rs.t[:, :], scalar=vcol(V_RK, j), in1=k2.t[:, :], op0=ALU.mult, op1=ALU.mult),
            rs.bufs + k2.bufs + vec_t.bufs, sqb.bufs)
        psr = nextbank()
        P.op(PE, lambda h: h.matmul(psr.t[:], lhsT=blk1.t[:], rhs=sqb.t[:, :], start=True, stop=True),
             reads=blk1.bufs + sqb.bufs, writes=psr.bufs)
        dve(lambda h: h.tensor_tensor(out=bonus.t[:, :], in0=psr.t[:], in1=vs.t[:, :], op=ALU.mult), psr.bufs + vs.bufs, bonus.bufs)

        ysb = F[I_Y]
        for c in range(4):
            chunk(j, c, ysb)

        yc = F[I_YC]
        pm = nextbank()
        P.op(PE, lambda h: h.matmul(pm.t[:], lhsT=blkm.t[:], rhs=ysb.t[:, :], start=True, stop=True),
             reads=blkm.bufs + ysb.bufs, writes=pm.bufs)
        dve(lambda h: h.tensor_tensor(out=yc.t[:, :], in0=ysb.t[:, :], in1=pm.t[:], op=ALU.subtract), ysb.bufs + pm.bufs, yc.bufs)
        act(ysb.t[:, :], yc.t[:, :], AF.Square, yc.bufs, ysb.bufs)
        pv = nextbank()
        P.op(PE, lambda h: h.matmul(pv.t[:], lhsT=blkm.t[:], rhs=ysb.t[:, :], start=True, stop=True),
             reads=blkm.bufs + ysb.bufs, writes=pv.bufs)
        act(l1.t[:, :], pv.t[:], AF.Ln, pv.bufs, l1.bufs, bias=cst.t[:, 3:4], scale=1.0)
        act(e1.t[:, :], l1.t[:, :], AF.Exp, l1.bufs, e1.bufs, scale=-0.5)
        dve(lambda h: h.tensor_tensor(out=yc.t[:, :], in0=yc.t[:, :], in1=e1.t[:, :], op=ALU.mult), yc.bufs + e1.bufs, yc.bufs)
        act(ysb.t[:, :], yc.t[:, :], AF.Identity, yc.bufs + vec_t.bufs, ysb.bufs, scale=vcol(V_LW, j), bias=vcol(V_LB, j))
        dve(lambda h: h.tensor_tensor(out=ysb.t[:, :], in0=ysb.t[:, :], in1=bonus.t[:, :], op=ALU.add), ysb.bufs + bonus.bufs, ysb.bufs)
        dve(lambda h: h.tensor_tensor(out=catT[4 + j].t, in0=ysb.t[:, :], in1=g_t.t[:, :], op=ALU.mult), ysb.bufs + g_t.bufs, catT[4 + j].bufs)

    def out_group(g, ps2):
        pacc = [nextbank() for _ in range(4)]
        wrb = []
        for d_ in range(4):
            wrb = wrb + pacc[d_].bufs

        def half_(half):
            k = slot_i["d"] % 2
            slot = WD[k]
            slot_i["d"] += 1
            r0 = half * 8 * 128
            P.op(SP, lambda h: h.dma_start(
                out=slot.t[:, 0:8, :], in_=woutb[r0:r0 + 8 * 128, g * 512:(g + 1) * 512].rearrange("(f p) c -> p f c", p=128)),
                reads=wbuf["wout"], writes=slot.bufs, dma_key=f"wd{k}")

            def mm(h):
                ins = None
                for fl in range(8):
                    cc = half * 8 + fl
                    for d_ in range(4):
                        ins = h.matmul(pacc[d_].t[:], lhsT=slot.t[:, fl, d_ * 128:(d_ + 1) * 128], rhs=catT[cc].t,
                                       start=(cc == 0), stop=(cc == 15))
                return ins
            rd = list(slot.bufs)
            for fl in range(8):
                rd = rd + catT[half * 8 + fl].bufs
            P.op(PE, mm, reads=rd, writes=wrb)
        half_(0)
        half_(1)
        for d_ in range(4):
            evac_f(pacc[d_], g * 4 + d_, ps2)

    src, dst = "x1T", ("x2T" if stage > 2 else "outT")
    final = stage == 2
    def mk_mu(h):
        h.tensor_copy(MU2x2.t[:, 0, :], MU2.t[:])
        return h.tensor_copy(MU2x2.t[:, 1, :], MU2.t[:])
    P.op(DVE, mk_mu, reads=MU2.bufs, writes=MU2x2.bufs)
    prenorm_p1(src, 0)
    prenorm_p2(src, 0, 1)
    for ti in range(NT):
        lora_inputs()
        slot = win_load([0, 1, 2, 3])
        for g in range(4):
            pool_group(ti, slot, g)
        for j in range(NPAIR):
            if j == NPAIR - 1 and ti + 1 < NT:
                def mid(ti=ti):
                    prenorm_p1(src, ti + 1)
                    prenorm_p2(src, ti + 1, 1)
                pair(j, mid)
            else:
                pair(j)
        ps2 = PSB[5]
        for g in range(4):
            out_group(g, ps2)
        postnorm(src, dst, ti, 1, ps2, final)


def prep_shared(inp):
    f = lambda a: np.ascontiguousarray(np.asarray(a, dtype=np.float32))
    sh = {}
    sh["w_ada"] = f(inp["w_ada"][0])
    sh["b_ada"] = f(np.asarray(inp["b_ada"][0]).reshape(144, 128).T)
    sh["npre"] = f(np.asarray(inp["norm_pre"][0]).reshape(3, 16, 128).transpose(2, 0, 1).reshape(128, 48))
    sh["npost"] = f(np.asarray(inp["norm_post"][0]).reshape(3, 16, 128).transpose(2, 0, 1).reshape(128, 48))
    sh["wg1"] = f(inp["ffn1_w_gate"][0]); sh["wu1"] = f(inp["ffn1_w_up"][0]); sh["wd1"] = f(inp["ffn1_w_down"][0])
    sh["wg2"] = f(inp["ffn2_w_gate"][0]); sh["wu2"] = f(inp["ffn2_w_up"][0]); sh["wd2"] = f(inp["ffn2_w_down"][0])
    sh["win"] = f(inp["w_in"][0]); sh["wout"] = f(inp["w_out"][0])
    mu = np.zeros(43 * 128, np.float32)
    mu[512:512 + 4960] = np.asarray(inp["mu_shift"][0])
    sh["muT"] = f(mu.reshape(43, 128).T)
    sh["poolw"] = f(np.asarray(inp["pool_w"][0]).transpose(1, 0, 2).reshape(128, 512))
    sh["pscale"] = f(np.asarray(inp["pool_scale"][0]).reshape(4, 128).T)
    vs = [inp["w0"][0], inp["a0"][0], inp["k_k"][0], inp["k_a"][0], np.asarray(inp["r_k"][0]).reshape(-1), inp["lnx_w"][0], inp["lnx_b"][0]]
    sh["vecs"] = f(np.concatenate([np.asarray(v).reshape(NPAIR, 128).T for v in vs], axis=1))
    sh["wa2"] = f(np.concatenate([np.asarray(inp["w2"][0]), np.asarray(inp["a2"][0])], axis=0))
    g2 = np.asarray(inp["g2"][0])
    sh["g2a"] = f(g2[0:128]); sh["g2b"] = f(g2[128:224])
    return sh


def prep_core(inp, b):
    return {"xT": np.ascontiguousarray(np.asarray(inp["x"][b], dtype=np.float32).T),
            "cT": np.ascontiguousarray(np.asarray(inp["c"][b], dtype=np.float32).reshape(16, 128).T)}


_NEEDED = {1: ["xT", "cT", "w_ada", "b_ada", "npre", "npost", "wg1", "wu1", "wd1"]}


def kernel(**inputs):
    x = np.asarray(inputs["x"])
    B, T, _ = x.shape
    nc = build_program(T, stage=3)
    sh = prep_shared(inputs)
    in_maps = []
    for b in range(B):
        m = dict(sh)
        m.update(prep_core(inputs, b))
        in_maps.append(m)
    res = run_bass_kernel_spmd(nc, in_maps, core_ids=list(range(B)))
    out = np.stack([np.ascontiguousarray(r["outT"].T) for r in res.results], axis=0)
    return out.astype(np.float32)
```

```python
import numpy as np
from contextlib import ExitStack
import concourse.bass as bass
import concourse.mybir as mybir
from concourse.bass_utils import run_bass_kernel_spmd

F32 = mybir.dt.float32
BF16 = mybir.dt.bfloat16
AF = mybir.ActivationFunctionType
ALU = mybir.AluOpType

PE, ACT, DVE, POOL, SP = "pe", "act", "dve", "pool", "sp"
ENGINES = [PE, ACT, DVE, POOL, SP]
EPOCH = 20000

D = 2048
FF = 5632
NKC = 16
NFC = 44
TT = 512
CH = 128
INW = 5472
RW = 1536
NPAIR = 12
NORM_EPS = 1e-6
LNX_EPS = 1e-5 * 64


class Buf:
    __slots__ = ("name", "last_w", "readers")

    def __init__(self, name):
        self.name = name
        self.last_w = None
        self.readers = []


class Op:
    __slots__ = ("eng", "emit", "deps", "is_dma", "dsem", "need_sig", "sig")

    def __init__(self, eng, emit, is_dma, dsem):
        self.eng = eng
        self.emit = emit
        self.deps = []
        self.is_dma = is_dma
        self.dsem = dsem
        self.need_sig = is_dma
        self.sig = None


class Prog:
    def __init__(self, nc, stack):
        self.nc = nc
        self.stack = stack
        self.ops = {e: [] for e in ENGINES}
        self.dma_sems = {}
        self.key_bufs = {}
        self.final_waits = []

    def new_sem(self, name):
        return self.stack.enter_context(self.nc.semaphore(name))

    def op(self, eng, emit, reads=(), writes=(), dma_key=None):
        is_dma = dma_key is not None
        dsem = None
        if is_dma:
            if dma_key not in self.dma_sems:
                self.dma_sems[dma_key] = [self.new_sem("d_" + dma_key), 0]
                self.key_bufs[dma_key] = Buf("k_" + dma_key)
            dsem = self.dma_sems[dma_key]
            writes = list(writes) + [self.key_bufs[dma_key]]
        o = Op(eng, emit, is_dma, dsem)
        deps = []

        def add(p, kind):
            if p is None or p is o:
                return
            if not p.is_dma and not is_dma and p.eng == eng:
                if eng == PE or kind == "war":
                    return
            deps.append(p)

        for b in reads:
            add(b.last_w, "raw")
        for b in writes:
            add(b.last_w, "waw")
            for r in b.readers:
                add(r, "war")
        for b in reads:
            b.readers.append(o)
        for b in writes:
            b.last_w = o
            b.readers = []
        seen = set()
        for p in deps:
            if id(p) not in seen:
                seen.add(id(p))
                p.need_sig = True
                o.deps.append(p)
        self.ops[eng].append(o)
        return o

    def emit_all(self):
        nc = self.nc
        for e in ENGINES:
            cnt = 0
            sems = []
            for o in self.ops[e]:
                if o.is_dma:
                    o.dsem[1] += 16
                    o.sig = (o.dsem[0], o.dsem[1], 16)
                elif o.need_sig:
                    ep = cnt // EPOCH
                    if ep >= len(sems):
                        sems.append(self.new_sem(f"s_{e}_{ep}"))
                    o.sig = (sems[ep], cnt % EPOCH + 1, 1)
                    cnt += 1
        final_waits = self.final_waits

        def run(e, h):
            waited = {}
            for o in self.ops[e]:
                for p in o.deps:
                    sem, val, _ = p.sig
                    k = id(sem)
                    if waited.get(k, 0) < val:
                        h.wait_ge(sem, val)
                        waited[k] = val
                ins = o.emit(h)
                if o.need_sig:
                    ins.then_inc(o.sig[0], o.sig[2])
            if e == SP:
                for o in final_waits:
                    sem, val, _ = o.sig
                    if waited.get(id(sem), 0) < val:
                        h.wait_ge(sem, val)
                        waited[id(sem)] = val

        with nc.Block() as block:
            @block.tensor
            def _(h):
                run(PE, h)

            @block.scalar
            def _(h):
                run(ACT, h)

            @block.vector
            def _(h):
                run(DVE, h)

            @block.gpsimd
            def _(h):
                run(POOL, h)

            @block.sync
            def _(h):
                run(SP, h)


class Tl:
    __slots__ = ("t", "bufs")

    def __init__(self, t, bufs):
        self.t = t
        self.bufs = bufs


def build_program(T, stage=3, debug=False):
    NT = T // TT
    nc = bass.Bass("TRN2", target_bir_lowering=False)

    def din(name, shape, dt=F32):
        return nc.dram_tensor(name, list(shape), dt, kind="ExternalInput").ap()

    def dscr(name, shape, dt):
        return nc.dram_tensor(name, list(shape), dt, kind="Internal").ap()

    xT = din("xT", [D, T])
    cT = din("cT", [128, NKC])
    w_ada = din("w_ada", [D, 9 * D])
    b_ada = din("b_ada", [128, 144])
    npre = din("npre", [128, 48])
    npost = din("npost", [128, 48])
    wsrc = {}
    for nm, shp in [("wg1", [D, FF]), ("wu1", [D, FF]), ("wd1", [FF, D]), ("win", [D, INW]), ("wout", [D, D]),
                    ("wg2", [D, FF]), ("wu2", [D, FF]), ("wd2", [FF, D])]:
        wsrc[nm] = (din(nm, shp), dscr(nm + "_b", shp, BF16), shp)
    muT = din("muT", [128, 43])
    poolw = din("poolw", [128, 4 * 128])
    pscale = din("pscale", [128, 4])
    vecs = din("vecs", [128, 7 * NPAIR])
    wa2 = din("wa2", [128, RW])
    g2a = din("g2a", [128, RW])
    g2b = din("g2b", [96, RW])
    outT = nc.dram_tensor("outT", [D, T], F32, kind="ExternalOutput").ap()
    x1T = dscr("x1T", [D, T], F32)
    x2T = dscr("x2T", [D, T], F32)

    with ExitStack() as st:
        P = Prog(nc, st)

        def sb(name, shape, dt=F32):
            return st.enter_context(nc.sbuf_tensor(name, list(shape), dt))

        def tile(name, shape, dt=F32):
            return Tl(sb(name, shape, dt), [Buf(name)])

        PSB = []
        for b in range(7):
            PSB.append(Tl(st.enter_context(nc.psum_tensor(f"ps{b}", [128, 512], F32)), [Buf(f"ps{b}")]))
        PST = Tl(st.enter_context(nc.psum_tensor("pst", [128, 1024], BF16)), [Buf("pst")])
        rr = {"i": 0}
        WORK = [0, 1, 2, 3, 4, 6]

        def nextbank():
            b = WORK[rr["i"] % len(WORK)]
            rr["i"] += 1
            return PSB[b]

        NSLAB = 44
        arena = sb("arena", [128, NSLAB * 512], BF16)
        slabB = [Buf(f"slab{i}") for i in range(NSLAB)]
        arena_f32 = None

        def slab_bf(i, n=1):
            return Tl(arena[:, i * 512:(i + n) * 512], slabB[i:i + n])


        hT = tile("hT", [128, NKC, TT], BF16)
        fbuf = Tl(sb("fbuf", [128, NKC, TT], F32), [Buf(f"fbuf{i}") for i in range(NKC)])
        NXR = 3
        xr = [tile(f"xr{i}", [128, 2, TT], F32) for i in range(NXR)]
        xr_i = {"i": 0}
        sq = [tile(f"sq{i}", [128, TT], BF16) for i in range(2)]
        sq_i = {"i": 0}
        rstd = tile("rstd", [128, TT], F32)
        rstd2 = tile("rstd2", [128, TT], F32)
        sgt = [tile(f"sgt{i}", [128, TT], F32) for i in range(2)]
        sgt_i = {"i": 0}
        t32 = [tile(f"t32_{i}", [128, TT], F32) for i in range(4)]
        t32_i = {"i": 0}

        def next_t32():
            t = t32[t32_i["i"] % len(t32)]
            t32_i["i"] += 1
            return t

        WGUF = [tile(f"wguslot{i}", [128, 2 * NKC * 256], BF16) for i in range(2)]
        WGU = [Tl(w.t[:, :].rearrange("p (a k c) -> p a k c", a=2, k=NKC), w.bufs) for w in WGUF]
        WGU4 = [Tl(w.t[:, :].rearrange("p (a k c) -> p a k c", a=4, k=NKC), w.bufs) for w in WGUF]
        WD = [tile(f"wdslot{i}", [128, 11, 512], BF16) for i in range(2)]

        ones_mean = tile("ones_mean", [128, 128], BF16)
        blk1 = tile("blk1", [128, 128], BF16)
        blkm = tile("blkm", [128, 128], F32)
        ident = tile("ident", [128, 128], BF16)
        MU2 = tile("MU2", [128, 256], BF16)
        ML = tile("ML", [128, 128], BF16)
        scanmask = tile("scanmask", [128, TT], F32)
        mod = tile("mod", [128, 144], F32)
        bada = tile("bada", [128, 144], F32)
        npre_t = tile("npre_t", [128, 48], F32)
        npost_t = tile("npost_t", [128, 48], F32)
        gA = tile("gA", [128, 48], F32)
        gB = tile("gB", [128, 48], F32)
        c_t = tile("c_t", [128, NKC], F32)
        cst = tile("cst", [128, 8], F32)
        sc_bf = tile("sc_bf", [128, NKC], BF16)

        def cv(h):
            h.memset(ones_mean.t[:], 1.0 / D)
            h.memset(blk1.t[:], 0.0)
            h.memset(blk1.t[0:64, 0:64], 1.0)
            h.memset(blk1.t[64:128, 64:128], 1.0)
            h.memset(blkm.t[:], 0.0)
            h.memset(blkm.t[0:64, 0:64], 1.0 / 64)
            h.memset(blkm.t[64:128, 64:128], 1.0 / 64)
            h.memset(scanmask.t[:], 1.0)
            for i_, v_ in enumerate([1.0, -0.5, 1e-18, LNX_EPS, NORM_EPS]):
                h.memset(cst.t[:, i_:i_ + 1], float(v_))
            ins = None
            for c in range(TT // CH):
                ins = h.memset(scanmask.t[:, c * CH:c * CH + 1], 0.0)
            return ins
        P.op(DVE, cv, writes=ones_mean.bufs + blk1.bufs + blkm.bufs + scanmask.bufs + cst.bufs)

        def cp(h):
            h.memset(ident.t[:], 1.0)
            h.affine_select(out=ident.t[:], in_=ident.t[:], pattern=[[1, 128]], compare_op=ALU.is_equal,
                            fill=0.0, base=0, channel_multiplier=-1)
            h.memset(MU2.t[:], 1.0)
            h.affine_select(out=MU2.t[:, 0:128], in_=MU2.t[:, 0:128], pattern=[[1, 128]], compare_op=ALU.is_gt,
                            fill=0.0, base=0, channel_multiplier=-1)
            h.affine_select(out=MU2.t[:, 128:256], in_=MU2.t[:, 128:256], pattern=[[1, 128]], compare_op=ALU.is_ge,
                            fill=0.0, base=0, channel_multiplier=-1)
            h.memset(ML.t[:], 1.0)
            return h.affine_select(out=ML.t[:], in_=ML.t[:], pattern=[[-1, 128]], compare_op=ALU.is_gt,
                                   fill=0.0, base=0, channel_multiplier=1)
        P.op(POOL, cp, writes=ident.bufs + MU2.bufs + ML.bufs)

        wbuf = {}
        conv_i = {"i": 0}

        def convert(nm):
            src, dst, shp = wsrc[nm]
            rows = shp[0]
            blk = 256 if shp[1] > 2048 else 512
            bufs = []
            for r0 in range(0, rows, blk):
                b = Buf(f"{nm}_{r0}")
                bufs.append(b)
                key = f"cv{conv_i['i'] % 16}"
                conv_i["i"] += 1
                P.op(POOL, (lambda h, r0=r0, blk=blk: h.dma_start(out=dst[r0:r0 + blk, :], in_=src[r0:r0 + blk, :])),
                     writes=[b], dma_key=key)
            wbuf[nm] = bufs

        def load_small(dst_tile, src_ap, eng=SP):
            P.op(eng, lambda h: h.dma_start(out=dst_tile.t[:], in_=src_ap), writes=dst_tile.bufs, dma_key="par")

        load_small(c_t, cT[:, :])
        load_small(bada, b_ada[:, :])
        load_small(npre_t, npre[:, :])
        load_small(npost_t, npost[:, :])

        convert("wg1")
        convert("wu1")

        P.op(ACT, lambda h: h.activation(out=sc_bf.t[:], in_=c_t.t[:], func=AF.Silu), reads=c_t.bufs, writes=sc_bf.bufs)
        ADA_F = [Tl(fbuf.t[:, i * 8:(i + 1) * 8, :].rearrange("p a t -> p (a t)").rearrange("p (k n) -> p k n", k=NKC),
                    fbuf.bufs[i * 8:(i + 1) * 8]) for i in range(2)]
        ADA_B = [Tl(arena[:, i * 8 * 512:(i + 1) * 8 * 512].rearrange("p (k n) -> p k n", k=NKC),
                    slabB[i * 8:(i + 1) * 8]) for i in range(4)]
        ps_mod = PSB[5]
        for sl in range(72):
            stg = ADA_F[sl % 2]
            slot = ADA_B[sl % 4]
            n0 = sl * 256
            P.op(SP, (lambda h, stg=stg, n0=n0: h.dma_start(
                out=stg.t, in_=w_ada[:, n0:n0 + 256].rearrange("(k p) n -> p k n", p=128))),
                writes=stg.bufs, dma_key=f"ada{sl % 2}")
            if sl % 2 == 0:
                P.op(DVE, (lambda h, stg=stg, slot=slot: h.tensor_copy(slot.t, stg.t)), reads=stg.bufs, writes=slot.bufs)
            else:
                P.op(ACT, (lambda h, stg=stg, slot=slot: h.activation(out=slot.t, in_=stg.t, func=AF.Copy)),
                     reads=stg.bufs, writes=slot.bufs)

            def mm(h, slot=slot, sl=sl):
                ins = None
                for jj in range(2):
                    j = sl * 2 + jj
                    for kc in range(NKC):
                        ins = h.matmul(ps_mod.t[:, j:j + 1], lhsT=slot.t[:, kc, jj * 128:(jj + 1) * 128],
                                       rhs=sc_bf.t[:, kc:kc + 1], start=(kc == 0), stop=(kc == NKC - 1))
                return ins
            P.op(PE, mm, reads=slot.bufs + sc_bf.bufs, writes=ps_mod.bufs)
        P.op(DVE, lambda h: h.tensor_tensor(out=mod.t[:], in0=ps_mod.t[:, 0:144], in1=bada.t[:], op=ALU.add),
             reads=ps_mod.bufs + bada.bufs, writes=mod.bufs)

        def modcol(s, m):
            return mod.t[:, (s * 3 + m) * 16:(s * 3 + m) * 16 + 16]

        def mkvec(h):
            ins = None
            for s in range(3):
                wt = 1.0 if s == 1 else 0.5
                h.scalar_tensor_tensor(out=gA.t[:, s * 16:(s + 1) * 16], in0=modcol(s, 1), scalar=1.0,
                                       in1=npre_t.t[:, s * 16:(s + 1) * 16], op0=ALU.add, op1=ALU.mult)
                ins = h.scalar_tensor_tensor(out=gB.t[:, s * 16:(s + 1) * 16], in0=modcol(s, 2), scalar=1.0,
                                             in1=npost_t.t[:, s * 16:(s + 1) * 16], op0=ALU.add, op1=ALU.mult)
            return ins
        P.op(DVE, mkvec, reads=mod.bufs + npre_t.bufs + npost_t.bufs, writes=gA.bufs + gB.bufs)

        def mkvec2(h):
            h.tensor_scalar(out=gB.t[:, 0:16], in0=gB.t[:, 0:16], scalar1=0.5, scalar2=None, op0=ALU.mult)
            return h.tensor_scalar(out=gB.t[:, 32:48], in0=gB.t[:, 32:48], scalar1=0.5, scalar2=None, op0=ALU.mult)
        P.op(DVE, mkvec2, reads=gB.bufs, writes=gB.bufs)

        convert("wd1")
        if stage >= 2:
            convert("win")
            convert("wout")
        if stage >= 3:
            convert("wg2")
            convert("wu2")
            convert("wd2")

        xbufs = {"xT": [Buf(f"xT{i}") for i in range(NT)], "x1T": [Buf(f"x1T{i}") for i in range(NT)],
                 "x2T": [Buf(f"x2T{i}") for i in range(NT)], "outT": [Buf(f"outT{i}") for i in range(NT)]}
        xaps = {"xT": xT, "x1T": x1T, "x2T": x2T, "outT": outT}

        def next_xr():
            t = xr[xr_i["i"] % NXR]
            k = xr_i["i"] % NXR
            xr_i["i"] += 1
            return t, k

        def load_x(src, ti, kc0):
            t, k = next_xr()
            ap = xaps[src][kc0 * 128:(kc0 + 2) * 128, ti * TT:(ti + 1) * TT].rearrange("(k p) t -> p k t", p=128)
            P.op(SP, lambda h: h.dma_start(out=t.t[:], in_=ap), reads=[xbufs[src][ti]], writes=t.bufs, dma_key=f"xr{k}")
            return t, k

        def rsqrt_from_psum(ps, out_t, eps):
            P.op(ACT, lambda h: h.activation(out=out_t.t[:], in_=ps.t[:], func=AF.Ln, bias=cst.t[:, 4:5], scale=1.0),
                 reads=ps.bufs + cst.bufs, writes=out_t.bufs)
            P.op(ACT, lambda h: h.activation(out=out_t.t[:], in_=out_t.t[:], func=AF.Exp, scale=-0.5),
                 reads=out_t.bufs, writes=out_t.bufs)

        def prenorm_p1_gen(src, ti):
            ps = PSB[5]
            nxt_ld = load_x(src, ti, 0)
            yield
            for kc0 in range(0, NKC, 2):
                t, _ = nxt_ld
                if kc0 + 2 < NKC:
                    nxt_ld = load_x(src, ti, kc0 + 2)
                for q in range(2):
                    kc = kc0 + q
                    s_ = sq[sq_i["i"] % 2]
                    sq_i["i"] += 1
                    P.op(ACT, (lambda h, t=t, q=q, s_=s_: h.activation(out=s_.t[:], in_=t.t[:, q, :], func=AF.Square)),
                         reads=t.bufs, writes=s_.bufs)
                    P.op(PE, (lambda h, s_=s_, kc=kc: h.matmul(ps.t[:], lhsT=ones_mean.t[:], rhs=s_.t[:],
                                                                start=(kc == 0), stop=(kc == NKC - 1))),
                         reads=s_.bufs + ones_mean.bufs, writes=ps.bufs)
                if kc0 == NKC - 2:
                    rsqrt_from_psum(ps, rstd, NORM_EPS)
                yield

        def prenorm_p2_gen(src, ti, s):
            nxt_ld = load_x(src, ti, 0)
            for kc0 in range(0, NKC, 2):
                t, _ = nxt_ld
                if kc0 + 2 < NKC:
                    nxt_ld = load_x(src, ti, kc0 + 2)
                for q in range(2):
                    kc = kc0 + q
                    tmp = next_t32()
                    P.op(DVE, (lambda h, t=t, q=q, tmp=tmp, kc=kc: h.scalar_tensor_tensor(
                        out=tmp.t[:], in0=t.t[:, q, :], scalar=gA.t[:, s * 16 + kc:s * 16 + kc + 1], in1=rstd.t[:],
                        op0=ALU.mult, op1=ALU.mult)), reads=t.bufs + gA.bufs + rstd.bufs, writes=tmp.bufs)
                    P.op(ACT, (lambda h, tmp=tmp, kc=kc: h.activation(
                        out=hT.t[:, kc, :], in_=tmp.t[:], func=AF.Identity,
                        bias=mod.t[:, (s * 3) * 16 + kc:(s * 3) * 16 + kc + 1], scale=1.0)),
                        reads=tmp.bufs + mod.bufs, writes=hT.bufs)
                yield

        def prenorm_p1(src, ti):
            for _ in prenorm_p1_gen(src, ti):
                pass

        def prenorm_p2(src, ti, s):
            for _ in prenorm_p2_gen(src, ti, s):
                pass

        def evac_f(ps, dc, ps2):
            P.op(ACT, lambda h: h.activation(out=fbuf.t[:, dc, :], in_=ps.t[:], func=AF.Copy),
                 reads=ps.bufs, writes=[fbuf.bufs[dc]])
            s_ = sq[sq_i["i"] % 2]
            sq_i["i"] += 1
            P.op(ACT, lambda h: h.activation(out=s_.t[:], in_=ps.t[:], func=AF.Square), reads=ps.bufs, writes=s_.bufs)
            P.op(PE, lambda h: h.matmul(ps2.t[:], lhsT=ones_mean.t[:], rhs=s_.t[:], start=(dc == 0), stop=(dc == NKC - 1)),
                 reads=s_.bufs + ones_mean.bufs, writes=ps2.bufs)

        def postnorm_gen(src, dst, ti, s, ps2, final):
            rsqrt_from_psum(ps2, rstd2, NORM_EPS)
            nxt_ld = load_x(src, ti, 0)
            yield
            for kc0 in range(0, NKC, 2):
                t, k = nxt_ld
                if kc0 + 2 < NKC:
                    nxt_ld = load_x(src, ti, kc0 + 2)
                for q in range(2):
                    kc = kc0 + q
                    tmp = next_t32()
                    P.op(DVE, (lambda h, tmp=tmp, kc=kc: h.scalar_tensor_tensor(
                        out=tmp.t[:], in0=fbuf.t[:, kc, :], scalar=gB.t[:, s * 16 + kc:s * 16 + kc + 1], in1=rstd2.t[:],
                        op0=ALU.mult, op1=ALU.mult)), reads=[fbuf.bufs[kc]] + gB.bufs + rstd2.bufs, writes=tmp.bufs)
                    P.op(DVE, (lambda h, t=t, q=q, tmp=tmp: h.tensor_tensor(out=t.t[:, q, :], in0=t.t[:, q, :], in1=tmp.t[:],
                                                                            op=ALU.add)),
                         reads=tmp.bufs + t.bufs, writes=t.bufs)
                ap = xaps[dst][kc0 * 128:(kc0 + 2) * 128, ti * TT:(ti + 1) * TT].rearrange("(k p) t -> p k t", p=128)
                o = P.op(SP, (lambda h, t=t, ap=ap: h.dma_start(out=ap, in_=t.t[:])), reads=t.bufs,
                         writes=[xbufs[dst][ti]], dma_key=f"st{k}")
                if final:
                    P.final_waits.append(o)
                yield

        def postnorm(src, dst, ti, s, ps2, final):
            for _ in postnorm_gen(src, dst, ti, s, ps2, final):
                pass

        def ffn_phase(s, src, dst, wg, wu, wd, final):
            wgb, wub, wdb = wsrc[wg][1], wsrc[wu][1], wsrc[wd][1]
            UT = [slab_bf(i) for i in range(NFC)]
            NGU = NT * (NFC // 2)
            NWD = NT * 16

            def gu_load(idx):
                if idx >= NGU:
                    return
                fp = idx % (NFC // 2)
                k = idx % 2
                slot = WGU[k]
                c0 = fp * 256
                P.op(SP, lambda h: h.dma_start(out=slot.t[:, 0, :, :], in_=wgb[:, c0:c0 + 256].rearrange("(k p) c -> p k c", p=128)),
                     reads=wbuf[wg], writes=slot.bufs, dma_key=f"wg{k}")
                P.op(SP, lambda h: h.dma_start(out=slot.t[:, 1, :, :], in_=wub[:, c0:c0 + 256].rearrange("(k p) c -> p k c", p=128)),
                     reads=wbuf[wu], writes=slot.bufs, dma_key=f"wu{k}")

            def wd_load(idx):
                if idx >= NWD:
                    return
                g = (idx % 16) // 4
                qq = idx % 4
                k = idx % 2
                slot = WD[k]
                r0 = qq * 11 * 128
                P.op(SP, lambda h: h.dma_start(
                    out=slot.t[:], in_=wdb[r0:r0 + 11 * 128, g * 512:(g + 1) * 512].rearrange("(f p) c -> p f c", p=128)),
                    reads=wbuf[wd], writes=slot.bufs, dma_key=f"wd{k}")

            def gu_step(idx):
                fp = idx % (NFC // 2)
                slot = WGU[idx % 2]
                for q in range(2):
                    gu_chunk(slot, fp * 2 + q, q)
                gu_load(idx + 2)

            def gu_chunk(slot, fc, q):
                pg = PSB[fc % 2]
                pu = PSB[2 + fc % 2]

                def mm(h):
                    ins = None
                    for kc in range(NKC):
                        h.matmul(pg.t[:], lhsT=slot.t[:, 0, kc, q * 128:(q + 1) * 128], rhs=hT.t[:, kc, :],
                                 start=(kc == 0), stop=(kc == NKC - 1))
                    for kc in range(NKC):
                        ins = h.matmul(pu.t[:], lhsT=slot.t[:, 1, kc, q * 128:(q + 1) * 128], rhs=hT.t[:, kc, :],
                                       start=(kc == 0), stop=(kc == NKC - 1))
                    return ins
                P.op(PE, mm, reads=slot.bufs + hT.bufs, writes=pg.bufs + pu.bufs)
                sg = sgt[sgt_i["i"] % 2]
                sgt_i["i"] += 1
                P.op(ACT, lambda h: h.activation(out=sg.t[:], in_=pg.t[:], func=AF.Silu), reads=pg.bufs, writes=sg.bufs)
                u = UT[fc]
                P.op(DVE, lambda h: h.tensor_tensor(out=u.t, in0=sg.t[:], in1=pu.t[:], op=ALU.mult),
                     reads=sg.bufs + pu.bufs, writes=u.bufs)

            def wd_step(idx):
                qq = idx % 4
                slot = WD[idx % 2]

                def mm(h):
                    ins = None
                    for fl in range(11):
                        fc = qq * 11 + fl
                        for d_ in range(4):
                            ins = h.matmul(PSB[d_].t[:], lhsT=slot.t[:, fl, d_ * 128:(d_ + 1) * 128], rhs=UT[fc].t,
                                           start=(fc == 0), stop=(fc == NFC - 1))
                    return ins
                rd = list(slot.bufs)
                for fl in range(11):
                    rd = rd + UT[qq * 11 + fl].bufs
                P.op(PE, mm, reads=rd, writes=PSB[0].bufs + PSB[1].bufs + PSB[2].bufs + PSB[3].bufs)
                wd_load(idx + 2)

            prenorm_p1(src, 0)
            prenorm_p2(src, 0, s)
            gu_load(0)
            gu_load(1)
            pend = []

            def tick():
                while pend:
                    try:
                        next(pend[0])
                        return
                    except StopIteration:
                        pend.pop(0)

            def drain():
                while pend:
                    tick()

            for ti in range(NT):
                for fp in range(NFC // 2):
                    gu_step(ti * (NFC // 2) + fp)
                    tick()
                    if ti == 0 and fp == 11:
                        wd_load(0)
                        wd_load(1)
                drain()
                if ti + 1 < NT:
                    pend.append(prenorm_p1_gen(src, ti + 1))
                    pend.append(prenorm_p2_gen(src, ti + 1, s))
                ps2 = PSB[4]
                for g in range(4):
                    for qq in range(4):
                        wd_step(ti * 16 + g * 4 + qq)
                        tick()
                    for d_ in range(4):
                        evac_f(PSB[d_], g * 4 + d_, ps2)
                drain()
                pn = postnorm_gen(src, dst, ti, s, ps2, final)
                next(pn)
                pend.append(pn)
            drain()

        ffn_phase(0, "xT", "x1T" if stage > 1 else "outT", "wg1", "wu1", "wd1", final=(stage == 1))

        if stage >= 2:
            mixer_phase(nc, P, locals())
        if stage >= 3:
            ffn_phase(2, "x2T", "outT", "wg2", "wu2", "wd2", final=True)

        P.emit_all()
    return nc


def mixer_phase(nc, P, E):
    sb = E["sb"]; tile = E["tile"]; PSB = E["PSB"]; PST = E["PST"]; nextbank = E["nextbank"]
    hT = E["hT"]; fbuf = E["fbuf"]; WGU4 = E["WGU4"]; WD = E["WD"]; wsrc = E["wsrc"]; wbuf = E["wbuf"]
    NT = E["NT"]; stage = E["stage"]; arena = E["arena"]; slabB = E["slabB"]; t32 = E["t32"]
    blk1 = E["blk1"]; blkm = E["blkm"]; ident = E["ident"]; MU2 = E["MU2"]; ML = E["ML"]; scanmask = E["scanmask"]
    prenorm_p1 = E["prenorm_p1"]; prenorm_p2 = E["prenorm_p2"]; postnorm = E["postnorm"]; evac_f = E["evac_f"]
    cst = E["cst"]; muT = E["muT"]; poolw = E["poolw"]; pscale = E["pscale"]; vecs = E["vecs"]; wa2 = E["wa2"]; g2a = E["g2a"]; g2b = E["g2b"]
    winb = wsrc["win"][1]; woutb = wsrc["wout"][1]

    def ft(i):
        return Tl(fbuf.t[:, i, :], [fbuf.bufs[i]])
    (I_RS, I_KS, I_VS, I_E1, I_L1, I_NL, I_A, I_G, I_KK, I_K2, I_BP, I_BONUS, I_CWN, I_EA, I_ER, I_EK) = range(16)
    F = [ft(i) for i in range(16)]
    I_CWX, I_Y, I_YC = I_NL, I_EA, I_ER

    def slab(i, n=1):
        return Tl(arena[:, i * 512:(i + n) * 512], slabB[i:i + n])
    catT = [slab(i) for i in range(16)]
    lora_in = slab(16)
    sg1 = slab(17)
    sg2 = slab(18)
    AR = Tl(arena[:, 19 * 512:21 * 512].rearrange("p (c t i) -> p c t i", c=4, t=2), slabB[19:21])
    Kt = slab(21)
    Bt = slab(22)
    Vt = slab(23)
    sqb = slab(24)
    pooled = slab(25)
    tok3 = [Tl(arena[:, (26 + i) * 512:(26 + i) * 512 + 384], [slabB[26 + i]]) for i in range(2)]

    def mat(sl, q, nm):
        return Tl(arena[:, sl * 512 + q * 128: sl * 512 + (q + 1) * 128], [Buf(nm)])
    G1M = [Tl(arena[:, 28 * 512 + hh * 256:28 * 512 + (hh + 1) * 256], [Buf(f"G1M{hh}")]) for hh in range(2)]
    G2M = [Tl(arena[:, 29 * 512 + hh * 256:29 * 512 + (hh + 1) * 256], [Buf(f"G2M{hh}")]) for hh in range(2)]
    PP = [[mat(30 + hh, pp, f"P{hh}{pp}") for pp in range(2)] for hh in range(2)]
    PPT = [[mat(30 + hh, 2 + pp, f"PT{hh}{pp}") for pp in range(2)] for hh in range(2)]
    TTm = [[mat(32, hh * 2 + pp, f"TT{hh}{pp}") for pp in range(2)] for hh in range(2)]
    Xbf = mat(33, 0, "Xbf")
    Ubf = mat(33, 1, "Ubf")
    Hz = [Tl(arena[:, 33 * 512 + 256 + hh * 64: 33 * 512 + 256 + (hh + 1) * 64], [Buf(f"Hz{hh}")]) for hh in range(2)]
    poolw_bf = Tl(arena[:, 34 * 512:35 * 512].rearrange("p (g d) -> p g d", g=4), [slabB[34]])
    wa2_bf = Tl(arena[:, 35 * 512:38 * 512], slabB[35:38])
    g2a_bf = Tl(arena[:, 38 * 512:41 * 512], slabB[38:41])
    g2b_bf = Tl(arena[:, 41 * 512:44 * 512], slabB[41:44])

    Hst = [tile(f"H{j}", [128, 64], F32) for j in range(NPAIR)]
    H0p = tile("H0p", [128, 64], F32)
    Ee = tile("Ee", [128, 16 + TT], F32)
    pcar = [tile(f"pcar{g}", [128, 16], F32) for g in range(4)]
    wtmp = [tile(f"wtmp{i}", [128, 16 + TT], F32) for i in range(2)]
    carry = tile("carry", [128, 43], F32)
    mu_t = tile("mu_t", [128, 43], F32)
    omu_t = tile("omu_t", [128, 43], F32)
    vec_t = tile("vec_t", [128, 7 * NPAIR], F32)
    dvec = tile("dvec", [128, 3 * NPAIR], F32)
    pscale_t = tile("pscale_t", [128, 4], F32)
    invc = tile("invc", [128, 16], F32)
    ncw = tile("ncw", [128, 4], F32)
    wc = tile("wc", [128, 4], F32)
    stage32 = Tl(fbuf.t[:, 0:3, :].rearrange("p a t -> p (a t)"), fbuf.bufs[0:3])

    V_W0, V_A0, V_KK, V_KA, V_RK, V_LW, V_LB = range(7)

    def vcol(v, j):
        return vec_t.t[:, v * NPAIR + j:v * NPAIR + j + 1]

    def act(out_ap, in_ap, func, reads, writes, **kw):
        P.op(ACT, lambda h: h.activation(out=out_ap, in_=in_ap, func=func, **kw), reads=reads, writes=writes)

    def dve(fn, reads, writes):
        P.op(DVE, fn, reads=reads, writes=writes)

    def pool(fn, reads, writes):
        P.op(POOL, fn, reads=reads, writes=writes)

    def ld(dst_ap, src_ap, bufs):
        P.op(SP, lambda h: h.dma_start(out=dst_ap, in_=src_ap), writes=bufs, dma_key="par")
    ld(mu_t.t[:], muT[:, :], mu_t.bufs)
    ld(vec_t.t[:], vecs[:, :], vec_t.bufs)
    ld(pscale_t.t[:], pscale[:, :], pscale_t.bufs)

    def setup_v(h):
        h.tensor_scalar(out=omu_t.t[:], in0=mu_t.t[:], scalar1=-1.0, scalar2=1.0, op0=ALU.mult, op1=ALU.add)
        h.tensor_scalar(out=dvec.t[:, 0:2 * NPAIR], in0=vec_t.t[:, 0:2 * NPAIR], scalar1=-1.0, scalar2=None, op0=ALU.mult)
        h.tensor_scalar(out=dvec.t[:, 2 * NPAIR:3 * NPAIR], in0=vec_t.t[:, V_KA * NPAIR:(V_KA + 1) * NPAIR],
                        scalar1=-1.0, scalar2=1.0, op0=ALU.mult, op1=ALU.add)
        h.memset(carry.t[:], 0.0)
        for g in range(4):
            h.memset(pcar[g].t[:], 0.0)
        for j in range(NPAIR):
            h.memset(Hst[j].t[:], 0.0)
        h.memset(Hz[0].t, 0.0)
        h.memset(Hz[1].t, 0.0)
        ins = None
        for t_ in range(16):
            ins = h.memset(invc.t[:, t_:t_ + 1], 1.0 / (t_ + 1))
        return ins
    wr = omu_t.bufs + dvec.bufs + carry.bufs + invc.bufs + Hz[0].bufs + Hz[1].bufs
    for g in range(4):
        wr = wr + pcar[g].bufs
    for j in range(NPAIR):
        wr = wr + Hst[j].bufs
    P.op(DVE, setup_v, reads=mu_t.bufs + vec_t.bufs, writes=wr)

    def ld_cast(dst_tl, dst_ap, src_ap, rows, width):
        P.op(SP, lambda h: h.dma_start(out=stage32.t[0:rows, 0:width], in_=src_ap), writes=stage32.bufs, dma_key="par")
        P.op(DVE, lambda h: h.tensor_copy(dst_ap, stage32.t[0:rows, 0:width]), reads=stage32.bufs, writes=dst_tl.bufs)
    ld_cast(poolw_bf, arena[:, 34 * 512:35 * 512], poolw[:, :], 128, 512)
    ld_cast(wa2_bf, wa2_bf.t[:, :], wa2[:, :], 128, RW)
    ld_cast(g2a_bf, g2a_bf.t[:, :], g2a[:, :], 128, RW)
    ld_cast(g2b_bf, g2b_bf.t[0:96, :], g2b[:, :], 96, RW)

    slot_i = {"w": 0, "d": 0}
    tab = {"i": 0}

    def win_load(chunks):
        k = slot_i["w"] % 2
        slot = WGU4[k]
        slot_i["w"] += 1
        for i, c in enumerate(chunks):
            wdt = min(128, INW - c * 128)
            dst_ap = slot.t[:, i, :, 0:wdt]
            src_ap = winb[:, c * 128:c * 128 + wdt].rearrange("(k p) c -> p k c", p=128)
            P.op(SP, (lambda h, dst_ap=dst_ap, src_ap=src_ap: h.dma_start(out=dst_ap, in_=src_ap)),
                 reads=wbuf["win"], writes=slot.bufs, dma_key=(f"wg{k}" if i % 2 == 0 else f"wu{k}"))
        return slot

    def proj(slot, i, rows):
        ps = nextbank()

        def mm(h):
            ins = None
            for kc in range(NKC):
                ins = h.matmul(ps.t[0:rows, :], lhsT=slot.t[:, i, kc, 0:rows], rhs=hT.t[:, kc, :], start=(kc == 0), stop=(kc == NKC - 1))
            return ins
        P.op(PE, mm, reads=slot.bufs + hT.bufs, writes=ps.bufs)
        return ps

    def tshift(ps, c, out, rows=128):
        a_ = t32[2 * (tab["i"] % 2)]
        b_ = t32[2 * (tab["i"] % 2) + 1]
        tab["i"] += 1
        act(a_.t[0:rows, :], ps.t[0:rows, :], AF.Identity, ps.bufs + omu_t.bufs, a_.bufs, scale=omu_t.t[0:rows, c:c + 1])
        act(b_.t[0:rows, :], ps.t[0:rows, :], AF.Identity, ps.bufs + mu_t.bufs, b_.bufs, scale=mu_t.t[0:rows, c:c + 1])

        def f(h):
            h.tensor_tensor(out=out.t[0:rows, 1:TT], in0=a_.t[0:rows, 1:TT], in1=b_.t[0:rows, 0:TT - 1], op=ALU.add)
            return h.tensor_tensor(out=out.t[0:rows, 0:1], in0=a_.t[0:rows, 0:1], in1=carry.t[0:rows, c:c + 1], op=ALU.add)
        pool(f, a_.bufs + b_.bufs + carry.bufs, out.bufs)
        pool(lambda h: h.tensor_copy(carry.t[0:rows, c:c + 1], b_.t[0:rows, TT - 1:TT]), b_.bufs, carry.bufs)

    def sigmoid_to(out_ap, out_bufs, in_t, rows):
        e1, l1 = F[I_E1], F[I_L1]
        act(e1.t[0:rows, :], in_t.t[0:rows, :], AF.Exp, in_t.bufs, e1.bufs, scale=-1.0)
        act(l1.t[0:rows, :], e1.t[0:rows, :], AF.Ln, e1.bufs, l1.bufs, bias=cst.t[0:rows, 0:1], scale=1.0)
        act(out_ap, l1.t[0:rows, :], AF.Exp, l1.bufs, out_bufs, scale=-1.0)

    def v3(t_):
        return t_.t.rearrange("p (c i) -> p c i", c=4)

    def lora_inputs():
        sx40, sx41, sx42 = F[I_RS], F[I_KS], F[I_VS]
        slot = win_load([40, 41, 42])
        ps = proj(slot, 0, 128)
        tshift(ps, 40, sx40)
        ps = proj(slot, 1, 128)
        tshift(ps, 41, sx41)
        ps = proj(slot, 2, 96)
        tshift(ps, 42, sx42, rows=96)
        e1, l1 = F[I_E1], F[I_L1]
        act(e1.t[0:64, :], sx40.t[0:64, :], AF.Exp, sx40.bufs, e1.bufs, scale=-2.0)
        act(l1.t[0:64, :], e1.t[0:64, :], AF.Ln, e1.bufs, l1.bufs, bias=cst.t[0:64, 0:1], scale=1.0)
        act(e1.t[0:64, :], l1.t[0:64, :], AF.Exp, l1.bufs, e1.bufs, scale=-1.0)
        dve(lambda h: h.tensor_scalar(out=lora_in.t[0:64, :], in0=e1.t[0:64, :], scalar1=2.0, scalar2=-1.0, op0=ALU.mult, op1=ALU.add),
            e1.bufs, lora_in.bufs)
        act(lora_in.t[64:128, :], sx40.t[64:128, :], AF.Copy, sx40.bufs, lora_in.bufs)
        sigmoid_to(sg1.t[:, :], sg1.bufs, sx41, 128)
        sigmoid_to(sg2.t[0:96, :], sg2.bufs, sx42, 96)

    def pool_group(ti, slot, g):
        win = 2 << g
        L = g + 1
        ps = proj(slot, g, 128)
        pool(lambda h: h.tensor_copy(Ee.t[:, 0:16], pcar[g].t[:]), pcar[g].bufs, Ee.bufs)
        act(Ee.t[:, 16:16 + TT], ps.t[:], AF.Copy, ps.bufs, Ee.bufs)
        pool(lambda h: h.tensor_copy(pcar[g].t[:], Ee.t[:, TT:TT + 16]), Ee.bufs, pcar[g].bufs)
        cur, cur_lo = Ee, 0
        for k in range(1, L + 1):
            lo = 16 - win + (1 << k)
            n = 16 + TT - lo
            nxt = wtmp[k % 2]
            a0 = lo - cur_lo
            b0 = lo - (1 << (k - 1)) - cur_lo
            assert b0 >= 0

            def f(h, nxt=nxt, cur=cur, a0=a0, b0=b0, n=n):
                return h.tensor_tensor(out=nxt.t[:, 0:n], in0=cur.t[:, a0:a0 + n], in1=cur.t[:, b0:b0 + n], op=ALU.add)
            pool(f, cur.bufs, nxt.bufs)
            cur, cur_lo = nxt, lo
        assert cur_lo == 16
        wl = cur
        dve(lambda h: h.scalar_tensor_tensor(out=pooled.t[:, :], in0=wl.t[:, 0:TT], scalar=1.0 / win, in1=Ee.t[:, 16:16 + TT],
                                             op0=ALU.mult, op1=ALU.subtract), wl.bufs + Ee.bufs, pooled.bufs)
        if ti == 0:
            tmpc = t32[0]
            nfix = win - 1
            dve(lambda h: h.tensor_tensor(out=tmpc.t[:, 0:nfix], in0=wl.t[:, 0:nfix], in1=invc.t[:, 0:nfix], op=ALU.mult),
                wl.bufs + invc.bufs, tmpc.bufs)
            dve(lambda h: h.tensor_tensor(out=pooled.t[:, 0:nfix], in0=tmpc.t[:, 0:nfix], in1=Ee.t[:, 16:16 + nfix], op=ALU.subtract),
                tmpc.bufs + Ee.bufs, pooled.bufs)
        psm = nextbank()
        P.op(PE, lambda h: h.matmul(psm.t[:], lhsT=poolw_bf.t[:, g, :], rhs=pooled.t[:, :], start=True, stop=True),
             reads=poolw_bf.bufs + pooled.bufs, writes=psm.bufs)
        act(catT[g].t, psm.t[:], AF.Identity, psm.bufs + pscale_t.bufs, catT[g].bufs, scale=pscale_t.t[:, g:g + 1])

    chn = sb("chn", [128, 8 * 256 + 8 * 128 * 2 + 4 * 384], BF16)
    G1A = arena[:, 26 * 512:30 * 512].rearrange("p (q n) -> p q n", q=8)
    PA = arena[:, 30 * 512:32 * 512].rearrange("p (q n) -> p q n", q=8)
    o = 0
    G2A = chn[:, o:o + 8 * 256].rearrange("p (q n) -> p q n", q=8); o += 8 * 256
    PTA = chn[:, o:o + 8 * 128].rearrange("p (q n) -> p q n", q=8); o += 8 * 128
    TTA = chn[:, o:o + 8 * 128].rearrange("p (q n) -> p q n", q=8); o += 8 * 128
    TOK = chn[:, o:o + 4 * 384].rearrange("p (c n) -> p c n", c=4); o += 4 * 384
    G1B = [slabB[26 + 2 * h] for h in range(2)]
    G1Bx = [[slabB[26 + 2 * h], slabB[27 + 2 * h]] for h in range(2)]
    G2B = [Buf(f"G2h{h}") for h in range(2)]
    PB = [slabB[30 + h] for h in range(2)]
    PTB = [Buf(f"PTh{h}") for h in range(2)]
    TTB = [Buf(f"TTh{h}") for h in range(2)]
    TOKB = [Buf(f"tok{i}") for i in range(2)]
    MU2x2 = Tl(arena[:, 32 * 512:33 * 512].rearrange("p (r n) -> p r n", r=2), [slabB[32]])
    MLx4 = tile("MLx4", [128, 4, 128], BF16)
    IDx4 = tile("IDx4", [128, 4, 128], BF16)

    def mkc(h):
        ins = None
        for r in range(2):
            pass
        for r in range(4):
            h.tensor_copy(MLx4.t[:, r, :], ML.t[:])
            ins = h.tensor_copy(IDx4.t[:, r, :], ident.t[:])
        return ins
    P.op(DVE, mkc, reads=MU2.bufs + ML.bufs + ident.bufs, writes=MLx4.bufs + IDx4.bufs)

    def chains(j):
        for half in range(2):
            def trp(h, half=half):
                ins = None
                for cc in range(2):
                    c = half * 2 + cc
                    cs = slice(c * CH, (c + 1) * CH)
                    h.transpose(PST.t[:, cc * 384:cc * 384 + 128], Bt.t[:, cs], ident.t[:])
                    h.transpose(PST.t[:, cc * 384 + 128:cc * 384 + 256], Kt.t[:, cs], ident.t[:])
                    ins = h.transpose(PST.t[:, cc * 384 + 256:cc * 384 + 384], Vt.t[:, cs], ident.t[:])
                return ins
            P.op(PE, trp, reads=Bt.bufs + Kt.bufs + Vt.bufs + ident.bufs, writes=PST.bufs)
            act(TOK[:, half * 2:half * 2 + 2, :], PST.t[:, 0:768].rearrange("p (c n) -> p c n", c=2), AF.Copy, PST.bufs, [TOKB[half]])
        for hh in range(2):
            ph = slice(64 * hh, 64 * hh + 64)
            for cp in range(2):
                b1, b2 = nextbank(), nextbank()

                def g12(h, b1=b1, b2=b2, ph=ph, cp=cp):
                    ins = None
                    for cc in range(2):
                        c = cp * 2 + cc
                        cs = slice(c * CH, (c + 1) * CH)
                        arf = AR.t[ph, c, :, :].rearrange("p t i -> p (t i)")
                        h.matmul(b1.t[:, cc * 256:(cc + 1) * 256], lhsT=Kt.t[ph, cs], rhs=arf, start=True, stop=True)
                        ins = h.matmul(b2.t[:, cc * 256:(cc + 1) * 256], lhsT=Bt.t[ph, cs], rhs=arf, start=True, stop=True)
                    return ins
                P.op(PE, g12, reads=Kt.bufs + Bt.bufs + AR.bufs, writes=b1.bufs + b2.bufs)
                q0 = hh * 4 + cp * 2
                dve(lambda h, b1=b1, q0=q0: h.tensor_tensor(out=G1A[:, q0:q0 + 2, :], in0=b1.t[:].rearrange("p (c n) -> p c n", c=2),
                                                           in1=MU2x2.t, op=ALU.mult), b1.bufs + MU2x2.bufs, G1Bx[hh])
                dve(lambda h, b2=b2, q0=q0: h.tensor_tensor(out=G2A[:, q0:q0 + 2, :], in0=b2.t[:].rearrange("p (c n) -> p c n", c=2),
                                                           in1=MU2x2.t, op=ALU.mult), b2.bufs + MU2x2.bufs, [G2B[hh]])
            b3 = nextbank()

            def g3(h, b3=b3, ph=ph):
                ins = None
                for c in range(4):
                    cs = slice(c * CH, (c + 1) * CH)
                    ins = h.matmul(b3.t[:, c * 128:(c + 1) * 128], lhsT=AR.t[ph, c, 0, :], rhs=Bt.t[ph, cs], start=True, stop=True)
                return ins
            P.op(PE, g3, reads=Bt.bufs + AR.bufs, writes=b3.bufs)
            dve(lambda h, b3=b3, hh=hh: h.tensor_tensor(out=PTA[:, hh * 4:hh * 4 + 4, :], in0=b3.t[:].rearrange("p (c n) -> p c n", c=4),
                                                       in1=MLx4.t[:], op=ALU.mult), b3.bufs + MLx4.bufs, [PTB[hh]])
            pool(lambda h, hh=hh: h.tensor_tensor(out=TTA[:, hh * 4:hh * 4 + 4, :], in0=G2A[:, hh * 4:hh * 4 + 4, 0:128], in1=IDx4.t[:], op=ALU.add),
                 [G2B[hh]] + IDx4.bufs, [TTB[hh]])
        for m in range(1, 7):
            pcs = []
            for hh in range(2):
                qs = range(hh * 4, hh * 4 + 4)
                Pin = (lambda q: G2A[:, q, 0:128]) if m == 1 else (lambda q: PA[:, q, :])
                Prd = [G2B[hh]] if m == 1 else [PB[hh]]
                if m < 6:
                    ba = nextbank()

                    def sqa(h, ba=ba, qs=qs, Pin=Pin):
                        ins = None
                        for i, q in enumerate(qs):
                            ins = h.matmul(ba.t[:, i * 128:(i + 1) * 128], lhsT=PTA[:, q, :], rhs=Pin(q), start=True, stop=True)
                        return ins
                    P.op(PE, sqa, reads=Prd + [PTB[hh]], writes=ba.bufs)
                bb = nextbank()

                def sqb_(h, bb=bb, qs=qs, Pin=Pin):
                    ins = None
                    for i, q in enumerate(qs):
                        ins = h.matmul(bb.t[:, i * 128:(i + 1) * 128], lhsT=Pin(q), rhs=PTA[:, q, :], start=True, stop=True)
                    return ins
                P.op(PE, sqb_, reads=Prd + [PTB[hh]], writes=bb.bufs)
                if hh == 0:
                    if m < 6:
                        act(PA[:, hh * 4:hh * 4 + 4, :], ba.t[:].rearrange("p (c n) -> p c n", c=4), AF.Copy, ba.bufs, [PB[hh]])
                    dve(lambda h, bb=bb, hh=hh: h.tensor_copy(PTA[:, hh * 4:hh * 4 + 4, :], bb.t[:].rearrange("p (c n) -> p c n", c=4)),
                        bb.bufs, [PTB[hh]])
                else:
                    act(PTA[:, hh * 4:hh * 4 + 4, :], bb.t[:].rearrange("p (c n) -> p c n", c=4), AF.Copy, bb.bufs, [PTB[hh]])
                    if m < 6:
                        act(PA[:, hh * 4:hh * 4 + 4, :], ba.t[:].rearrange("p (c n) -> p c n", c=4), AF.Copy, ba.bufs, [PB[hh]])
            for hh in range(2):
                qs = range(hh * 4, hh * 4 + 4)
                bc = nextbank()

                def ttu(h, bc=bc, qs=qs):
                    ins = None
                    for i, q in enumerate(qs):
                        ins = h.matmul(bc.t[:, i * 128:(i + 1) * 128], lhsT=PTA[:, q, :], rhs=TTA[:, q, :], start=True, stop=True)
                    return ins
                P.op(PE, ttu, reads=[PTB[hh], TTB[hh]], writes=bc.bufs)
                dve(lambda h, bc=bc, hh=hh: h.tensor_tensor(out=TTA[:, hh * 4:hh * 4 + 4, :], in0=bc.t[:].rearrange("p (c n) -> p c n", c=4),
                                                           in1=TTA[:, hh * 4:hh * 4 + 4, :], op=ALU.add), bc.bufs + [TTB[hh]], [TTB[hh]])

    def chunk(j, c, ysb):
        Hj = Hst[j]
        cs = slice(c * CH, (c + 1) * CH)
        tkb = [TOKB[c // 2]]
        Btok = lambda hh: TOK[:, c, hh * 64:(hh + 1) * 64]
        Ktok = lambda hh: TOK[:, c, 128 + hh * 64:128 + (hh + 1) * 64]
        Vtok = lambda hh: TOK[:, c, 256 + hh * 64:256 + (hh + 1) * 64]
        q_ = lambda hh: hh * 4 + c
        dve(lambda h: h.tensor_scalar(out=H0p.t[:], in0=Hj.t[:], scalar1=wc.t[:, c:c + 1], scalar2=None, op0=ALU.mult),
            Hj.bufs + wc.bufs, H0p.bufs)
        dve(lambda h: h.tensor_scalar(out=Hz[0].t[0:64, :], in0=Hj.t[0:64, :], scalar1=wc.t[0:64, c:c + 1], scalar2=None, op0=ALU.mult),
            Hj.bufs + wc.bufs, Hz[0].bufs)
        dve(lambda h: h.tensor_scalar(out=Hz[1].t[64:128, :], in0=Hj.t[64:128, :], scalar1=wc.t[64:128, c:c + 1], scalar2=None, op0=ALU.mult),
            Hj.bufs + wc.bufs, Hz[1].bufs)
        px = nextbank()

        def mx(h):
            ins = None
            for hh in range(2):
                h.matmul(px.t[:, hh * 64:(hh + 1) * 64], lhsT=AR.t[:, c, 0, :], rhs=Hz[hh].t, start=True, stop=False)
                ins = h.matmul(px.t[:, hh * 64:(hh + 1) * 64], lhsT=G1A[:, q_(hh), 0:128], rhs=Vtok(hh), start=False, stop=True)
            return ins
        P.op(PE, mx, reads=AR.bufs + Hz[0].bufs + Hz[1].bufs + G1Bx[0] + G1Bx[1] + tkb, writes=px.bufs)
        act(Xbf.t, px.t[:, 0:128], AF.Copy, px.bufs, Xbf.bufs)
        pu_ = nextbank()

        def mu_(h):
            ins = None
            for hh in range(2):
                ins = h.matmul(pu_.t[:, hh * 64:(hh + 1) * 64], lhsT=TTA[:, q_(hh), :], rhs=Xbf.t[:, hh * 64:(hh + 1) * 64], start=True, stop=True)
            return ins
        P.op(PE, mu_, reads=TTB + Xbf.bufs, writes=pu_.bufs)
        act(Ubf.t, pu_.t[:, 0:128], AF.Copy, pu_.bufs, Ubf.bufs)
        py, psn = nextbank(), nextbank()

        def my(h):
            ins = None
            for hh in range(2):
                po = slice(64 * hh, 64 * hh + 64)
                h.matmul(py.t[po, 0:128], lhsT=Hz[hh].t, rhs=AR.t[:, c, 1, :], start=True, stop=False)
                h.matmul(py.t[po, 0:128], lhsT=Ubf.t[:, hh * 64:(hh + 1) * 64], rhs=G2A[:, q_(hh), 128:256], start=False, stop=False)
                h.matmul(py.t[po, 0:128], lhsT=Vtok(hh), rhs=G1A[:, q_(hh), 128:256], start=False, stop=True)
            for hh in range(2):
                po = slice(64 * hh, 64 * hh + 64)
                h.matmul(psn.t[po, 0:64], lhsT=Btok(hh), rhs=Ubf.t[:, hh * 64:(hh + 1) * 64], start=True, stop=False)
                ins = h.matmul(psn.t[po, 0:64], lhsT=Ktok(hh), rhs=Vtok(hh), start=False, stop=True)
            return ins
        P.op(PE, my, reads=Hz[0].bufs + Hz[1].bufs + AR.bufs + Ubf.bufs + G1Bx[0] + G1Bx[1] + G2B + tkb, writes=py.bufs + psn.bufs)
        dve(lambda h: h.tensor_tensor(out=Hj.t[:], in0=psn.t[:, 0:64], in1=H0p.t[:], op=ALU.add), psn.bufs + H0p.bufs, Hj.bufs)
        act(ysb.t[:, cs], py.t[:, 0:128], AF.Copy, py.bufs, ysb.bufs)

    def pair(j, mid=None):
        slot = win_load([4 + j, 16 + j, 28 + j])
        rs, ks, vs = F[I_RS], F[I_KS], F[I_VS]
        e1, l1, nl, a_t, g_t = F[I_E1], F[I_L1], F[I_NL], F[I_A], F[I_G]
        kk, k2, bp, bonus = F[I_KK], F[I_K2], F[I_BP], F[I_BONUS]
        cwn, cwx, eA, eR, eK = F[I_CWN], F[I_CWX], F[I_EA], F[I_ER], F[I_EK]
        pzw, pza, pg_ = nextbank(), nextbank(), nextbank()
        P.op(PE, lambda h: h.matmul(pzw.t[:], lhsT=wa2_bf.t[0:64, j * 128:(j + 1) * 128], rhs=lora_in.t[0:64, :], start=True, stop=True),
             reads=wa2_bf.bufs + lora_in.bufs, writes=pzw.bufs)
        P.op(PE, lambda h: h.matmul(pza.t[:], lhsT=wa2_bf.t[64:128, j * 128:(j + 1) * 128], rhs=lora_in.t[64:128, :], start=True, stop=True),
             reads=wa2_bf.bufs + lora_in.bufs, writes=pza.bufs)

        def mg(h):
            h.matmul(pg_.t[:], lhsT=g2a_bf.t[:, j * 128:(j + 1) * 128], rhs=sg1.t[:, :], start=True, stop=False)
            return h.matmul(pg_.t[:], lhsT=g2b_bf.t[0:96, j * 128:(j + 1) * 128], rhs=sg2.t[0:96, :], start=False, stop=True)
        P.op(PE, mg, reads=g2a_bf.bufs + g2b_bf.bufs + sg1.bufs + sg2.bufs, writes=pg_.bufs)
        act(e1.t[:, :], pzw.t[:], AF.Exp, pzw.bufs + dvec.bufs, e1.bufs, scale=-1.0, bias=dvec.t[:, j:j + 1])
        act(l1.t[:, :], e1.t[:, :], AF.Ln, e1.bufs, l1.bufs, bias=cst.t[:, 0:1], scale=1.0)
        act(nl.t[:, :], l1.t[:, :], AF.Exp, l1.bufs, nl.bufs, scale=-1.0, bias=cst.t[:, 1:2])
        dve(lambda h: h.tensor_tensor_scan(out=cwn.t[:, :], data0=scanmask.t[:], data1=nl.t[:, :], initial=0.0, op0=ALU.mult, op1=ALU.add),
            scanmask.bufs + nl.bufs, cwn.bufs)
        dve(lambda h: h.tensor_tensor(out=cwx.t[:, :], in0=cwn.t[:, :], in1=nl.t[:, :], op=ALU.subtract), cwn.bufs + nl.bufs, cwx.bufs)
        cend = v3(cwn)[:, :, CH - 1]
        dve(lambda h: h.tensor_scalar(out=ncw.t[:], in0=cend, scalar1=-1.0, scalar2=None, op0=ALU.mult), cwn.bufs, ncw.bufs)
        act(e1.t[:, :], pza.t[:], AF.Exp, pza.bufs + dvec.bufs, e1.bufs, scale=-1.0, bias=dvec.t[:, NPAIR + j:NPAIR + j + 1])
        act(l1.t[:, :], e1.t[:, :], AF.Ln, e1.bufs, l1.bufs, bias=cst.t[:, 0:1], scale=1.0)
        act(a_t.t[:, :], l1.t[:, :], AF.Exp, l1.bufs, a_t.bufs, scale=-1.0)
        act(wc.t[:], cend, AF.Exp, cwn.bufs, wc.bufs, scale=-1.0)

        def exps(h):
            ins = None
            for c in range(4):
                cs = slice(c * CH, (c + 1) * CH)
                ce = cwn.t[:, c * CH + CH - 1:c * CH + CH]
                h.activation(out=eR.t[:, cs], in_=cwn.t[:, cs], func=AF.Exp, scale=-1.0, bias=ce)
                h.activation(out=eA.t[:, cs], in_=cwx.t[:, cs], func=AF.Exp, scale=-1.0, bias=ce)
                ins = h.activation(out=eK.t[:, cs], in_=cwn.t[:, cs], func=AF.Exp, scale=1.0, bias=ncw.t[:, c:c + 1])
            return ins
        P.op(ACT, exps, reads=cwn.bufs + cwx.bufs + ncw.bufs, writes=eR.bufs + eA.bufs + eK.bufs)
        ps = proj(slot, 0, 128)
        tshift(ps, 4 + j, rs)
        ps = proj(slot, 1, 128)
        tshift(ps, 16 + j, ks)
        ps = proj(slot, 2, 128)
        tshift(ps, 28 + j, vs)
        act(g_t.t[:, :], pg_.t[:], AF.Copy, pg_.bufs, g_t.bufs)
        pool(lambda h: h.tensor_copy(Vt.t[:, :], vs.t[:, :]), vs.bufs, Vt.bufs)
        dve(lambda h: h.tensor_tensor(out=AR.t[:, :, 1, :], in0=v3(rs), in1=v3(eR), op=ALU.mult), rs.bufs + eR.bufs, AR.bufs)
        dve(lambda h: h.tensor_scalar(out=kk.t[:, :], in0=ks.t[:, :], scalar1=vcol(V_KK, j), scalar2=None, op0=ALU.mult),
            ks.bufs + vec_t.bufs, kk.bufs)
        act(sqb.t[:, :], kk.t[:, :], AF.Square, kk.bufs, sqb.bufs)
        pss = nextbank()
        P.op(PE, lambda h: h.matmul(pss.t[:], lhsT=blk1.t[:], rhs=sqb.t[:, :], start=True, stop=True),
             reads=blk1.bufs + sqb.bufs, writes=pss.bufs)
        dve(lambda h: h.tensor_scalar(out=k2.t[:, :], in0=a_t.t[:, :], scalar1=vcol(V_KA, j), scalar2=dvec.t[:, 2 * NPAIR + j:2 * NPAIR + j + 1],
                                      op0=ALU.mult, op1=ALU.add), a_t.bufs + vec_t.bufs + dvec.bufs, k2.bufs)
        dve(lambda h: h.tensor_tensor(out=k2.t[:, :], in0=k2.t[:, :], in1=ks.t[:, :], op=ALU.mult), k2.bufs + ks.bufs, k2.bufs)
        dve(lambda h: h.tensor_tensor(out=Kt.t[:, :], in0=k2.t[:, :], in1=eK.t[:, :], op=ALU.mult), k2.bufs + eK.bufs, Kt.bufs)
        act(l1.t[:, :], pss.t[:], AF.Ln, pss.bufs, l1.bufs, bias=cst.t[:, 2:3], scale=1.0)
        act(e1.t[:, :], l1.t[:, :], AF.Exp, l1.bufs, e1.bufs, scale=-0.5)
        dve(lambda h: h.tensor_tensor(out=kk.t[:, :], in0=kk.t[:, :], in1=e1.t[:, :], op=ALU.mult), kk.bufs + e1.bufs, kk.bufs)
        dve(lambda h: h.scalar_tensor_tensor(out=AR.t[:, :, 0, :], in0=v3(kk), scalar=-1.0, in1=v3(eA), op0=ALU.mult, op1=ALU.mult),
            kk.bufs + eA.bufs, AR.bufs)
        dve(lambda h: h.tensor_tensor(out=bp.t[:, :], in0=kk.t[:, :], in1=a_t.t[:, :], op=ALU.mult), kk.bufs + a_t.bufs, bp.bufs)
        dve(lambda h: h.tensor_tensor(out=Bt.t[:, :], in0=bp.t[:, :], in1=eK.t[:, :], op=ALU.mult), bp.bufs + eK.bufs, Bt.bufs)
        if mid is not None:
            mid()
        chains(j)
        dve(lambda h: h.scalar_tensor_tensor(out=sqb.t[:, :], in0=rs.t[:, :], scalar=vcol(V_RK, j), in1=k2.t[:, :], op0=ALU.mult, op1=ALU.mult),
            rs.bufs + k2.bufs + vec_t.bufs, sqb.bufs)
        psr = nextbank()
        P.op(PE, lambda h: h.matmul(psr.t[:], lhsT=blk1.t[:], rhs=sqb.t[:, :], start=True, stop=True),
             reads=blk1.bufs + sqb.bufs, writes=psr.bufs)
        dve(lambda h: h.tensor_tensor(out=bonus.t[:, :], in0=psr.t[:], in1=vs.t[:, :], op=ALU.mult), psr.bufs + vs.bufs, bonus.bufs)

        ysb = F[I_Y]
        for c in range(4):
            chunk(j, c, ysb)

        yc = F[I_YC]
        pm = nextbank()
        P.op(PE, lambda h: h.matmul(pm.t[:], lhsT=blkm.t[:], rhs=ysb.t[:, :], start=True, stop=True),
             reads=blkm.bufs + ysb.bufs, writes=pm.bufs)
        dve(lambda h: h.tensor_tensor(out=yc.t[:, :], in0=ysb.t[:, :], in1=pm.t[:], op=ALU.subtract), ysb.bufs + pm.bufs, yc.bufs)
        act(ysb.t[:, :], yc.t[:, :], AF.Square, yc.bufs, ysb.bufs)
        pv = nextbank()
        P.op(PE, lambda h: h.matmul(pv.t[:], lhsT=blkm.t[:], rhs=ysb.t[:, :], start=True, stop=True),
             reads=blkm.bufs + ysb.bufs, writes=pv.bufs)
        act(l1.t[:, :], pv.t[:], AF.Ln, pv.bufs, l1.bufs, bias=cst.t[:, 3:4], scale=1.0)
        act(e1.t[:, :], l1.t[:, :], AF.Exp, l1.bufs, e1.bufs, scale=-0.5)
        dve(lambda h: h.tensor_tensor(out=yc.t[:, :], in0=yc.t[:, :], in1=e1.t[:, :], op=ALU.mult), yc.bufs + e1.bufs, yc.bufs)
        act(ysb.t[:, :], yc.t[:, :], AF.Identity, yc.bufs + vec_t.bufs, ysb.bufs, scale=vcol(V_LW, j), bias=vcol(V_LB, j))
        dve(lambda h: h.tensor_tensor(out=ysb.t[:, :], in0=ysb.t[:, :], in1=bonus.t[:, :], op=ALU.add), ysb.bufs + bonus.bufs, ysb.bufs)
        dve(lambda h: h.tensor_tensor(out=catT[4 + j].t, in0=ysb.t[:, :], in1=g_t.t[:, :], op=ALU.mult), ysb.bufs + g_t.bufs, catT[4 + j].bufs)

    def out_group(g, ps2):
        pacc = [nextbank() for _ in range(4)]
        wrb = []
        for d_ in range(4):
            wrb = wrb + pacc[d_].bufs

        def half_(half):
            k = slot_i["d"] % 2
            slot = WD[k]
            slot_i["d"] += 1
            r0 = half * 8 * 128
            P.op(SP, lambda h: h.dma_start(
                out=slot.t[:, 0:8, :], in_=woutb[r0:r0 + 8 * 128, g * 512:(g + 1) * 512].rearrange("(f p) c -> p f c", p=128)),
                reads=wbuf["wout"], writes=slot.bufs, dma_key=f"wd{k}")

            def mm(h):
                ins = None
                for fl in range(8):
                    cc = half * 8 + fl
                    for d_ in range(4):
                        ins = h.matmul(pacc[d_].t[:], lhsT=slot.t[:, fl, d_ * 128:(d_ + 1) * 128], rhs=catT[cc].t,
                                       start=(cc == 0), stop=(cc == 15))
                return ins
            rd = list(slot.bufs)
            for fl in range(8):
                rd = rd + catT[half * 8 + fl].bufs
            P.op(PE, mm, reads=rd, writes=wrb)
        half_(0)
        half_(1)
        for d_ in range(4):
            evac_f(pacc[d_], g * 4 + d_, ps2)

    src, dst = "x1T", ("x2T" if stage > 2 else "outT")
    final = stage == 2
    def mk_mu(h):
        h.tensor_copy(MU2x2.t[:, 0, :], MU2.t[:])
        return h.tensor_copy(MU2x2.t[:, 1, :], MU2.t[:])
    P.op(DVE, mk_mu, reads=MU2.bufs, writes=MU2x2.bufs)
    prenorm_p1(src, 0)
    prenorm_p2(src, 0, 1)
    for ti in range(NT):
        lora_inputs()
        slot = win_load([0, 1, 2, 3])
        for g in range(4):
            pool_group(ti, slot, g)
        for j in range(NPAIR):
            if j == NPAIR - 1 and ti + 1 < NT:
                def mid(ti=ti):
                    prenorm_p1(src, ti + 1)
                    prenorm_p2(src, ti + 1, 1)
                pair(j, mid)
            else:
                pair(j)
        ps2 = PSB[5]
        for g in range(4):
            out_group(g, ps2)
        postnorm(src, dst, ti, 1, ps2, final)


def prep_shared(inp):
    f = lambda a: np.ascontiguousarray(np.asarray(a, dtype=np.float32))
    sh = {}
    sh["w_ada"] = f(inp["w_ada"][0])
    sh["b_ada"] = f(np.asarray(inp["b_ada"][0]).reshape(144, 128).T)
    sh["npre"] = f(np.asarray(inp["norm_pre"][0]).reshape(3, 16, 128).transpose(2, 0, 1).reshape(128, 48))
    sh["npost"] = f(np.asarray(inp["norm_post"][0]).reshape(3, 16, 128).transpose(2, 0, 1).reshape(128, 48))
    sh["wg1"] = f(inp["ffn1_w_gate"][0]); sh["wu1"] = f(inp["ffn1_w_up"][0]); sh["wd1"] = f(inp["ffn1_w_down"][0])
    sh["wg2"] = f(inp["ffn2_w_gate"][0]); sh["wu2"] = f(inp["ffn2_w_up"][0]); sh["wd2"] = f(inp["ffn2_w_down"][0])
    sh["win"] = f(inp["w_in"][0]); sh["wout"] = f(inp["w_out"][0])
    mu = np.zeros(43 * 128, np.float32)
    mu[512:512 + 4960] = np.asarray(inp["mu_shift"][0])
    sh["muT"] = f(mu.reshape(43, 128).T)
    sh["poolw"] = f(np.asarray(inp["pool_w"][0]).transpose(1, 0, 2).reshape(128, 512))
    sh["pscale"] = f(np.asarray(inp["pool_scale"][0]).reshape(4, 128).T)
    vs = [inp["w0"][0], inp["a0"][0], inp["k_k"][0], inp["k_a"][0], np.asarray(inp["r_k"][0]).reshape(-1), inp["lnx_w"][0], inp["lnx_b"][0]]
    sh["vecs"] = f(np.concatenate([np.asarray(v).reshape(NPAIR, 128).T for v in vs], axis=1))
    sh["wa2"] = f(np.concatenate([np.asarray(inp["w2"][0]), np.asarray(inp["a2"][0])], axis=0))
    g2 = np.asarray(inp["g2"][0])
    sh["g2a"] = f(g2[0:128]); sh["g2b"] = f(g2[128:224])
    return sh


def prep_core(inp, b):
    return {"xT": np.ascontiguousarray(np.asarray(inp["x"][b], dtype=np.float32).T),
            "cT": np.ascontiguousarray(np.asarray(inp["c"][b], dtype=np.float32).reshape(16, 128).T)}


_NEEDED = {1: ["xT", "cT", "w_ada", "b_ada", "npre", "npost", "wg1", "wu1", "wd1"]}


def kernel(**inputs):
    x = np.asarray(inputs["x"])
    B, T, _ = x.shape
    nc = build_program(T, stage=3)
    sh = prep_shared(inputs)
    in_maps = []
    for b in range(B):
        m = dict(sh)
        m.update(prep_core(inputs, b))
        in_maps.append(m)
    res = run_bass_kernel_spmd(nc, in_maps, core_ids=list(range(B)))
    out = np.stack([np.ascontiguousarray(r["outT"].T) for r in res.results], axis=0)
    return out.astype(np.float32)
```
